# Optimizing a Trainium2 kernel written in Bass

```python
import math, functools
import jax, jax.numpy as jnp
from jax import lax
import numpy as np

D_MODEL = 1024
BATCH = 4
SEQ = 8192
DEPTH = 2

N_MEM = 256
D_MAIN = D_MODEL
N_POOL_GROUPS = 4
POOL_WINDOWS = (2, 4, 8, 16)
POOL_GROUP = D_MAIN // N_POOL_GROUPS
FOX_HEADS = 16
FOX_HEAD_DIM = D_MAIN // FOX_HEADS
MEM_HEADS = 4
MEM_HEAD_DIM = 128
D_MEM = MEM_HEADS * MEM_HEAD_DIM
D_MIX = D_MAIN + D_MEM
D_IN = 2 * D_MIX
N_A = DEPTH // 2
N_B = DEPTH - N_A
Q_BLOCK = 128
ALPHA = (2 * DEPTH) ** 0.25
BETA = (8 * DEPTH) ** -0.25
LN_EPS = 1e-5

kernel_name = "yoco_pool_fox_memory_deepnorm"


def layer_norm(x, g, b):
    xf = x.astype(jnp.float32)
    mu = jnp.mean(xf, axis=-1, keepdims=True)
    var = jnp.mean(jnp.square(xf - mu), axis=-1, keepdims=True)
    y = (xf - mu) * lax.rsqrt(var + LN_EPS) * g.astype(jnp.float32) + b.astype(jnp.float32)
    return y.astype(x.dtype)


def causal_multiscale_pool(u, pool_w, pool_scale):
    B, S, _ = u.shape
    ug = u.reshape(B, S, N_POOL_GROUPS, POOL_GROUP)
    cs = jnp.cumsum(ug.astype(jnp.float32), axis=1)
    cs = jnp.pad(cs, ((0, 0), (1, 0), (0, 0), (0, 0)))
    t = jnp.arange(S)
    outs = []
    for gi, w in enumerate(POOL_WINDOWS):
        c = cs[:, :, gi]
        upper = c[:, 1:]
        lower = jnp.concatenate(
            [jnp.zeros((B, w - 1, POOL_GROUP), jnp.float32), c[:, :S + 1 - w]], axis=1)
        count = jnp.minimum(t + 1, w).astype(jnp.float32)[None, :, None]
        outs.append((upper - lower) / count - ug[:, :, gi].astype(jnp.float32))
    pm = jnp.stack(outs, axis=2).astype(u.dtype)
    mixed = jnp.einsum('bsgc,gcd->bsgd', pm, pool_w)
    return mixed.reshape(B, S, D_MAIN) * pool_scale


def shared_kv(x, w_kv_shared, b_forget):
    B, S, _ = x.shape
    h = x @ w_kv_shared
    k = h[..., :D_MAIN].reshape(B, S, FOX_HEADS, FOX_HEAD_DIM)
    v = h[..., D_MAIN:2 * D_MAIN].reshape(B, S, FOX_HEADS, FOX_HEAD_DIM)
    f_logit = h[..., 2 * D_MAIN:].astype(jnp.float32) + b_forget.astype(jnp.float32)
    log_f = jax.nn.log_sigmoid(f_logit)
    cum = jnp.cumsum(log_f, axis=1)
    return k, v, cum


def forgetting_attention(u, k, v, cum):
    B, S, _ = u.shape
    q = u.reshape(B, S, FOX_HEADS, FOX_HEAD_DIM)
    nb = S // Q_BLOCK
    qb = q.reshape(B, nb, Q_BLOCK, FOX_HEADS, FOX_HEAD_DIM).transpose(1, 0, 2, 3, 4)
    cb = cum.reshape(B, nb, Q_BLOCK, FOX_HEADS).transpose(1, 0, 3, 2)
    cum_k = cum.transpose(0, 2, 1)
    starts = jnp.arange(nb) * Q_BLOCK
    kpos = jnp.arange(S)
    scale = FOX_HEAD_DIM ** -0.5

    def block(args):
        qi, ci, s0 = args
        logits = jnp.einsum('bqhd,bkhd->bhqk', qi, k,
                            preferred_element_type=jnp.float32) * scale
        logits = logits + (ci[..., :, None] - cum_k[:, :, None, :])
        qpos = s0 + jnp.arange(Q_BLOCK)
        mask = kpos[None, :] <= qpos[:, None]
        logits = jnp.where(mask[None, None], logits, -jnp.inf)
        p = jax.nn.softmax(logits, axis=-1)
        return jnp.einsum('bhqk,bkhd->bqhd', p.astype(v.dtype), v)

    out = lax.map(block, (qb, cb, starts))
    return out.transpose(1, 0, 2, 3, 4).reshape(B, S, D_MAIN)


def memory_attention(q_mem, mem, w_mem_kv):
    B, S, _ = q_mem.shape
    M = mem.shape[1]
    mkv = mem @ w_mem_kv
    mk = mkv[..., :D_MEM].reshape(B, M, MEM_HEADS, MEM_HEAD_DIM)
    mv = mkv[..., D_MEM:].reshape(B, M, MEM_HEADS, MEM_HEAD_DIM)
    q = q_mem.reshape(B, S, MEM_HEADS, MEM_HEAD_DIM)
    logits = jnp.einsum('bshd,bmhd->bhsm', q, mk,
                        preferred_element_type=jnp.float32) * (MEM_HEAD_DIM ** -0.5)
    p = jax.nn.softmax(logits, axis=-1)
    return jnp.einsum('bhsm,bmhd->bshd', p.astype(mv.dtype), mv).reshape(B, S, D_MEM)


def mixer_sublayer(x, mem, w_in, w_mem_kv, w_out, main_fn):
    h = x @ w_in
    u_main = h[..., :D_MAIN]
    q_mem = h[..., D_MAIN:D_MIX]
    g_main = h[..., D_MIX:D_MIX + D_MAIN]
    g_mem = h[..., D_MIX + D_MAIN:]
    y_main = main_fn(u_main)
    y_mem = memory_attention(q_mem, mem, w_mem_kv)
    y = jnp.concatenate([y_main * jax.nn.silu(g_main), y_mem * jax.nn.silu(g_mem)], axis=-1)
    return y @ w_out


def setup_inputs(seed: int = 0) -> dict:
    key = jax.random.key(seed)
    ks = jax.random.split(key, 12)
    x = jax.random.normal(ks[0], (BATCH, SEQ, D_MODEL), jnp.float32)
    mem = jax.random.normal(ks[1], (BATCH, N_MEM, D_MODEL), jnp.float32)
    w_in = jax.random.normal(ks[2], (DEPTH, D_MODEL, D_IN), jnp.float32) * D_MODEL ** -0.5
    w_mem_kv = jax.random.normal(ks[3], (DEPTH, D_MODEL, 2 * D_MEM), jnp.float32) * D_MODEL ** -0.5
    w_out = jax.random.normal(ks[4], (DEPTH, D_MIX, D_MODEL), jnp.float32) * (D_MIX ** -0.5 * BETA)
    ln_g = 1.0 + 0.02 * jax.random.normal(ks[5], (DEPTH, D_MODEL), jnp.float32)
    ln_b = 0.02 * jax.random.normal(ks[6], (DEPTH, D_MODEL), jnp.float32)
    pool_w = jax.random.normal(ks[7], (N_A, N_POOL_GROUPS, POOL_GROUP, POOL_GROUP), jnp.float32) * POOL_GROUP ** -0.5
    pool_scale = 1.0 + 0.1 * jax.random.normal(ks[8], (N_A, D_MAIN), jnp.float32)
    w_kv_shared = jax.random.normal(ks[9], (D_MODEL, 2 * D_MAIN + FOX_HEADS), jnp.float32) * D_MODEL ** -0.5
    b_forget = jax.random.uniform(ks[10], (FOX_HEADS,), jnp.float32, 1.0, 5.0)
    return {"x": x, "mem": mem, "w_in": w_in, "w_mem_kv": w_mem_kv, "w_out": w_out,
            "ln_g": ln_g, "ln_b": ln_b, "pool_w": pool_w, "pool_scale": pool_scale,
            "w_kv_shared": w_kv_shared, "b_forget": b_forget}


def reference(x, mem, w_in, w_mem_kv, w_out, ln_g, ln_b, pool_w, pool_scale,
              w_kv_shared, b_forget):
    k_sh = v_sh = cum_sh = None
    for layer in range(DEPTH):
        if layer < N_A:
            main_fn = functools.partial(causal_multiscale_pool,
                                        pool_w=pool_w[layer], pool_scale=pool_scale[layer])
        else:
            if layer == N_A:
                k_sh, v_sh, cum_sh = shared_kv(x, w_kv_shared, b_forget)
            main_fn = functools.partial(forgetting_attention, k=k_sh, v=v_sh, cum=cum_sh)
        y = mixer_sublayer(x, mem, w_in[layer], w_mem_kv[layer], w_out[layer], main_fn)
        x = layer_norm(ALPHA * x + y, ln_g[layer], ln_b[layer])
    return x
```

```python
import numpy as np
import ml_dtypes
import concourse.bass as bass
import concourse.mybir as mybir
from concourse.bass_utils import run_bass_kernel_spmd

dt = mybir.dt
F32, BF16 = dt.float32, dt.bfloat16
AF = mybir.ActivationFunctionType
ALU = mybir.AluOpType

D = 1024
NMEM = 256
DIN = 3072
DMIX = 1536
NH = 16
DH = 64
ALPHA = 4.0 ** 0.25
EPS = 1e-5
POOLW = (2, 4, 8, 16)
MASKNEG = -30000.0

SAME_ENGINE_SYNC = True


class Buf:
    __slots__ = ("name", "w", "r", "jd")

    def __init__(self, name):
        self.name = name
        self.w = []
        self.r = []
        self.jd = []


class Op:
    __slots__ = ("eng", "fn", "deps", "sig", "sem", "val", "dma", "key", "idx")


ENGS = ("pe", "act", "dve", "pool", "sp")


class Prog:
    def __init__(self, nc):
        self.nc = nc
        self.ops = {e: [] for e in ENGS}
        self.n = 0

    def op(self, eng, fn, r=(), w=(), wj=(), dma=False, key=None):
        o = Op()
        o.eng, o.fn, o.dma, o.key = eng, fn, dma, key
        o.sig = dma
        o.sem = None
        o.val = 0
        o.idx = self.n
        self.n += 1
        deps = []
        for b in r:
            deps += b.w
        for b in w:
            deps += b.w
            deps += b.r
        for b in wj:
            deps += b.jd
        for b in r:
            b.r.append(o)
        for b in w:
            b.jd = b.w + b.r
            b.w = [o]
            b.r = []
        for b in wj:
            b.w.append(o)
        seen = set()
        fd = []
        for d in deps:
            if id(d) in seen or d is o:
                continue
            seen.add(id(d))
            if (not d.dma) and d.eng == eng and (eng == "pe" or not SAME_ENGINE_SYNC):
                continue
            fd.append(d)
            d.sig = True
        o.deps = fd
        self.ops[eng].append(o)
        return o

    def barrier(self, bufs):
        b = Buf("barrier")
        lasts = []
        for e in ENGS:
            if self.ops[e]:
                lasts.append(self.ops[e][-1])
        for x in bufs:
            lasts += x.w + x.r
        for e in ENGS:
            o = Op()
            o.eng, o.fn, o.dma, o.key = e, None, False, None
            o.sig = False
            o.sem = None
            o.val = 0
            o.idx = self.n
            self.n += 1
            o.deps = []
            seen = set()
            for d in lasts:
                if id(d) in seen:
                    continue
                seen.add(id(d))
                if (not d.dma) and d.eng == e:
                    continue
                o.deps.append(d)
                d.sig = True
            self.ops[e].append(o)

    def emit(self):
        nc = self.nc
        engsem = {e: nc.alloc_semaphore("s_" + e) for e in ("pe", "act", "dve", "pool")}
        keysem = {}
        cnt = {}
        for e in ENGS:
            for o in self.ops[e]:
                if o.dma:
                    k = o.key
                    if k not in keysem:
                        keysem[k] = nc.alloc_semaphore("d_" + k)
                        cnt[k] = 0
                    cnt[k] += 16
                    o.sem, o.val = keysem[k], cnt[k]
                elif o.sig:
                    cnt[e] = cnt.get(e, 0) + 1
                    o.sem, o.val = engsem[e], cnt[e]
        finals = [(keysem[k], cnt[k]) for k in keysem]
        ops = self.ops

        def mk(e):
            def body(eng):
                wm = {}
                for o in ops[e]:
                    need = {}
                    for d in o.deps:
                        k = d.sem.num
                        if wm.get(k, 0) < d.val and need.get(k, (None, 0))[1] < d.val:
                            need[k] = (d.sem, d.val)
                    for k, (s, v) in need.items():
                        eng.wait_ge(s, v)
                        wm[k] = v
                    if o.fn is None:
                        continue
                    inst = o.fn(eng)
                    if o.sig:
                        inst.then_inc(o.sem, 16 if o.dma else 1)
                if e == "sp":
                    for s, v in finals:
                        eng.wait_ge(s, v)
            return body

        with nc.Block() as blk:
            blk.tensor(mk("pe"))
            blk.scalar(mk("act"))
            blk.vector(mk("dve"))
            blk.gpsimd(mk("pool"))
            blk.sync(mk("sp"))


class Arena:
    def __init__(self, nc, nbytes):
        self.h32 = nc.alloc_sbuf_tensor("arena", [128, nbytes // 4], F32)
        self.h16 = self.h32.bitcast(BF16)
        self.top = 0
        self.cap = nbytes

    def mark(self):
        return self.top

    def release(self, m):
        self.top = m

    def t(self, dtype, *shape):
        n = 1
        for s in shape:
            n *= s
        nb = n * (4 if dtype == F32 else 2)
        nb = (nb + 63) // 64 * 64
        off = self.top
        self.top += nb
        assert self.top <= self.cap, f"SBUF arena overflow {self.top} > {self.cap}"
        if dtype == F32:
            ap = self.h32[:, off // 4: off // 4 + n]
        else:
            ap = self.h16[:, off // 2: off // 2 + n]
        if len(shape) == 2:
            ap = ap.rearrange("p (a b) -> p a b", a=shape[0])
        elif len(shape) == 3:
            ap = ap.rearrange("p (a b c) -> p a b c", a=shape[0], b=shape[1])
        return ap


def own_tiles(hh, n512):
    return [g for g in range(n512) if ((g % 4) in (0, 3)) == (hh == 0)]


def I(name, *a, **k):
    return lambda e: getattr(e, name)(*a, **k)


class Bank:
    def __init__(self, P32, P16, k):
        self.buf = Buf(f"bank{k}")
        self.f = P32[:, k * 512:(k + 1) * 512]
        self.h = P16[:, k * 1024:(k + 1) * 1024]


def slot_tiles(n512):
    res = []
    for j in range(n512 // 2):
        p, e = j // 2, j % 2
        g0 = 4 * p + (0, 3)[e]
        g1 = 4 * p + (1, 2)[e]
        res.append((g0, g1, max(g0, g1) + 1))
    return res


def build(S, phases=("a1", "a2", "b"), debug=False, TA=256):
    assert S % 2048 == 0
    NT512 = S // 512
    NBLK = S // 128
    NSLOT = NT512 // 2
    nc = bass.Bass("TRN2", target_bir_lowering=False)
    P = Prog(nc)

    def din(name, shape, dtype=F32):
        return nc.dram_tensor(name, list(shape), dtype, kind="ExternalInput").ap()

    x_d = din("x", [S, D])
    mem_d = din("mem", [NMEM, D])
    w_in_d = din("w_in", [2, D, DIN])
    w_mkv_d = din("w_mem_kv", [2, D, D])
    w_out_d = din("w_out", [2, DMIX, D])
    pool_w_d = din("pool_w", [4, 256, 256])
    w_kv_d = din("w_kv", [D, 2064])
    lnp_d = din("lnp", [2, 2, D])
    pscale_d = din("pscale", [128, 8])
    bfor_d = din("bfor", [16])
    cb16_d = din("cb16", [128, 3 * 128 + 16 * 67], BF16)
    cf32_d = din("cf32", [128, 128 + 128 + 64 + 64], F32)
    sel_d = din("sel", [128, 2])
    masks_d = din("masks", [2, 128, 2 * 4 * 512], BF16)
    out_d = nc.dram_tensor("out", [NSLOT * 512, D], F32, kind="ExternalOutput").ap()

    kind_scr = "ExternalOutput" if debug else "Internal"
    X1F = nc.dram_tensor("x1f_scr", [S, D], F32, kind=kind_scr).ap()
    X1T = nc.dram_tensor("x1t_scr", [8, 128, S], BF16, kind=kind_scr).ap()
    KT = nc.dram_tensor("kt_scr", [8, 128, S], BF16, kind=kind_scr).ap()
    VA = nc.dram_tensor("va_scr", [NBLK, 128, NH, 128], BF16, kind=kind_scr).ap()
    if debug:
        CK_d = nc.dram_tensor("ck_dbg", [128, NBLK * 16], F32, kind="ExternalOutput").ap()

    A = Arena(nc, 212736)
    P32 = nc.alloc_psum_tensor("ps", [128, 4096], F32)
    P16 = P32.bitcast(BF16)
    banks = [Bank(P32, P16, k) for k in range(8)]
    bank_rr = [0]

    def nb():
        b = banks[bank_rr[0] % 8]
        bank_rr[0] += 1
        return b

    cb16 = A.t(BF16, 3 * 128 + 16 * 67)
    ident = cb16[:, 0:128]
    ones16 = cb16[:, 128:256]
    Emat = cb16[:, 384:384 + 16 * 67].rearrange("p (h m) -> p h m", h=16)
    cf32 = A.t(F32, 384)
    tri = cf32[:, 0:128]
    allones = cf32[:, 128:256]
    invc = cf32[:, 256:320].rearrange("p (w t) -> p w t", w=4)
    ones64 = cf32[:, 320:384]
    pscale = A.t(F32, 8)
    bfor = A.t(F32, 16)
    selt = A.t(F32, 2)
    CK = A.t(F32, NBLK, 16)
    CS = A.t(BF16, NBLK, 48)
    Tn = A.t(F32, 16)
    lng = A.t(F32, D)
    lnb = A.t(F32, D)
    mkT = A.t(BF16, 4, NMEM)
    mv = A.t(BF16, 2, 512)
    small = A.t(F32, 64)

    B_const = Buf("const")
    B_ln = Buf("lnp")
    B_mkv = Buf("mkv")
    B_CK = [Buf(f"ck{n}") for n in range(NBLK)]
    B_CS = [Buf(f"cs{n}") for n in range(NBLK)]
    B_Tn = Buf("Tn")
    B_X1F = [Buf(f"x1f{g}") for g in range(NT512)]
    B_X1T = [Buf(f"x1t{g}") for g in range(NT512)]
    B_KV = [Buf(f"kv{g}") for g in range(NT512)]

    def sp_load(out, in_, w, key, r=()):
        return P.op("sp", I("dma_start", out=out, in_=in_), r=r, w=w, dma=True, key=key)

    def pq_load(out, in_, w, key, r=(), wj=()):
        return P.op("pool", I("dma_start", out=out, in_=in_), r=r, w=w, wj=wj, dma=True, key=key)

    def cast_load(out, in_, w, key, r=(), wj=()):
        return P.op("pool", I("dma_start", out=out, in_=in_), r=r, w=w, wj=wj, dma=True, key=key)

    sp_load(cb16, cb16_d, [B_const], "c0")
    P.op("sp", I("dma_start", out=cf32, in_=cf32_d), wj=[B_const], dma=True, key="c0")
    P.op("sp", I("dma_start", out=pscale, in_=pscale_d), wj=[B_const], dma=True, key="c0")
    P.op("sp", I("dma_start", out=bfor, in_=bfor_d.partition_broadcast(128)), wj=[B_const], dma=True, key="c0")
    P.op("sp", I("dma_start", out=selt, in_=sel_d), wj=[B_const], dma=True, key="c0")

    def load_ln(layer):
        sp_load(lng, lnp_d[layer, 0].partition_broadcast(128), [B_ln], "ln")
        P.op("sp", I("dma_start", out=lnb, in_=lnp_d[layer, 1].partition_broadcast(128)),
             wj=[B_ln], dma=True, key="ln")

    def load_w(dst, src_rows_by_cols, nk, ncols, buf, key, first=True):
        srcv = src_rows_by_cols.rearrange("(k p) n -> p k n", p=128)
        for k in range(nk):
            for c0 in range(0, ncols, 1024):
                step = min(1024, ncols - c0)
                o = dst[:, k, c0:c0 + step]
                i = srcv[:, k, c0:c0 + step]
                if first:
                    cast_load(o, i, [buf], key)
                    first = False
                else:
                    cast_load(o, i, [], key, wj=[buf])

    def transposes_to(dstT, src_tok, nblk, r_src, w_dst, alt, all_act=False):
        for c in range(8):
            bk = nb()
            for b in range(nblk):
                P.op("pe", I("transpose",
                    out=bk.h[:, b * 128:(b + 1) * 128], in_=src_tok[:, b, c * 128:(c + 1) * 128],
                    identity=ident), r=[r_src[b] if isinstance(r_src, list) else r_src, B_const], w=[bk.buf])
            n = nblk * 128
            if (c + alt) % 2 == 0 and not all_act:
                P.op("dve", I("tensor_copy", out=dstT[:, c, 0:n], in_=bk.h[:, 0:n]),
                     r=[bk.buf], w=[] if c else [w_dst], wj=[w_dst] if c else [])
            else:
                P.op("act", I("activation", out=dstT[:, c, 0:n], in_=bk.h[:, 0:n], func=AF.Copy),
                     r=[bk.buf], w=[] if c else [w_dst], wj=[w_dst] if c else [])

    def proj(bk, W, wbuf, col0, m, xT, xbuf, n, extra_first=None):
        first = True
        if extra_first is not None:
            extra_first(bk)
            first = False
        for k in range(8):
            P.op("pe", I("matmul",
                bk.f[0:m, 0:n], W[:, k, col0:col0 + m], xT[:, k, 0:n], start=first, stop=(k == 7)),
                r=[wbuf, xbuf], w=[bk.buf])
            first = False

    def mem_kv(layer, wtmp, B_wtmp, memb, B_memb, memT, B_memT):
        load_w(wtmp, w_mkv_d[layer], 8, D, B_wtmp, "wtmp")
        cast_load(memb, mem_d.rearrange("(b p) d -> p b d", p=128), [B_memb], "memb")
        transposes_to(memT, memb, 2, B_memb, B_memT, 0)
        for h in range(4):
            bk = nb()
            proj(bk, wtmp, B_wtmp, h * 128, 128, memT, B_memT, NMEM)
            P.op("act", I("activation", out=mkT[:, h, :], in_=bk.f[:, 0:NMEM], func=AF.Copy),
                 r=[bk.buf], w=[B_mkv] if h == 0 else [], wj=[] if h == 0 else [B_mkv])
        for j in range(2):
            bk = nb()
            for k in range(8):
                P.op("pe", I("matmul",
                    bk.f[:, :], memT[:, k, j * 128:(j + 1) * 128], wtmp[:, k, 512:1024],
                    start=(k == 0), stop=(k == 7)), r=[B_wtmp, B_memT], w=[bk.buf])
            P.op("dve", I("tensor_copy", out=mv[:, j, :], in_=bk.f[:, :]),
                 r=[bk.buf], wj=[B_mkv])

    def mem_attn(h, qm, B_qm, n, PT, B_PT, ybk, dbk):
        for j in range(2):
            lb = nb()
            P.op("pe", I("matmul",
                lb.f[:, 0:n], mkT[:, h, j * 128:(j + 1) * 128], qm[:, h, 0:n], start=True, stop=True),
                r=[B_mkv, B_qm], w=[lb.buf])
            P.op("act", I("activation", out=PT[j][:, 0:n], in_=lb.f[:, 0:n], func=AF.Exp),
                 r=[lb.buf], w=[B_PT[j]])
        for j in range(2):
            P.op("pe", I("matmul",
                ybk.f[:, 0:n], mv[:, j, h * 128:(h + 1) * 128], PT[j][:, 0:n], start=(j == 0), stop=(j == 1)),
                r=[B_mkv, B_PT[j]], w=[ybk.buf])
        for j in range(2):
            P.op("pe", I("matmul",
                dbk.f[:, 0:n], ones16, PT[j][:, 0:n], start=(j == 0), stop=(j == 1)),
                r=[B_const, B_PT[j]], w=[dbk.buf])

    def ln_block(ybk0, ybk1, xres, B_xres, z, B_z, zn, B_zn, junk, B_junk, slot, x1b, B_x1b,
                 dst_dram, B_dst, key, B_dst_join=False, res_scale=ALPHA, st_eng="sp", ln_act=False, part="all"):
        sm = small[:, slot * 8:(slot + 1) * 8]
        B_sm = B_small[slot]
        if not isinstance(xres, list):
            xres_l = [(xres, B_xres, float(res_scale))]
        else:
            xres_l = xres
        for hf, yb in enumerate((ybk0, ybk1) if part in ("all", "z") else ()):
            for ci, (xr, Bxr, sc) in enumerate(xres_l):
                last = ci == len(xres_l) - 1
                firstw = (hf == 0 and ci == 0)
                kw = dict(accum_out=sm[:, hf:hf + 1]) if last else {}
                in1 = yb.f[:, :] if ci == 0 else z[:, hf * 512:(hf + 1) * 512]
                rr = [Bxr, yb.buf] if ci == 0 else [Bxr, B_z]
                if not isinstance(sc, float):
                    rr = rr + [B_const]
                P.op("dve", I("scalar_tensor_tensor",
                    out=z[:, hf * 512:(hf + 1) * 512], in0=xr[:, hf * 512:(hf + 1) * 512], scalar=sc,
                    in1=in1, op0=ALU.mult, op1=ALU.add, **kw),
                    r=rr, w=[B_z, B_sm] if firstw else [], wj=[] if firstw else [B_z, B_sm])
        if part == "z":
            return
        if ln_act:
            P.op("act", I("activation", out=zn, in_=z, func=AF.Square, accum_out=sm[:, 2:3]), r=[B_z], w=[B_zn], wj=[B_sm])
        else:
            P.op("dve", I("scalar_tensor_tensor", out=zn, in0=z, scalar=1.0, in1=z, op0=ALU.mult, op1=ALU.mult,
                          accum_out=sm[:, 2:3]), r=[B_z], w=[B_zn], wj=[B_sm])
        P.op("dve", I("tensor_scalar", out=sm[:, 3:4], in0=sm[:, 0:1], scalar1=sm[:, 1:2], scalar2=1.0 / D,
                                              op0=ALU.add, op1=ALU.mult), r=[B_sm], wj=[B_sm])
        P.op("dve", I("tensor_tensor", out=sm[:, 4:5], in0=sm[:, 3:4], in1=sm[:, 3:4], op=ALU.mult),
             r=[B_sm], wj=[B_sm])
        P.op("dve", I("scalar_tensor_tensor", out=sm[:, 5:6], in0=sm[:, 2:3], scalar=1.0 / D, in1=sm[:, 4:5],
                                                     op0=ALU.mult, op1=ALU.subtract), r=[B_sm], wj=[B_sm])
        P.op("dve", I("tensor_scalar", out=sm[:, 5:6], in0=sm[:, 5:6], scalar1=EPS, scalar2=None, op0=ALU.add),
             r=[B_sm], wj=[B_sm])
        P.op("pool", I("tensor_tensor", out=sm[:, 6:7], in0=sm[:, 5:6], in1=epsc[:, 1:2], op=ALU.pow),
             r=[B_sm, B_const], wj=[B_sm])
        P.op("dve", I("scalar_tensor_tensor", out=sm[:, 7:8], in0=sm[:, 3:4], scalar=-1.0, in1=sm[:, 6:7],
                                                     op0=ALU.mult, op1=ALU.mult), r=[B_sm], wj=[B_sm])
        P.op("dve", I("tensor_scalar", out=zn, in0=z, scalar1=sm[:, 6:7], scalar2=sm[:, 7:8], op0=ALU.mult, op1=ALU.add),
             r=[B_z, B_sm], w=[B_zn])
        P.op("dve", I("tensor_tensor", out=z, in0=zn, in1=lng, op=ALU.mult), r=[B_zn, B_ln], w=[B_z])
        P.op("pool" if ln_act else "dve", I("tensor_tensor", out=zn, in0=z, in1=lnb, op=ALU.add), r=[B_z, B_ln], w=[B_zn])
        if x1b is not None:
            P.op("pool", I("tensor_copy", out=x1b, in_=zn), r=[B_zn], w=[B_x1b])
        if dst_dram is not None:
            if B_dst_join:
                P.op(st_eng, I("dma_start", out=dst_dram, in_=zn), r=[B_zn], wj=[B_dst], dma=True, key=key)
            else:
                P.op(st_eng, I("dma_start", out=dst_dram, in_=zn), r=[B_zn], w=[B_dst], dma=True, key=key)

    epsc = A.t(F32, 2)
    P.op("dve", I("memset", epsc[:, 0:1], EPS), wj=[B_const])
    P.op("dve", I("memset", epsc[:, 1:2], -0.5), wj=[B_const])
    B_small = [Buf(f"small{i}") for i in range(8)]
    P.op("dve", I("memset", Tn, 0.0), w=[B_Tn])

    base_mark = A.mark()

    if "a1" in phases:
        NB_A = TA // 128
        NTA = S // TA
        win = A.t(BF16, 8, DIN)
        wout = A.t(BF16, 12, D)
        poolw = A.t(BF16, 8, 256)
        B_win, B_wout, B_poolw = Buf("win"), Buf("wout"), Buf("poolw")
        m_tmp = A.mark()
        wtmp = A.t(BF16, 8, D)
        memb = A.t(BF16, 2, D)
        memT = A.t(BF16, 8, NMEM)
        B_wtmp, B_memb, B_memT = Buf("wtmp"), Buf("memb"), Buf("memT")
        load_ln(0)
        mem_kv(0, wtmp, B_wtmp, memb, B_memb, memT, B_memT)
        load_w(win, w_in_d[0], 8, DIN, B_win, "win")
        load_w(poolw, pool_w_d.rearrange("g r c -> (g r) c"), 8, 256, B_poolw, "poolw")
        load_w(wout, w_out_d[0], 12, D, B_wout, "wout")
        P.barrier([B_mkv])
        A.release(m_tmp)

        xb = [A.t(BF16, NB_A, D)] * 2
        B_xb = [Buf("xb0")] * 2
        xc = A.t(F32, NB_A, D)
        B_xc = Buf("xc")
        xf = [A.t(F32, D) for _ in range(2)]
        B_xf = [Buf("xf0"), Buf("xf1")]
        xT = [A.t(BF16, 8, TA) for _ in range(2)]
        B_xT = [Buf("xT0"), Buf("xT1")]
        HW_ = TA + 16
        U = [A.t(F32, 2, HW_) for _ in range(4)]
        B_U = [Buf(f"U{g}") for g in range(4)]
        T1 = A.t(F32, 2, HW_)
        T2 = A.t(F32, 2, HW_)
        B_T1, B_T2 = Buf("T1"), Buf("T2")
        Hh = A.t(F32, 8, 16)
        B_H = [Buf(f"H{c}") for c in range(8)]
        pm = [A.t(BF16, 2, TA) for _ in range(4)]
        B_pm = [Buf(f"pm{g}") for g in range(4)]
        gsb = A.t(F32, 12, TA)
        B_gsb = [Buf(f"gsb{i}") for i in range(12)]
        qm = A.t(BF16, 4, TA)
        B_qm = Buf("qm")
        PT8 = [[A.t(BF16, TA) for _ in range(2)] for _ in range(4)]
        B_PT8 = [[Buf(f"PT{h}{j}") for j in range(2)] for h in range(4)]
        rd = A.t(F32, TA)
        t1 = A.t(F32, TA)
        B_rd, B_t1 = Buf("rd"), Buf("t1")
        tfix = A.t(F32, 16)
        B_tfix = Buf("tfix")
        gated = A.t(BF16, 12, TA)
        B_gated = Buf("gated")
        zz = [A.t(F32, D) for _ in range(2)]
        zn = [A.t(F32, D) for _ in range(2)]
        B_zz = [Buf("z0"), Buf("z1")]
        B_zn = [Buf("zn0"), Buf("zn1")]
        junk, B_junk = None, None
        x1b = A.t(BF16, NB_A, D)
        B_x1b = [Buf(f"x1b{b}") for b in range(NB_A)]
        x1T = A.t(BF16, 8, TA)
        B_x1T = Buf("x1T")
        P.op("pool", I("memset", Hh, 0.0), w=B_H)

        def load_xc(i):
            if i < NTA:
                sp_load(xc, x_d[i * TA:(i + 1) * TA, :].rearrange("(b p) d -> p b d", p=128), [B_xc], "xc")

        def st_T(i):
            s = i % 2
            for b in range(NB_A):
                P.op("act", I("activation", out=xb[s][:, b, :], in_=xc[:, b, :], func=AF.Copy), r=[B_xc],
                     w=[B_xb[s]] if b == 0 else [], wj=[] if b == 0 else [B_xb[s]])
            transposes_to(xT[s], xb[s], NB_A, B_xb[s], B_xT[s], i, all_act=True)
            load_xc(i + 1)

        def st_U(i):
            s = i % 2
            for g in range(4):
                for ch in range(2):
                    c = 2 * g + ch
                    bk = nb()
                    proj(bk, win, B_win, c * 128, 128, xT[s], B_xT[s], TA)
                    P.op("pool", I("tensor_copy", out=U[g][:, ch, 0:16], in_=Hh[:, c, :]),
                         r=[B_H[c]], w=[B_U[g]] if ch == 0 else [], wj=[] if ch == 0 else [B_U[g]])
                    P.op("act", I("activation", out=U[g][:, ch, 16:16 + TA], in_=bk.f[:, 0:TA], func=AF.Copy),
                         r=[bk.buf], wj=[B_U[g]])
                    P.op("act", I("activation", out=Hh[:, c, :], in_=bk.f[:, TA - 16:TA], func=AF.Copy),
                         r=[bk.buf], w=[B_H[c]])

        def st_Upool(i):
            for g in range(4):
                src_, Bs = U[g], B_U[g]
                dsts = [(T1, B_T1), (T2, B_T2)]
                sh = 1
                for lvl in range(g + 1):
                    dstt, Bd = dsts[lvl % 2]
                    lo = 2 * sh - 1
                    P.op("dve", I("tensor_tensor", out=dstt[:, :, lo:HW_], in0=src_[:, :, lo:HW_],
                                  in1=src_[:, :, lo - sh:HW_ - sh], op=ALU.add), r=[Bs], w=[Bd])
                    src_, Bs = dstt, Bd
                    sh *= 2
                wv = POOLW[g]
                P.op("dve", I("scalar_tensor_tensor", out=pm[g][:, :, :], in0=src_[:, :, 16:HW_], scalar=1.0 / wv,
                              in1=U[g][:, :, 16:HW_], op0=ALU.mult, op1=ALU.subtract), r=[Bs, B_U[g]], w=[B_pm[g]])
                if i == 0:
                    for ch in range(2):
                        P.op("dve", I("tensor_tensor", out=tfix, in0=src_[:, ch, 16:32], in1=invc[:, g, :], op=ALU.mult),
                             r=[Bs, B_const], w=[B_tfix])
                        P.op("dve", I("tensor_tensor", out=pm[g][:, ch, 0:16], in0=tfix, in1=U[g][:, ch, 16:32],
                                      op=ALU.subtract), r=[B_tfix, B_U[g]], wj=[B_pm[g]])

        def st_S2(i):
            s = i % 2
            xTs, BxT = xT[s], B_xT[s]
            for h in range(4):
                bk = nb()
                proj(bk, win, B_win, D + h * 128, 128, xTs, BxT, TA)
                P.op("act", I("activation", out=qm[:, h, :], in_=bk.f[:, 0:TA], func=AF.Copy, scale=128.0 ** -0.5),
                     r=[bk.buf], w=[B_qm] if h == 0 else [], wj=[] if h == 0 else [B_qm])

            def gate(c):
                gb = nb()
                proj(gb, win, B_win, DMIX + c * 128, 128, xTs, BxT, TA)
                P.op("act", I("activation", out=gsb[:, c, :], in_=gb.f[:, 0:TA], func=AF.Silu),
                     r=[gb.buf], w=[B_gsb[c]])
            for c in range(4):
                gate(c)
            for h in range(4):
                for j in range(2):
                    lb = nb()
                    P.op("pe", I("matmul", lb.f[:, 0:TA], mkT[:, h, j * 128:(j + 1) * 128], qm[:, h, :], start=True, stop=True),
                         r=[B_mkv, B_qm], w=[lb.buf])
                    P.op("act", I("activation", out=PT8[h][j], in_=lb.f[:, 0:TA], func=AF.Exp),
                         r=[lb.buf], w=[B_PT8[h][j]])
            for c in range(4, 12):
                gate(c)
            ybks, dbks = [], []
            for h in range(4):
                ybk = nb()
                if h % 2 == 0:
                    dbk2 = nb()
                ybks.append(ybk)
                dbks.append((dbk2, (h % 2) * TA))
                for j in range(2):
                    P.op("pe", I("matmul", ybk.f[:, 0:TA], mv[:, j, h * 128:(h + 1) * 128], PT8[h][j], start=(j == 0),
                                 stop=(j == 1)), r=[B_mkv, B_PT8[h][j]], w=[ybk.buf])
                for j in range(2):
                    P.op("pe", I("matmul", dbk2.f[:, (h % 2) * TA:(h % 2) * TA + TA], ones16, PT8[h][j], start=(j == 0),
                                 stop=(j == 1)), r=[B_const, B_PT8[h][j]], w=[dbk2.buf])
            for h in range(4):
                ybk = ybks[h]
                dbk, doff = dbks[h]
                P.op("dve", I("reciprocal", out=rd, in_=dbk.f[:, doff:doff + TA]), r=[dbk.buf], w=[B_rd])
                P.op("dve", I("tensor_tensor", out=t1, in0=ybk.f[:, 0:TA], in1=rd, op=ALU.mult),
                     r=[ybk.buf, B_rd], w=[B_t1])
                P.op("dve", I("tensor_tensor", out=gated[:, 8 + h, :], in0=t1, in1=gsb[:, 8 + h, :], op=ALU.mult),
                     r=[B_t1, B_gsb[8 + h]], w=[B_gated] if h == 0 else [], wj=[] if h == 0 else [B_gated])

        def st_S3(i):
            for g in range(4):
                for oc in range(2):
                    c = 2 * g + oc
                    mb = nb()
                    for kc in range(2):
                        P.op("pe", I("matmul", mb.f[:, 0:TA], poolw[:, g * 2 + kc, oc * 128:(oc + 1) * 128], pm[g][:, kc, :],
                                     start=(kc == 0), stop=(kc == 1)), r=[B_poolw, B_pm[g]], w=[mb.buf])
                    P.op("dve", I("scalar_tensor_tensor", out=gated[:, c, :], in0=mb.f[:, 0:TA], scalar=pscale[:, c:c + 1],
                                  in1=gsb[:, c, :], op0=ALU.mult, op1=ALU.mult),
                         r=[mb.buf, B_gsb[c], B_const], wj=[B_gated])

        def st_S4(i):
            t0 = i * TA
            g512 = t0 // 512
            for b in range(NB_A):
                q = (i * NB_A + b) % 2
                sp_load(xf[q], x_d[t0 + b * 128:t0 + (b + 1) * 128, :], [B_xf[q]], f"xf{q}")
            for b in range(NB_A):
                q = (i * NB_A + b) % 2
                yb = [nb(), nb()]
                for hf in range(2):
                    for k in range(12):
                        P.op("pe", I("matmul", yb[hf].f[:, :], gated[:, k, b * 128:(b + 1) * 128],
                                     wout[:, k, hf * 512:(hf + 1) * 512], start=(k == 0), stop=(k == 11)),
                             r=[B_gated, B_wout], w=[yb[hf].buf])
                tok0 = t0 + b * 128
                ln_block(yb[0], yb[1], xf[q], B_xf[q], zz[q], B_zz[q], zn[q], B_zn[q], junk, B_junk, q,
                         x1b[:, b, :], B_x1b[b], X1F[tok0:tok0 + 128, :], B_X1F[g512], f"x1f{q}",
                         B_dst_join=(tok0 % 512 != 0), ln_act=True, part="z")
            for b in range(NB_A):
                q = (i * NB_A + b) % 2
                tok0 = t0 + b * 128
                ln_block(None, None, xf[q], B_xf[q], zz[q], B_zz[q], zn[q], B_zn[q], junk, B_junk, q,
                         x1b[:, b, :], B_x1b[b], X1F[tok0:tok0 + 128, :], B_X1F[g512], f"x1f{q}",
                         B_dst_join=(tok0 % 512 != 0), ln_act=True, part="rest")

        def st_X1(i):
            t0 = i * TA
            g512 = t0 // 512
            transposes_to(x1T, x1b, NB_A, B_x1b, B_x1T, i, all_act=True)
            if t0 % 512 == 0:
                P.op("sp", I("dma_start", out=X1T[:, :, t0:t0 + TA].rearrange("c p t -> p c t"), in_=x1T),
                     r=[B_x1T], w=[B_X1T[g512]], dma=True, key="x1t")
            else:
                P.op("sp", I("dma_start", out=X1T[:, :, t0:t0 + TA].rearrange("c p t -> p c t"), in_=x1T),
                     r=[B_x1T], wj=[B_X1T[g512]], dma=True, key="x1t")

        load_xc(0)
        st_T(0)
        st_U(0)
        st_Upool(0)
        for i in range(NTA):
            if i + 1 < NTA:
                st_T(i + 1)
            st_S2(i)
            st_S3(i)
            if i + 1 < NTA:
                st_U(i + 1)
            if i >= 1:
                st_X1(i - 1)
            st_S4(i)
            if i + 1 < NTA:
                st_Upool(i + 1)
        st_X1(NTA - 1)
        P.barrier(B_X1F + B_X1T)
        A.release(base_mark)

    pre_b = None
    if "b" in phases:
        winB = A.t(BF16, 8, DIN)
        woutB = A.t(BF16, 12, D)
        B_winB, B_woutB = Buf("win1"), Buf("wout1")
        a2_base = A.mark()
        m_tmpB = A.mark()
        wtmpB = A.t(BF16, 8, D)
        membB = A.t(BF16, 2, D)
        memTB = A.t(BF16, 8, NMEM)
        B_wtmpB, B_membB, B_memTB = Buf("wtmp1"), Buf("memb1"), Buf("memT1")
        pre_b = True
    else:
        a2_base = base_mark
    if "a2" in phases:
        wkv = A.t(BF16, 8, 2064)
        B_wkv = Buf("wkv")
        load_w(wkv, w_kv_d, 8, 2064, B_wkv, "wkv")
    if pre_b:
        load_ln(1)
        mem_kv(1, wtmpB, B_wtmpB, membB, B_membB, memTB, B_memTB)
        load_w(winB, w_in_d[1], 8, DIN, B_winB, "win")
        load_w(woutB, w_out_d[1], 12, D, B_woutB, "wout")
    if "a2" in phases:
        xt2 = [A.t(BF16, 8, 512) for _ in range(2)]
        B_xt2 = [Buf("xt2a"), Buf("xt2b")]
        ksb = A.t(BF16, 8, 512)
        B_ksb = Buf("ksb")
        vsb = A.t(BF16, 4, NH, 128)
        B_vsb = Buf("vsb")
        bfor4 = A.t(F32, 4, 16)
        fl = A.t(F32, 4, 16)
        e1 = A.t(F32, 4, 16)
        lp = A.t(F32, 4, 16)
        r1 = A.t(F32, 16)
        r2 = A.t(F32, 16)
        B_fl, B_e1, B_lp, B_r1, B_r2 = Buf("fl"), Buf("e1"), Buf("lp"), Buf("r1"), Buf("r2")
        B_bf4 = Buf("bf4")
        for b in range(4):
            P.op("pool", I("tensor_copy", out=bfor4[:, b, :], in_=bfor), r=[B_const],
                 w=[B_bf4] if b == 0 else [], wj=[] if b == 0 else [B_bf4])
        P.op("pool", I("memset", vsb[:, :, :, 64:128], 1.0), w=[B_vsb])
        sp_load(xt2[0], X1T[:, :, 0:512].rearrange("c p t -> p c t"), [B_xt2[0]], "xt2a", r=[B_X1T[0]])
        for g in range(NT512):
            s = g % 2
            if g + 1 < NT512:
                sp_load(xt2[1 - s], X1T[:, :, (g + 1) * 512:(g + 2) * 512].rearrange("c p t -> p c t"),
                        [B_xt2[1 - s]], "xt2b" if 1 - s else "xt2a", r=[B_X1T[g + 1]])
            xt = xt2[s]
            Bxt = B_xt2[s]
            fb = nb()
            for b in range(4):
                for k in range(8):
                    P.op("pe", I("matmul",
                        fb.f[:, b * 16:(b + 1) * 16], xt[:, k, b * 128:(b + 1) * 128], wkv[:, k, 2048:2064],
                        start=(k == 0), stop=(k == 7)), r=[Bxt, B_wkv], w=[fb.buf])
            P.op("dve", I("tensor_tensor", out=fl, in0=fb.f[:, 0:64].rearrange("p (b h) -> p b h", b=4),
                                                         in1=bfor4, op=ALU.add), r=[fb.buf, B_bf4], w=[B_fl])
            P.op("act", I("activation", out=e1, in_=fl, func=AF.Exp, scale=-1.0), r=[B_fl], w=[B_e1])
            P.op("act", I("activation", out=lp, in_=e1, func=AF.Ln, bias=1.0), r=[B_e1], w=[B_lp])
            for b in range(4):
                n = g * 4 + b
                cb = nb()
                P.op("pe", I("matmul", cb.f[:, 0:16], tri, lp[:, b, :], start=True, stop=True),
                     r=[B_const, B_lp], w=[cb.buf])
                P.op("pe", I("matmul", cb.f[:, 16:32], allones, lp[:, b, :], start=True, stop=True),
                     r=[B_const, B_lp], w=[cb.buf])
                P.op("dve", I("tensor_tensor", out=CK[:, n, :], in0=cb.f[:, 0:16], in1=Tn, op=ALU.add),
                     r=[cb.buf, B_Tn], w=[B_CK[n]])
                P.op("dve", I("tensor_tensor", out=Tn, in0=cb.f[:, 16:32], in1=Tn, op=ALU.add),
                     r=[cb.buf], w=[B_Tn])
                CSv = CS[:, n, :].rearrange("p (h i) -> p h i", i=3)
                P.op("pool", I("tensor_scalar", out=CSv[:, :, 0], in0=CK[:, n, :], scalar1=-8.0,
                                                                      scalar2=None, op0=ALU.mult),
                     r=[B_CK[n]], w=[B_CS[n]])
                P.op("dve", I("scalar_tensor_tensor", out=r1, in0=CK[:, n, :], scalar=-8.0,
                                                                           in1=CSv[:, :, 0], op0=ALU.mult,
                                                                           op1=ALU.subtract),
                     r=[B_CK[n], B_CS[n]], w=[B_r1])
                P.op("pool", I("tensor_copy", out=CSv[:, :, 1], in_=r1), r=[B_r1], wj=[B_CS[n]])
                P.op("pool", I("tensor_tensor", out=r2, in0=r1, in1=CSv[:, :, 1], op=ALU.subtract),
                     r=[B_r1, B_CS[n]], w=[B_r2])
                P.op("pool", I("tensor_copy", out=CSv[:, :, 2], in_=r2), r=[B_r2], wj=[B_CS[n]])
            for c in range(8):
                bk = nb()
                proj(bk, wkv, B_wkv, c * 128, 128, xt, Bxt, 512)
                if c % 2 == 0:
                    P.op("act", I("activation", out=ksb[:, c, :], in_=bk.f[:, :], func=AF.Copy),
                         r=[bk.buf], w=[B_ksb] if c == 0 else [], wj=[] if c == 0 else [B_ksb])
                else:
                    P.op("dve", I("tensor_copy", out=ksb[:, c, :], in_=bk.f[:, :]),
                         r=[bk.buf], wj=[B_ksb])
            P.op("sp", I("dma_start", out=KT[:, :, g * 512:(g + 1) * 512].rearrange("c p t -> p c t"),
                                                  in_=ksb), r=[B_ksb], w=[B_KV[g]], dma=True, key="kst")
            for b in range(4):
                for hf in range(2):
                    bk = nb()
                    for k in range(8):
                        P.op("pe", I("matmul",
                            bk.f[:, :], xt[:, k, b * 128:(b + 1) * 128], wkv[:, k, 1024 + hf * 512:1024 + (hf + 1) * 512],
                            start=(k == 0), stop=(k == 7)), r=[Bxt, B_wkv], w=[bk.buf])
                    src_v = bk.f[:, :].rearrange("p (h d) -> p h d", h=8)
                    first = (b == 0 and hf == 0)
                    if (b + hf) % 2 == 0:
                        P.op("dve", I("tensor_copy",
                            out=vsb[:, b, hf * 8:(hf + 1) * 8, 0:64], in_=src_v),
                            r=[bk.buf], w=[B_vsb] if first else [], wj=[] if first else [B_vsb])
                    else:
                        P.op("act", I("activation",
                            out=vsb[:, b, hf * 8:(hf + 1) * 8, 0:64], in_=src_v, func=AF.Copy),
                            r=[bk.buf], wj=[B_vsb])
            P.op("sp", I("dma_start", out=VA[g * 4:(g + 1) * 4].rearrange("b p h e -> p b h e"), in_=vsb),
                 r=[B_vsb], wj=[B_KV[g]], dma=True, key="vst")
        if debug:
            P.op("sp", I("dma_start", out=CK_d, in_=CK.rearrange("p n h -> p (n h)")), r=B_CK, dma=True, key="dbg")
        P.barrier(B_KV + B_CK + B_CS + [B_mkv])
        A.release(a2_base)

    if "b" in phases:
        win, wout, B_win, B_wout = winB, woutB, B_winB, B_woutB
        masks = A.t(BF16, 2, 4, 512)
        B_masks = Buf("masks")
        am = A.t(F32, 2)
        P.op("dve", I("tensor_scalar", out=am, in0=selt, scalar1=float(ALPHA), scalar2=None, op0=ALU.mult),
             r=[B_const], wj=[B_const])
        if "a2" not in phases:
            P.barrier([B_mkv])

        xst = [A.t(BF16, 2, 2, 512)] * 2
        B_xst = [Buf("xst0")] * 2
        xsel = A.t(BF16, 8, 512)
        B_xsel = Buf("xsel")
        csel = A.t(BF16, 4, 48)
        B_csel = Buf("csel")
        csT = A.t(BF16, 512)
        B_csT = Buf("csT")
        gmain = [A.t(F32, 2, 512) for _ in range(2)]
        B_gmain = [Buf("gm0"), Buf("gm1")]
        Qa = [A.t(BF16, 4, 512) for _ in range(2)]
        B_Qa = [Buf("Qa0"), Buf("Qa1")]
        kbuf = [A.t(BF16, 4, 512) for _ in range(2)]
        B_kbuf = [Buf("kb0"), Buf("kb1")]
        vbuf = [A.t(BF16, 4, 4, 128) for _ in range(2)]
        B_vbuf = [Buf("vb0"), Buf("vb1")]
        PTb = [A.t(BF16, 512) for _ in range(3)]
        B_PTb = [Buf(f"ptb{i}") for i in range(3)]
        gsb = [A.t(F32, 512) for _ in range(2)]
        B_gsb = [Buf("gsbB0"), Buf("gsbB1")]
        qm = A.t(BF16, 4, 512)
        B_qm = Buf("qmB")
        PT = PTb[0:2]
        B_PT = B_PTb[0:2]
        rd = A.t(F32, 512)
        t1 = A.t(F32, 512)
        B_rd, B_t1 = Buf("rdB"), Buf("t1B")
        rden = [A.t(F32, 512) for _ in range(2)]
        B_rden = [Buf("rden0"), Buf("rden1")]
        gated = A.t(BF16, 12, 512)
        B_gated = Buf("gatedB")
        xfA = [A.t(F32, D)] * 2
        xfB = [A.t(F32, D)] * 2
        B_xfA = [Buf("xfA0")] * 2
        B_xfB = [Buf("xfB0")] * 2
        zz = [A.t(F32, D) for _ in range(2)]
        zn = [A.t(F32, D) for _ in range(2)]
        B_zz = [Buf("zB0"), Buf("zB1")]
        B_zn = [Buf("znB0"), Buf("znB1")]
        junk, B_junk = None, None
        tmpn = [rd, t1, gsb[0], gsb[1]]
        B_tmpn = [B_rd, B_t1, B_gsb[0], B_gsb[1]]
        B_out = Buf("out")
        for i in range(2):
            P.op("pool", I("memset", kbuf[i][64:67, :, :], 1.0), w=[B_kbuf[i]])

        Obk = banks[0:4]
        Sbk = banks[4:7]
        Mbk = banks[7]
        kv_rr = [0]
        s_rr = [0]
        pt_rr = [0]
        gs_rr = [0]
        xs_rr = [0]
        def do_select(g0, g1):
                for pc in range(4):
                    q = xs_rr[0] % 2
                    xs_rr[0] += 1
                    pq_load(xst[q][:, 0, :, :], X1T[2 * pc:2 * pc + 2, :, g0 * 512:(g0 + 1) * 512].rearrange("c p t -> p c t"),
                            [B_xst[q]], "xst")
                    P.op("pool", I("dma_start",
                        out=xst[q][:, 1, :, :], in_=X1T[2 * pc:2 * pc + 2, :, g1 * 512:(g1 + 1) * 512].rearrange("c p t -> p c t")),
                        wj=[B_xst[q]], dma=True, key="xst")
                    P.op("dve", I("tensor_scalar", out=xsel[:, 2 * pc:2 * pc + 2, :], in0=xst[q][:, 0, :, :],
                                                                     scalar1=selt[:, 0:1], scalar2=None, op0=ALU.mult),
                         r=[B_xst[q], B_const], w=[B_xsel] if pc == 0 else [], wj=[] if pc == 0 else [B_xsel])
                    P.op("dve", I("scalar_tensor_tensor",
                        out=xsel[:, 2 * pc:2 * pc + 2, :], in0=xst[q][:, 1, :, :], scalar=selt[:, 1:2],
                        in1=xsel[:, 2 * pc:2 * pc + 2, :], op0=ALU.mult, op1=ALU.add),
                        r=[B_xst[q], B_const], wj=[B_xsel])
                P.op("dve", I("tensor_scalar", out=csel, in0=CS[:, g0 * 4:(g0 + 1) * 4, :], scalar1=selt[:, 0:1],
                                                             scalar2=None, op0=ALU.mult),
                     r=B_CS[g0 * 4:(g0 + 1) * 4] + [B_const], w=[B_csel])
                P.op("dve", I("scalar_tensor_tensor", out=csel, in0=CS[:, g1 * 4:(g1 + 1) * 4, :],
                                                                    scalar=selt[:, 1:2], in1=csel, op0=ALU.mult, op1=ALU.add),
                     r=B_CS[g1 * 4:(g1 + 1) * 4] + [B_const], wj=[B_csel])

        slots = slot_tiles(NT512)
        def proj_items(G):
            gq = G % 2
            items = []
            for oc in range(2):
                def it(bkf, oc=oc):
                    gb = bkf()
                    proj(gb, win, B_win, DMIX + (2 * G + oc) * 128, 128, xsel, B_xsel, 512)
                    P.op("act", I("activation", out=gmain[gq][:, oc, :], in_=gb.f[:, :], func=AF.Silu),
                         r=[gb.buf], w=[B_gmain[gq]] if oc == 0 else [], wj=[] if oc == 0 else [B_gmain[gq]])
                items.append(it)
            for hl in range(4):
                def it(bkf, hl=hl):
                    h = 4 * G + hl
                    qb = bkf()

                    def efirst(bk):
                        P.op("pe", I("matmul", bk.f[0:67, :], Emat[0:48, h, :], csT[0:48, :], start=True, stop=False),
                             r=[B_const, B_csT], w=[bk.buf])
                    proj(qb, win, B_win, h * 64, 64, xsel, B_xsel, 512, extra_first=efirst)
                    if hl % 2 == 0:
                        P.op("act", I("activation", out=Qa[gq][0:67, hl, :], in_=qb.f[0:67, :], func=AF.Copy),
                             r=[qb.buf], w=[B_Qa[gq]] if hl == 0 else [], wj=[] if hl == 0 else [B_Qa[gq]])
                    else:
                        P.op("dve", I("tensor_copy", out=Qa[gq][0:67, hl, :], in_=qb.f[0:67, :]),
                             r=[qb.buf], wj=[B_Qa[gq]])
                items.append(it)
            return items

        def pre_A(j, g0, g1):
            sp_load(masks, masks_d[j % 2].rearrange("p (a b c) -> p a b c", a=2, b=4), [B_masks], "masks")
            if j == 0:
                do_select(g0, g1)
            cbk = nb()
            for b in range(4):
                P.op("pe", I("transpose", out=cbk.h[0:48, b * 128:(b + 1) * 128], in_=csel[:, b, :],
                                                               identity=ident), r=[B_csel, B_const], w=[cbk.buf])
            P.op("dve", I("tensor_copy", out=csT[0:48, :], in_=cbk.h[0:48, 0:512]), r=[cbk.buf], w=[B_csT])

            for h in range(4):
                bk = nb()
                proj(bk, win, B_win, D + h * 128, 128, xsel, B_xsel, 512)
                P.op("act", I("activation", out=qm[:, h, :], in_=bk.f[:, :], func=AF.Copy,
                                                               scale=128.0 ** -0.5),
                     r=[bk.buf], w=[B_qm] if h == 0 else [], wj=[] if h == 0 else [B_qm])
            for it in proj_items(0):
                it(nb)

        def pre_B(j):
            for h in range(4):
                ybk, dbk = nb(), nb()
                mem_attn(h, qm, B_qm, 512, PT, B_PT, ybk, dbk)
                gb = nb()
                proj(gb, win, B_win, DMIX + D + h * 128, 128, xsel, B_xsel, 512)
                gi = gs_rr[0] % 2
                gs_rr[0] += 1
                P.op("act", I("activation", out=gsb[gi], in_=gb.f[:, :], func=AF.Silu),
                     r=[gb.buf], w=[B_gsb[gi]])
                P.op("dve", I("reciprocal", out=rd, in_=dbk.f[:, :]), r=[dbk.buf], w=[B_rd])
                P.op("dve", I("tensor_tensor", out=t1, in0=ybk.f[:, :], in1=rd, op=ALU.mult),
                     r=[ybk.buf, B_rd], w=[B_t1])
                P.op("dve", I("tensor_tensor", out=gated[:, 8 + h, :], in0=t1, in1=gsb[gi], op=ALU.mult),
                     r=[B_t1, B_gsb[gi]], w=[B_gated] if h == 0 else [], wj=[] if h == 0 else [B_gated])

        def attention(j, nch):
            norm_pending = []
            for G in range(4):
                gq = G % 2
                nxt = proj_items(G + 1) if G < 3 else []
                if G == 3 and j + 1 < len(slots):
                    do_select(slots[j + 1][0], slots[j + 1][1])
                units = [(kc, kb, hl) for kc in range(nch) for kb in range(4) for hl in range(4)]
                pend = []
                kq = 0
                for idx, (kc, kb, hl) in enumerate(units):
                    if norm_pending and idx in (2, 4):
                        norm_pending.pop(0)()
                    h = 4 * G + hl
                    n = kc * 4 + kb
                    mtype = kc - (nch - 2)
                    if kb == 0 and hl == 0:
                        kq = kv_rr[0] % 2
                        kv_rr[0] += 1
                        sp_load(kbuf[kq][0:64, :, :],
                                KT.rearrange("c (two d) t -> (c two) d t", two=2)[4 * G:4 * G + 4, :, kc * 512:(kc + 1) * 512]
                                .rearrange("h d t -> d h t"), [B_kbuf[kq]], f"kb{kq}", r=[B_KV[kc]])
                        sp_load(vbuf[kq], VA[kc * 4:(kc + 1) * 4, :, 4 * G:4 * G + 4, :].rearrange("b p h e -> p b h e"),
                                [B_vbuf[kq]], f"vb{kq}", r=[B_KV[kc]])
                    sb = Sbk[s_rr[0] % 3]
                    s_rr[0] += 1
                    P.op("pe", I("matmul", sb.f[:, :], kbuf[kq][0:67, hl, kb * 128:(kb + 1) * 128], Qa[gq][0:67, hl, :],
                                 start=True, stop=(mtype < 0)), r=[B_kbuf[kq], B_Qa[gq]], w=[sb.buf])
                    if mtype >= 0:
                        P.op("pe", I("matmul", sb.f[:, :], ident, masks[:, mtype, kb, :], start=False, stop=True),
                             r=[B_const, B_masks], w=[sb.buf])
                    pi = pt_rr[0] % 3
                    pt_rr[0] += 1
                    P.op("act", I("activation", out=PTb[pi], in_=sb.f[:, :], func=AF.Exp, scale=0.125,
                                  bias=CK[:, n, h:h + 1]), r=[sb.buf, B_CK[n]], w=[B_PTb[pi]])

                    def pv(hl=hl, kq=kq, kb=kb, pi=pi, first=(idx < 4), last=(idx >= len(units) - 4)):
                        P.op("pe", I("matmul", Obk[hl].f[:, :], vbuf[kq][:, kb, hl, :], PTb[pi], start=first, stop=last),
                             r=[B_vbuf[kq], B_PTb[pi]], w=[Obk[hl].buf])
                    pend.append(pv)
                    if len(pend) > 2:
                        pend.pop(0)()
                    if nxt and idx >= 3 and idx % 2 == 1:
                        nxt.pop(0)(lambda: Mbk)
                for f in pend:
                    f()
                for it in nxt:
                    it(lambda: Mbk)
                def mk_norm(G, pair, gq):
                    def f():
                        hls = (2 * pair, 2 * pair + 1)
                        for hl in hls:
                            po = (hl % 2) * 64
                            P.op("dve", I("tensor_tensor", out=tmpn[hl][0:64, :], in0=Obk[hl].f[0:64, :],
                                          in1=gmain[gq][po:po + 64, hl // 2, :], op=ALU.mult),
                                 r=[Obk[hl].buf, B_gmain[gq]], w=[B_tmpn[hl]])
                        for hl in hls:
                            P.op("act", I("activation", out=rden[hl % 2][0:64, :], in_=Obk[hl].f[64:128, :], func=AF.Ln),
                                 r=[Obk[hl].buf], w=[B_rden[hl % 2]])
                        for hl in hls:
                            P.op("act", I("activation", out=rden[hl % 2][0:64, :], in_=rden[hl % 2][0:64, :], func=AF.Exp,
                                          scale=-1.0), r=[B_rden[hl % 2]], w=[B_rden[hl % 2]])
                        for hl in hls:
                            h = 4 * G + hl
                            po = (hl % 2) * 64
                            P.op("dve", I("tensor_tensor", out=gated[po:po + 64, h // 2, :], in0=tmpn[hl][0:64, :],
                                          in1=rden[hl % 2][0:64, :], op=ALU.mult), r=[B_tmpn[hl], B_rden[hl % 2]],
                                 wj=[B_gated])
                    return f
                norm_pending = [mk_norm(G, pair, gq) for pair in range(2)]
                if G == 3:
                    for f in norm_pending:
                        f()
                    norm_pending = []

        def post(j, g0, g1):
            for b in (0, 1, -1, 2, 3, -2):
                if b < 0:
                    for b2 in ((0, 1) if b == -1 else (2, 3)):
                        q = b2 % 2
                        row0 = j * 512 + b2 * 128
                        ln_block(None, None, [(xfA[q], B_xfA[q], am[:, 0:1]), (xfB[q], B_xfB[q], am[:, 1:2])], None,
                                 zz[q], B_zz[q], zn[q], B_zn[q], junk, B_junk, q, None, None,
                                 out_d[row0:row0 + 128, :], B_out, f"outst{q}", B_dst_join=True, st_eng="pool",
                                 part="rest")
                    continue
                q = b % 2
                pq_load(xfA[q], X1F[g0 * 512 + b * 128:g0 * 512 + (b + 1) * 128, :], [B_xfA[q]], "xfA", r=[B_X1F[g0]])
                pq_load(xfB[q], X1F[g1 * 512 + b * 128:g1 * 512 + (b + 1) * 128, :], [B_xfB[q]], "xfB", r=[B_X1F[g1]])
                yb = [nb(), nb()]
                for hf in range(2):
                    for k in range(12):
                        P.op("pe", I("matmul",
                            yb[hf].f[:, :], gated[:, k, b * 128:(b + 1) * 128], wout[:, k, hf * 512:(hf + 1) * 512],
                            start=(k == 0), stop=(k == 11)), r=[B_gated, B_wout], w=[yb[hf].buf])
                row0 = j * 512 + b * 128
                ln_block(yb[0], yb[1], [(xfA[q], B_xfA[q], am[:, 0:1]), (xfB[q], B_xfB[q], am[:, 1:2])], None,
                         zz[q], B_zz[q], zn[q], B_zn[q], junk, B_junk, q, None, None,
                         out_d[row0:row0 + 128, :], B_out, f"outst{q}", B_dst_join=True, st_eng="pool", part="z")
        for j, (g0, g1, nch) in enumerate(slots):
            if j == 0:
                pre_A(0, g0, g1)
                pre_B(0)
            attention(j, nch)
            if j + 1 < len(slots):
                pre_A(j + 1, slots[j + 1][0], slots[j + 1][1])
            post(j, g0, g1)
            if j + 1 < len(slots):
                pre_B(j + 1)
        A.release(base_mark)

    P.emit()
    return nc


def _consts():
    bf = ml_dtypes.bfloat16
    cb = np.zeros((128, 3 * 128 + 16 * 67), np.float32)
    cb[:, 0:128] = np.eye(128)
    cb[:, 128:256] = 1.0
    E = np.zeros((128, 16, 67), np.float32)
    for h in range(16):
        for i in range(3):
            E[h * 3 + i, h, 64 + i] = 1.0
    cb[:, 384:] = E.reshape(128, -1)
    cf = np.zeros((128, 384), np.float32)
    s = np.arange(128)
    cf[:, 0:128] = (s[:, None] <= s[None, :]).astype(np.float32)
    cf[:, 128:256] = 1.0
    invc = np.zeros((4, 16), np.float32)
    for g, w in enumerate(POOLW):
        invc[g] = 1.0 / np.minimum(np.arange(16) + 1, w)
    cf[:, 256:320] = invc.reshape(1, 64)
    cf[:, 320:384] = 1.0
    return cb.astype(bf), cf


def _masks(hh):
    bf = ml_dtypes.bfloat16
    s = np.arange(128)[:, None, None]
    kb = np.arange(4)[None, :, None]
    t = np.arange(512)[None, None, :]
    diag = np.where(kb * 128 + s > t, MASKNEG, 0.0).astype(np.float32)
    full = np.full((128, 4, 512), MASKNEG, np.float32)
    none = np.zeros((128, 4, 512), np.float32)
    a = np.stack([diag, full], axis=1).reshape(128, -1)
    b = np.stack([none, diag], axis=1).reshape(128, -1)
    m = np.stack([a, b] if hh == 0 else [b, a], axis=0)
    return m.astype(bf)


def make_in_maps(inputs, S):
    cb, cf = _consts()
    x = np.asarray(inputs["x"], np.float32)
    B = x.shape[0]
    lnp = np.ascontiguousarray(np.stack([inputs["ln_g"], inputs["ln_b"]], axis=1).astype(np.float32))
    pscale = np.ascontiguousarray(np.asarray(inputs["pool_scale"], np.float32).reshape(8, 128).T)
    maps = []
    for c in range(2 * B):
        b, hh = c // 2, c % 2
        sel = np.zeros((128, 2), np.float32)
        sel[:, hh] = 1.0
        maps.append({
            "x": np.ascontiguousarray(x[b]),
            "mem": np.ascontiguousarray(np.asarray(inputs["mem"], np.float32)[b]),
            "w_in": np.asarray(inputs["w_in"], np.float32),
            "w_mem_kv": np.asarray(inputs["w_mem_kv"], np.float32),
            "w_out": np.asarray(inputs["w_out"], np.float32),
            "pool_w": np.ascontiguousarray(np.asarray(inputs["pool_w"], np.float32)[0]),
            "w_kv": np.asarray(inputs["w_kv_shared"], np.float32),
            "lnp": lnp,
            "pscale": pscale,
            "bfor": np.asarray(inputs["b_forget"], np.float32),
            "cb16": cb,
            "cf32": cf,
            "sel": sel,
            "masks": _masks(hh),
        })
    return maps


_NC_CACHE = {}


def kernel(x, mem, w_in, w_mem_kv, w_out, ln_g, ln_b, pool_w, pool_scale, w_kv_shared, b_forget):
    inputs = dict(x=x, mem=mem, w_in=w_in, w_mem_kv=w_mem_kv, w_out=w_out, ln_g=ln_g, ln_b=ln_b,
                  pool_w=pool_w, pool_scale=pool_scale, w_kv_shared=w_kv_shared, b_forget=b_forget)
    B, S, _ = np.asarray(x).shape
    if S not in _NC_CACHE:
        _NC_CACHE[S] = build(S)
    nc = _NC_CACHE[S]
    maps = make_in_maps(inputs, S)
    res = run_bass_kernel_spmd(nc, maps, core_ids=list(range(2 * B)))
    out = np.zeros((B, S, D), np.float32)
    st = slot_tiles(S // 512)
    for c in range(2 * B):
        b, hh = c // 2, c % 2
        o = np.asarray(res.results[c]["out"], np.float32)
        for j, (g0, g1, _) in enumerate(st):
            g = g0 if hh == 0 else g1
            out[b, g * 512:(g + 1) * 512, :] = o[j * 512:(j + 1) * 512, :]
    return out
```

```python
import numpy as np
import ml_dtypes
import concourse.bass as bass
import concourse.mybir as mybir
from concourse.bass_utils import run_bass_kernel_spmd

dt = mybir.dt
F32, BF16 = dt.float32, dt.bfloat16
AF = mybir.ActivationFunctionType
ALU = mybir.AluOpType

D = 1024
NMEM = 256
DIN = 3072
DMIX = 1536
NH = 16
DH = 64
ALPHA = 4.0 ** 0.25
EPS = 1e-5
POOLW = (2, 4, 8, 16)
MASKNEG = -30000.0

SAME_ENGINE_SYNC = True


class Buf:
    __slots__ = ("name", "w", "r", "jd")

    def __init__(self, name):
        self.name = name
        self.w = []
        self.r = []
        self.jd = []


class Op:
    __slots__ = ("eng", "fn", "deps", "sig", "sem", "val", "dma", "key", "idx")


ENGS = ("pe", "act", "dve", "pool", "sp")


class Prog:
    def __init__(self, nc):
        self.nc = nc
        self.ops = {e: [] for e in ENGS}
        self.n = 0

    def op(self, eng, fn, r=(), w=(), wj=(), dma=False, key=None):
        o = Op()
        o.eng, o.fn, o.dma, o.key = eng, fn, dma, key
        o.sig = dma
        o.sem = None
        o.val = 0
        o.idx = self.n
        self.n += 1
        deps = []
        for b in r:
            deps += b.w
        for b in w:
            deps += b.w
            deps += b.r
        for b in wj:
            deps += b.jd
        for b in r:
            b.r.append(o)
        for b in w:
            b.jd = b.w + b.r
            b.w = [o]
            b.r = []
        for b in wj:
            b.w.append(o)
        seen = set()
        fd = []
        for d in deps:
            if id(d) in seen or d is o:
                continue
            seen.add(id(d))
            if (not d.dma) and d.eng == eng and (eng == "pe" or not SAME_ENGINE_SYNC):
                continue
            fd.append(d)
            d.sig = True
        o.deps = fd
        self.ops[eng].append(o)
        return o

    def barrier(self, bufs):
        b = Buf("barrier")
        lasts = []
        for e in ENGS:
            if self.ops[e]:
                lasts.append(self.ops[e][-1])
        for x in bufs:
            lasts += x.w + x.r
        for e in ENGS:
            o = Op()
            o.eng, o.fn, o.dma, o.key = e, None, False, None
            o.sig = False
            o.sem = None
            o.val = 0
            o.idx = self.n
            self.n += 1
            o.deps = []
            seen = set()
            for d in lasts:
                if id(d) in seen:
                    continue
                seen.add(id(d))
                if (not d.dma) and d.eng == e:
                    continue
                o.deps.append(d)
                d.sig = True
            self.ops[e].append(o)

    def emit(self):
        nc = self.nc
        engsem = {e: nc.alloc_semaphore("s_" + e) for e in ("pe", "act", "dve", "pool")}
        keysem = {}
        cnt = {}
        for e in ENGS:
            for o in self.ops[e]:
                if o.dma:
                    k = o.key
                    if k not in keysem:
                        keysem[k] = nc.alloc_semaphore("d_" + k)
                        cnt[k] = 0
                    cnt[k] += 16
                    o.sem, o.val = keysem[k], cnt[k]
                elif o.sig:
                    cnt[e] = cnt.get(e, 0) + 1
                    o.sem, o.val = engsem[e], cnt[e]
        finals = [(keysem[k], cnt[k]) for k in keysem]
        ops = self.ops

        def mk(e):
            def body(eng):
                wm = {}
                for o in ops[e]:
                    need = {}
                    for d in o.deps:
                        k = d.sem.num
                        if wm.get(k, 0) < d.val and need.get(k, (None, 0))[1] < d.val:
                            need[k] = (d.sem, d.val)
                    for k, (s, v) in need.items():
                        eng.wait_ge(s, v)
                        wm[k] = v
                    if o.fn is None:
                        continue
                    inst = o.fn(eng)
                    if o.sig:
                        inst.then_inc(o.sem, 16 if o.dma else 1)
                if e == "sp":
                    for s, v in finals:
                        eng.wait_ge(s, v)
            return body

        with nc.Block() as blk:
            blk.tensor(mk("pe"))
            blk.scalar(mk("act"))
            blk.vector(mk("dve"))
            blk.gpsimd(mk("pool"))
            blk.sync(mk("sp"))


class Arena:
    def __init__(self, nc, nbytes):
        self.h32 = nc.alloc_sbuf_tensor("arena", [128, nbytes // 4], F32)
        self.h16 = self.h32.bitcast(BF16)
        self.top = 0
        self.cap = nbytes

    def mark(self):
        return self.top

    def release(self, m):
        self.top = m

    def t(self, dtype, *shape):
        n = 1
        for s in shape:
            n *= s
        nb = n * (4 if dtype == F32 else 2)
        nb = (nb + 63) // 64 * 64
        off = self.top
        self.top += nb
        assert self.top <= self.cap, f"SBUF arena overflow {self.top} > {self.cap}"
        if dtype == F32:
            ap = self.h32[:, off // 4: off // 4 + n]
        else:
            ap = self.h16[:, off // 2: off // 2 + n]
        if len(shape) == 2:
            ap = ap.rearrange("p (a b) -> p a b", a=shape[0])
        elif len(shape) == 3:
            ap = ap.rearrange("p (a b c) -> p a b c", a=shape[0], b=shape[1])
        return ap


def own_tiles(hh, n512):
    return [g for g in range(n512) if ((g % 4) in (0, 3)) == (hh == 0)]


def I(name, *a, **k):
    return lambda e: getattr(e, name)(*a, **k)


class Bank:
    def __init__(self, P32, P16, k):
        self.buf = Buf(f"bank{k}")
        self.f = P32[:, k * 512:(k + 1) * 512]
        self.h = P16[:, k * 1024:(k + 1) * 1024]


def slot_tiles(n512):
    res = []
    for j in range(n512 // 2):
        p, e = j // 2, j % 2
        g0 = 4 * p + (0, 3)[e]
        g1 = 4 * p + (1, 2)[e]
        res.append((g0, g1, max(g0, g1) + 1))
    return res


def build(S, phases=("a1", "a2", "b"), debug=False, TA=256):
    assert S % 2048 == 0
    NT512 = S // 512
    NBLK = S // 128
    NSLOT = NT512 // 2
    nc = bass.Bass("TRN2", target_bir_lowering=False)
    P = Prog(nc)

    def din(name, shape, dtype=F32):
        return nc.dram_tensor(name, list(shape), dtype, kind="ExternalInput").ap()

    x_d = din("x", [S, D])
    mem_d = din("mem", [NMEM, D])
    w_in_d = din("w_in", [2, D, DIN])
    w_mkv_d = din("w_mem_kv", [2, D, D])
    w_out_d = din("w_out", [2, DMIX, D])
    pool_w_d = din("pool_w", [4, 256, 256])
    w_kv_d = din("w_kv", [D, 2064])
    lnp_d = din("lnp", [2, 2, D])
    pscale_d = din("pscale", [128, 8])
    bfor_d = din("bfor", [16])
    cb16_d = din("cb16", [128, 3 * 128 + 16 * 67], BF16)
    cf32_d = din("cf32", [128, 128 + 128 + 64 + 64], F32)
    sel_d = din("sel", [128, 2])
    masks_d = din("masks", [2, 128, 2 * 4 * 512], BF16)
    out_d = nc.dram_tensor("out", [NSLOT * 512, D], F32, kind="ExternalOutput").ap()

    kind_scr = "ExternalOutput" if debug else "Internal"
    X1F = nc.dram_tensor("x1f_scr", [S, D], F32, kind=kind_scr).ap()
    X1T = nc.dram_tensor("x1t_scr", [8, 128, S], BF16, kind=kind_scr).ap()
    KT = nc.dram_tensor("kt_scr", [8, 128, S], BF16, kind=kind_scr).ap()
    VA = nc.dram_tensor("va_scr", [NBLK, 128, NH, 128], BF16, kind=kind_scr).ap()
    if debug:
        CK_d = nc.dram_tensor("ck_dbg", [128, NBLK * 16], F32, kind="ExternalOutput").ap()

    A = Arena(nc, 212736)
    P32 = nc.alloc_psum_tensor("ps", [128, 4096], F32)
    P16 = P32.bitcast(BF16)
    banks = [Bank(P32, P16, k) for k in range(8)]
    bank_rr = [0]

    def nb():
        b = banks[bank_rr[0] % 8]
        bank_rr[0] += 1
        return b

    cb16 = A.t(BF16, 3 * 128 + 16 * 67)
    ident = cb16[:, 0:128]
    ones16 = cb16[:, 128:256]
    Emat = cb16[:, 384:384 + 16 * 67].rearrange("p (h m) -> p h m", h=16)
    cf32 = A.t(F32, 384)
    tri = cf32[:, 0:128]
    allones = cf32[:, 128:256]
    invc = cf32[:, 256:320].rearrange("p (w t) -> p w t", w=4)
    ones64 = cf32[:, 320:384]
    pscale = A.t(F32, 8)
    bfor = A.t(F32, 16)
    selt = A.t(F32, 2)
    CK = A.t(F32, NBLK, 16)
    CS = A.t(BF16, NBLK, 48)
    Tn = A.t(F32, 16)
    lng = A.t(F32, D)
    lnb = A.t(F32, D)
    mkT = A.t(BF16, 4, NMEM)
    mv = A.t(BF16, 2, 512)
    small = A.t(F32, 64)

    B_const = Buf("const")
    B_ln = Buf("lnp")
    B_mkv = Buf("mkv")
    B_CK = [Buf(f"ck{n}") for n in range(NBLK)]
    B_CS = [Buf(f"cs{n}") for n in range(NBLK)]
    B_Tn = Buf("Tn")
    B_X1F = [Buf(f"x1f{g}") for g in range(NT512)]
    B_X1T = [Buf(f"x1t{g}") for g in range(NT512)]
    B_KV = [Buf(f"kv{g}") for g in range(NT512)]

    def sp_load(out, in_, w, key, r=()):
        return P.op("sp", I("dma_start", out=out, in_=in_), r=r, w=w, dma=True, key=key)

    def pq_load(out, in_, w, key, r=(), wj=()):
        return P.op("pool", I("dma_start", out=out, in_=in_), r=r, w=w, wj=wj, dma=True, key=key)

    def cast_load(out, in_, w, key, r=(), wj=()):
        return P.op("pool", I("dma_start", out=out, in_=in_), r=r, w=w, wj=wj, dma=True, key=key)

    sp_load(cb16, cb16_d, [B_const], "c0")
    P.op("sp", I("dma_start", out=cf32, in_=cf32_d), wj=[B_const], dma=True, key="c0")
    P.op("sp", I("dma_start", out=pscale, in_=pscale_d), wj=[B_const], dma=True, key="c0")
    P.op("sp", I("dma_start", out=bfor, in_=bfor_d.partition_broadcast(128)), wj=[B_const], dma=True, key="c0")
    P.op("sp", I("dma_start", out=selt, in_=sel_d), wj=[B_const], dma=True, key="c0")

    def load_ln(layer):
        sp_load(lng, lnp_d[layer, 0].partition_broadcast(128), [B_ln], "ln")
        P.op("sp", I("dma_start", out=lnb, in_=lnp_d[layer, 1].partition_broadcast(128)),
             wj=[B_ln], dma=True, key="ln")

    def load_w(dst, src_rows_by_cols, nk, ncols, buf, key, first=True):
        srcv = src_rows_by_cols.rearrange("(k p) n -> p k n", p=128)
        for k in range(nk):
            for c0 in range(0, ncols, 1024):
                step = min(1024, ncols - c0)
                o = dst[:, k, c0:c0 + step]
                i = srcv[:, k, c0:c0 + step]
                if first:
                    cast_load(o, i, [buf], key)
                    first = False
                else:
                    cast_load(o, i, [], key, wj=[buf])

    def transposes_to(dstT, src_tok, nblk, r_src, w_dst, alt, all_act=False):
        for c in range(8):
            bk = nb()
            for b in range(nblk):
                P.op("pe", I("transpose",
                    out=bk.h[:, b * 128:(b + 1) * 128], in_=src_tok[:, b, c * 128:(c + 1) * 128],
                    identity=ident), r=[r_src[b] if isinstance(r_src, list) else r_src, B_const], w=[bk.buf])
            n = nblk * 128
            if (c + alt) % 2 == 0 and not all_act:
                P.op("dve", I("tensor_copy", out=dstT[:, c, 0:n], in_=bk.h[:, 0:n]),
                     r=[bk.buf], w=[] if c else [w_dst], wj=[w_dst] if c else [])
            else:
                P.op("act", I("activation", out=dstT[:, c, 0:n], in_=bk.h[:, 0:n], func=AF.Copy),
                     r=[bk.buf], w=[] if c else [w_dst], wj=[w_dst] if c else [])

    def proj(bk, W, wbuf, col0, m, xT, xbuf, n, extra_first=None):
        first = True
        if extra_first is not None:
            extra_first(bk)
            first = False
        for k in range(8):
            P.op("pe", I("matmul",
                bk.f[0:m, 0:n], W[:, k, col0:col0 + m], xT[:, k, 0:n], start=first, stop=(k == 7)),
                r=[wbuf, xbuf], w=[bk.buf])
            first = False

    def mem_kv(layer, wtmp, B_wtmp, memb, B_memb, memT, B_memT):
        load_w(wtmp, w_mkv_d[layer], 8, D, B_wtmp, "wtmp")
        cast_load(memb, mem_d.rearrange("(b p) d -> p b d", p=128), [B_memb], "memb")
        transposes_to(memT, memb, 2, B_memb, B_memT, 0)
        for h in range(4):
            bk = nb()
            proj(bk, wtmp, B_wtmp, h * 128, 128, memT, B_memT, NMEM)
            P.op("act", I("activation", out=mkT[:, h, :], in_=bk.f[:, 0:NMEM], func=AF.Copy),
                 r=[bk.buf], w=[B_mkv] if h == 0 else [], wj=[] if h == 0 else [B_mkv])
        for j in range(2):
            bk = nb()
            for k in range(8):
                P.op("pe", I("matmul",
                    bk.f[:, :], memT[:, k, j * 128:(j + 1) * 128], wtmp[:, k, 512:1024],
                    start=(k == 0), stop=(k == 7)), r=[B_wtmp, B_memT], w=[bk.buf])
            P.op("dve", I("tensor_copy", out=mv[:, j, :], in_=bk.f[:, :]),
                 r=[bk.buf], wj=[B_mkv])

    def mem_attn(h, qm, B_qm, n, PT, B_PT, ybk, dbk):
        for j in range(2):
            lb = nb()
            P.op("pe", I("matmul",
                lb.f[:, 0:n], mkT[:, h, j * 128:(j + 1) * 128], qm[:, h, 0:n], start=True, stop=True),
                r=[B_mkv, B_qm], w=[lb.buf])
            P.op("act", I("activation", out=PT[j][:, 0:n], in_=lb.f[:, 0:n], func=AF.Exp),
                 r=[lb.buf], w=[B_PT[j]])
        for j in range(2):
            P.op("pe", I("matmul",
                ybk.f[:, 0:n], mv[:, j, h * 128:(h + 1) * 128], PT[j][:, 0:n], start=(j == 0), stop=(j == 1)),
                r=[B_mkv, B_PT[j]], w=[ybk.buf])
        for j in range(2):
            P.op("pe", I("matmul",
                dbk.f[:, 0:n], ones16, PT[j][:, 0:n], start=(j == 0), stop=(j == 1)),
                r=[B_const, B_PT[j]], w=[dbk.buf])

    def ln_block(ybk0, ybk1, xres, B_xres, z, B_z, zn, B_zn, junk, B_junk, slot, x1b, B_x1b,
                 dst_dram, B_dst, key, B_dst_join=False, res_scale=ALPHA, st_eng="sp", ln_act=False, part="all"):
        sm = small[:, slot * 8:(slot + 1) * 8]
        B_sm = B_small[slot]
        if not isinstance(xres, list):
            xres_l = [(xres, B_xres, float(res_scale))]
        else:
            xres_l = xres
        for hf, yb in enumerate((ybk0, ybk1) if part in ("all", "z") else ()):
            for ci, (xr, Bxr, sc) in enumerate(xres_l):
                last = ci == len(xres_l) - 1
                firstw = (hf == 0 and ci == 0)
                kw = dict(accum_out=sm[:, hf:hf + 1]) if last else {}
                in1 = yb.f[:, :] if ci == 0 else z[:, hf * 512:(hf + 1) * 512]
                rr = [Bxr, yb.buf] if ci == 0 else [Bxr, B_z]
                if not isinstance(sc, float):
                    rr = rr + [B_const]
                P.op("dve", I("scalar_tensor_tensor",
                    out=z[:, hf * 512:(hf + 1) * 512], in0=xr[:, hf * 512:(hf + 1) * 512], scalar=sc,
                    in1=in1, op0=ALU.mult, op1=ALU.add, **kw),
                    r=rr, w=[B_z, B_sm] if firstw else [], wj=[] if firstw else [B_z, B_sm])
        if part == "z":
            return
        if ln_act:
            P.op("act", I("activation", out=zn, in_=z, func=AF.Square, accum_out=sm[:, 2:3]), r=[B_z], w=[B_zn], wj=[B_sm])
        else:
            P.op("dve", I("scalar_tensor_tensor", out=zn, in0=z, scalar=1.0, in1=z, op0=ALU.mult, op1=ALU.mult,
                          accum_out=sm[:, 2:3]), r=[B_z], w=[B_zn], wj=[B_sm])
        P.op("dve", I("tensor_scalar", out=sm[:, 3:4], in0=sm[:, 0:1], scalar1=sm[:, 1:2], scalar2=1.0 / D,
                                              op0=ALU.add, op1=ALU.mult), r=[B_sm], wj=[B_sm])
        P.op("dve", I("tensor_tensor", out=sm[:, 4:5], in0=sm[:, 3:4], in1=sm[:, 3:4], op=ALU.mult),
             r=[B_sm], wj=[B_sm])
        P.op("dve", I("scalar_tensor_tensor", out=sm[:, 5:6], in0=sm[:, 2:3], scalar=1.0 / D, in1=sm[:, 4:5],
                                                     op0=ALU.mult, op1=ALU.subtract), r=[B_sm], wj=[B_sm])
        P.op("dve", I("tensor_scalar", out=sm[:, 5:6], in0=sm[:, 5:6], scalar1=EPS, scalar2=None, op0=ALU.add),
             r=[B_sm], wj=[B_sm])
        P.op("pool", I("tensor_tensor", out=sm[:, 6:7], in0=sm[:, 5:6], in1=epsc[:, 1:2], op=ALU.pow),
             r=[B_sm, B_const], wj=[B_sm])
        P.op("dve", I("scalar_tensor_tensor", out=sm[:, 7:8], in0=sm[:, 3:4], scalar=-1.0, in1=sm[:, 6:7],
                                                     op0=ALU.mult, op1=ALU.mult), r=[B_sm], wj=[B_sm])
        P.op("dve", I("tensor_scalar", out=zn, in0=z, scalar1=sm[:, 6:7], scalar2=sm[:, 7:8], op0=ALU.mult, op1=ALU.add),
             r=[B_z, B_sm], w=[B_zn])
        P.op("dve", I("tensor_tensor", out=z, in0=zn, in1=lng, op=ALU.mult), r=[B_zn, B_ln], w=[B_z])
        P.op("pool" if ln_act else "dve", I("tensor_tensor", out=zn, in0=z, in1=lnb, op=ALU.add), r=[B_z, B_ln], w=[B_zn])
        if x1b is not None:
            P.op("pool", I("tensor_copy", out=x1b, in_=zn), r=[B_zn], w=[B_x1b])
        if dst_dram is not None:
            if B_dst_join:
                P.op(st_eng, I("dma_start", out=dst_dram, in_=zn), r=[B_zn], wj=[B_dst], dma=True, key=key)
            else:
                P.op(st_eng, I("dma_start", out=dst_dram, in_=zn), r=[B_zn], w=[B_dst], dma=True, key=key)

    epsc = A.t(F32, 2)
    P.op("dve", I("memset", epsc[:, 0:1], EPS), wj=[B_const])
    P.op("dve", I("memset", epsc[:, 1:2], -0.5), wj=[B_const])
    B_small = [Buf(f"small{i}") for i in range(8)]
    P.op("dve", I("memset", Tn, 0.0), w=[B_Tn])

    base_mark = A.mark()

    if "a1" in phases:
        NB_A = TA // 128
        NTA = S // TA
        win = A.t(BF16, 8, DIN)
        wout = A.t(BF16, 12, D)
        poolw = A.t(BF16, 8, 256)
        B_win, B_wout, B_poolw = Buf("win"), Buf("wout"), Buf("poolw")
        m_tmp = A.mark()
        wtmp = A.t(BF16, 8, D)
        memb = A.t(BF16, 2, D)
        memT = A.t(BF16, 8, NMEM)
        B_wtmp, B_memb, B_memT = Buf("wtmp"), Buf("memb"), Buf("memT")
        load_ln(0)
        mem_kv(0, wtmp, B_wtmp, memb, B_memb, memT, B_memT)
        load_w(win, w_in_d[0], 8, DIN, B_win, "win")
        load_w(poolw, pool_w_d.rearrange("g r c -> (g r) c"), 8, 256, B_poolw, "poolw")
        load_w(wout, w_out_d[0], 12, D, B_wout, "wout")
        P.barrier([B_mkv])
        A.release(m_tmp)

        xb = [A.t(BF16, NB_A, D)] * 2
        B_xb = [Buf("xb0")] * 2
        xc = A.t(F32, NB_A, D)
        B_xc = Buf("xc")
        xf = [A.t(F32, D) for _ in range(2)]
        B_xf = [Buf("xf0"), Buf("xf1")]
        xT = [A.t(BF16, 8, TA) for _ in range(2)]
        B_xT = [Buf("xT0"), Buf("xT1")]
        HW_ = TA + 16
        U = [A.t(F32, 2, HW_) for _ in range(4)]
        B_U = [Buf(f"U{g}") for g in range(4)]
        T1 = A.t(F32, 2, HW_)
        T2 = A.t(F32, 2, HW_)
        B_T1, B_T2 = Buf("T1"), Buf("T2")
        Hh = A.t(F32, 8, 16)
        B_H = [Buf(f"H{c}") for c in range(8)]
        pm = [A.t(BF16, 2, TA) for _ in range(4)]
        B_pm = [Buf(f"pm{g}") for g in range(4)]
        gsb = A.t(F32, 12, TA)
        B_gsb = [Buf(f"gsb{i}") for i in range(12)]
        qm = A.t(BF16, 4, TA)
        B_qm = Buf("qm")
        PT8 = [[A.t(BF16, TA) for _ in range(2)] for _ in range(4)]
        B_PT8 = [[Buf(f"PT{h}{j}") for j in range(2)] for h in range(4)]
        rd = A.t(F32, TA)
        t1 = A.t(F32, TA)
        B_rd, B_t1 = Buf("rd"), Buf("t1")
        tfix = A.t(F32, 16)
        B_tfix = Buf("tfix")
        gated = A.t(BF16, 12, TA)
        B_gated = Buf("gated")
        zz = [A.t(F32, D) for _ in range(2)]
        zn = [A.t(F32, D) for _ in range(2)]
        B_zz = [Buf("z0"), Buf("z1")]
        B_zn = [Buf("zn0"), Buf("zn1")]
        junk, B_junk = None, None
        x1b = A.t(BF16, NB_A, D)
        B_x1b = [Buf(f"x1b{b}") for b in range(NB_A)]
        x1T = A.t(BF16, 8, TA)
        B_x1T = Buf("x1T")
        P.op("pool", I("memset", Hh, 0.0), w=B_H)

        def load_xc(i):
            if i < NTA:
                sp_load(xc, x_d[i * TA:(i + 1) * TA, :].rearrange("(b p) d -> p b d", p=128), [B_xc], "xc")

        def st_T(i):
            s = i % 2
            for b in range(NB_A):
                P.op("act", I("activation", out=xb[s][:, b, :], in_=xc[:, b, :], func=AF.Copy), r=[B_xc],
                     w=[B_xb[s]] if b == 0 else [], wj=[] if b == 0 else [B_xb[s]])
            transposes_to(xT[s], xb[s], NB_A, B_xb[s], B_xT[s], i, all_act=True)
            load_xc(i + 1)

        def st_U(i):
            s = i % 2
            for g in range(4):
                for ch in range(2):
                    c = 2 * g + ch
                    bk = nb()
                    proj(bk, win, B_win, c * 128, 128, xT[s], B_xT[s], TA)
                    P.op("pool", I("tensor_copy", out=U[g][:, ch, 0:16], in_=Hh[:, c, :]),
                         r=[B_H[c]], w=[B_U[g]] if ch == 0 else [], wj=[] if ch == 0 else [B_U[g]])
                    P.op("act", I("activation", out=U[g][:, ch, 16:16 + TA], in_=bk.f[:, 0:TA], func=AF.Copy),
                         r=[bk.buf], wj=[B_U[g]])
                    P.op("act", I("activation", out=Hh[:, c, :], in_=bk.f[:, TA - 16:TA], func=AF.Copy),
                         r=[bk.buf], w=[B_H[c]])

        def st_Upool(i):
            for g in range(4):
                src_, Bs = U[g], B_U[g]
                dsts = [(T1, B_T1), (T2, B_T2)]
                sh = 1
                for lvl in range(g + 1):
                    dstt, Bd = dsts[lvl % 2]
                    lo = 2 * sh - 1
                    P.op("dve", I("tensor_tensor", out=dstt[:, :, lo:HW_], in0=src_[:, :, lo:HW_],
                                  in1=src_[:, :, lo - sh:HW_ - sh], op=ALU.add), r=[Bs], w=[Bd])
                    src_, Bs = dstt, Bd
                    sh *= 2
                wv = POOLW[g]
                P.op("dve", I("scalar_tensor_tensor", out=pm[g][:, :, :], in0=src_[:, :, 16:HW_], scalar=1.0 / wv,
                              in1=U[g][:, :, 16:HW_], op0=ALU.mult, op1=ALU.subtract), r=[Bs, B_U[g]], w=[B_pm[g]])
                if i == 0:
                    for ch in range(2):
                        P.op("dve", I("tensor_tensor", out=tfix, in0=src_[:, ch, 16:32], in1=invc[:, g, :], op=ALU.mult),
                             r=[Bs, B_const], w=[B_tfix])
                        P.op("dve", I("tensor_tensor", out=pm[g][:, ch, 0:16], in0=tfix, in1=U[g][:, ch, 16:32],
                                      op=ALU.subtract), r=[B_tfix, B_U[g]], wj=[B_pm[g]])

        def st_S2(i):
            s = i % 2
            xTs, BxT = xT[s], B_xT[s]
            for h in range(4):
                bk = nb()
                proj(bk, win, B_win, D + h * 128, 128, xTs, BxT, TA)
                P.op("act", I("activation", out=qm[:, h, :], in_=bk.f[:, 0:TA], func=AF.Copy, scale=128.0 ** -0.5),
                     r=[bk.buf], w=[B_qm] if h == 0 else [], wj=[] if h == 0 else [B_qm])

            def gate(c):
                gb = nb()
                proj(gb, win, B_win, DMIX + c * 128, 128, xTs, BxT, TA)
                P.op("act", I("activation", out=gsb[:, c, :], in_=gb.f[:, 0:TA], func=AF.Silu),
                     r=[gb.buf], w=[B_gsb[c]])
            for c in range(4):
                gate(c)
            for h in range(4):
                for j in range(2):
                    lb = nb()
                    P.op("pe", I("matmul", lb.f[:, 0:TA], mkT[:, h, j * 128:(j + 1) * 128], qm[:, h, :], start=True, stop=True),
                         r=[B_mkv, B_qm], w=[lb.buf])
                    P.op("act", I("activation", out=PT8[h][j], in_=lb.f[:, 0:TA], func=AF.Exp),
                         r=[lb.buf], w=[B_PT8[h][j]])
            for c in range(4, 12):
                gate(c)
            ybks, dbks = [], []
            for h in range(4):
                ybk = nb()
                if h % 2 == 0:
                    dbk2 = nb()
                ybks.append(ybk)
                dbks.append((dbk2, (h % 2) * TA))
                for j in range(2):
                    P.op("pe", I("matmul", ybk.f[:, 0:TA], mv[:, j, h * 128:(h + 1) * 128], PT8[h][j], start=(j == 0),
                                 stop=(j == 1)), r=[B_mkv, B_PT8[h][j]], w=[ybk.buf])
                for j in range(2):
                    P.op("pe", I("matmul", dbk2.f[:, (h % 2) * TA:(h % 2) * TA + TA], ones16, PT8[h][j], start=(j == 0),
                                 stop=(j == 1)), r=[B_const, B_PT8[h][j]], w=[dbk2.buf])
            for h in range(4):
                ybk = ybks[h]
                dbk, doff = dbks[h]
                P.op("dve", I("reciprocal", out=rd, in_=dbk.f[:, doff:doff + TA]), r=[dbk.buf], w=[B_rd])
                P.op("dve", I("tensor_tensor", out=t1, in0=ybk.f[:, 0:TA], in1=rd, op=ALU.mult),
                     r=[ybk.buf, B_rd], w=[B_t1])
                P.op("dve", I("tensor_tensor", out=gated[:, 8 + h, :], in0=t1, in1=gsb[:, 8 + h, :], op=ALU.mult),
                     r=[B_t1, B_gsb[8 + h]], w=[B_gated] if h == 0 else [], wj=[] if h == 0 else [B_gated])

        def st_S3(i):
            for g in range(4):
                for oc in range(2):
                    c = 2 * g + oc
                    mb = nb()
                    for kc in range(2):
                        P.op("pe", I("matmul", mb.f[:, 0:TA], poolw[:, g * 2 + kc, oc * 128:(oc + 1) * 128], pm[g][:, kc, :],
                                     start=(kc == 0), stop=(kc == 1)), r=[B_poolw, B_pm[g]], w=[mb.buf])
                    P.op("dve", I("scalar_tensor_tensor", out=gated[:, c, :], in0=mb.f[:, 0:TA], scalar=pscale[:, c:c + 1],
                                  in1=gsb[:, c, :], op0=ALU.mult, op1=ALU.mult),
                         r=[mb.buf, B_gsb[c], B_const], wj=[B_gated])

        def st_S4(i):
            t0 = i * TA
            g512 = t0 // 512
            for b in range(NB_A):
                q = (i * NB_A + b) % 2
                sp_load(xf[q], x_d[t0 + b * 128:t0 + (b + 1) * 128, :], [B_xf[q]], f"xf{q}")
            for b in range(NB_A):
                q = (i * NB_A + b) % 2
                yb = [nb(), nb()]
                for hf in range(2):
                    for k in range(12):
                        P.op("pe", I("matmul", yb[hf].f[:, :], gated[:, k, b * 128:(b + 1) * 128],
                                     wout[:, k, hf * 512:(hf + 1) * 512], start=(k == 0), stop=(k == 11)),
                             r=[B_gated, B_wout], w=[yb[hf].buf])
                tok0 = t0 + b * 128
                ln_block(yb[0], yb[1], xf[q], B_xf[q], zz[q], B_zz[q], zn[q], B_zn[q], junk, B_junk, q,
                         x1b[:, b, :], B_x1b[b], X1F[tok0:tok0 + 128, :], B_X1F[g512], f"x1f{q}",
                         B_dst_join=(tok0 % 512 != 0), ln_act=True, part="z")
            for b in range(NB_A):
                q = (i * NB_A + b) % 2
                tok0 = t0 + b * 128
                ln_block(None, None, xf[q], B_xf[q], zz[q], B_zz[q], zn[q], B_zn[q], junk, B_junk, q,
                         x1b[:, b, :], B_x1b[b], X1F[tok0:tok0 + 128, :], B_X1F[g512], f"x1f{q}",
                         B_dst_join=(tok0 % 512 != 0), ln_act=True, part="rest")

        def st_X1(i):
            t0 = i * TA
            g512 = t0 // 512
            transposes_to(x1T, x1b, NB_A, B_x1b, B_x1T, i, all_act=True)
            if t0 % 512 == 0:
                P.op("sp", I("dma_start", out=X1T[:, :, t0:t0 + TA].rearrange("c p t -> p c t"), in_=x1T),
                     r=[B_x1T], w=[B_X1T[g512]], dma=True, key="x1t")
            else:
                P.op("sp", I("dma_start", out=X1T[:, :, t0:t0 + TA].rearrange("c p t -> p c t"), in_=x1T),
                     r=[B_x1T], wj=[B_X1T[g512]], dma=True, key="x1t")

        load_xc(0)
        st_T(0)
        st_U(0)
        st_Upool(0)
        for i in range(NTA):
            if i + 1 < NTA:
                st_T(i + 1)
            st_S2(i)
            st_S3(i)
            if i + 1 < NTA:
                st_U(i + 1)
            if i >= 1:
                st_X1(i - 1)
            st_S4(i)
            if i + 1 < NTA:
                st_Upool(i + 1)
        st_X1(NTA - 1)
        P.barrier(B_X1F + B_X1T)
        A.release(base_mark)

    pre_b = None
    if "b" in phases:
        winB = A.t(BF16, 8, DIN)
        woutB = A.t(BF16, 12, D)
        B_winB, B_woutB = Buf("win1"), Buf("wout1")
        a2_base = A.mark()
        m_tmpB = A.mark()
        wtmpB = A.t(BF16, 8, D)
        membB = A.t(BF16, 2, D)
        memTB = A.t(BF16, 8, NMEM)
        B_wtmpB, B_membB, B_memTB = Buf("wtmp1"), Buf("memb1"), Buf("memT1")
        pre_b = True
    else:
        a2_base = base_mark
    if "a2" in phases:
        wkv = A.t(BF16, 8, 2064)
        B_wkv = Buf("wkv")
        load_w(wkv, w_kv_d, 8, 2064, B_wkv, "wkv")
    if pre_b:
        load_ln(1)
        mem_kv(1, wtmpB, B_wtmpB, membB, B_membB, memTB, B_memTB)
        load_w(winB, w_in_d[1], 8, DIN, B_winB, "win")
        load_w(woutB, w_out_d[1], 12, D, B_woutB, "wout")
    if "a2" in phases:
        xt2 = [A.t(BF16, 8, 512) for _ in range(2)]
        B_xt2 = [Buf("xt2a"), Buf("xt2b")]
        ksb = A.t(BF16, 8, 512)
        B_ksb = Buf("ksb")
        vsb = A.t(BF16, 4, NH, 128)
        B_vsb = Buf("vsb")
        bfor4 = A.t(F32, 4, 16)
        fl = A.t(F32, 4, 16)
        e1 = A.t(F32, 4, 16)
        lp = A.t(F32, 4, 16)
        r1 = A.t(F32, 16)
        r2 = A.t(F32, 16)
        B_fl, B_e1, B_lp, B_r1, B_r2 = Buf("fl"), Buf("e1"), Buf("lp"), Buf("r1"), Buf("r2")
        B_bf4 = Buf("bf4")
        for b in range(4):
            P.op("pool", I("tensor_copy", out=bfor4[:, b, :], in_=bfor), r=[B_const],
                 w=[B_bf4] if b == 0 else [], wj=[] if b == 0 else [B_bf4])
        P.op("pool", I("memset", vsb[:, :, :, 64:128], 1.0), w=[B_vsb])
        sp_load(xt2[0], X1T[:, :, 0:512].rearrange("c p t -> p c t"), [B_xt2[0]], "xt2a", r=[B_X1T[0]])
        for g in range(NT512):
            s = g % 2
            if g + 1 < NT512:
                sp_load(xt2[1 - s], X1T[:, :, (g + 1) * 512:(g + 2) * 512].rearrange("c p t -> p c t"),
                        [B_xt2[1 - s]], "xt2b" if 1 - s else "xt2a", r=[B_X1T[g + 1]])
            xt = xt2[s]
            Bxt = B_xt2[s]
            fb = nb()
            for b in range(4):
                for k in range(8):
                    P.op("pe", I("matmul",
                        fb.f[:, b * 16:(b + 1) * 16], xt[:, k, b * 128:(b + 1) * 128], wkv[:, k, 2048:2064],
                        start=(k == 0), stop=(k == 7)), r=[Bxt, B_wkv], w=[fb.buf])
            P.op("dve", I("tensor_tensor", out=fl, in0=fb.f[:, 0:64].rearrange("p (b h) -> p b h", b=4),
                                                         in1=bfor4, op=ALU.add), r=[fb.buf, B_bf4], w=[B_fl])
            P.op("act", I("activation", out=e1, in_=fl, func=AF.Exp, scale=-1.0), r=[B_fl], w=[B_e1])
            P.op("act", I("activation", out=lp, in_=e1, func=AF.Ln, bias=1.0), r=[B_e1], w=[B_lp])
            for b in range(4):
                n = g * 4 + b
                cb = nb()
                P.op("pe", I("matmul", cb.f[:, 0:16], tri, lp[:, b, :], start=True, stop=True),
                     r=[B_const, B_lp], w=[cb.buf])
                P.op("pe", I("matmul", cb.f[:, 16:32], allones, lp[:, b, :], start=True, stop=True),
                     r=[B_const, B_lp], w=[cb.buf])
                P.op("dve", I("tensor_tensor", out=CK[:, n, :], in0=cb.f[:, 0:16], in1=Tn, op=ALU.add),
                     r=[cb.buf, B_Tn], w=[B_CK[n]])
                P.op("dve", I("tensor_tensor", out=Tn, in0=cb.f[:, 16:32], in1=Tn, op=ALU.add),
                     r=[cb.buf], w=[B_Tn])
                CSv = CS[:, n, :].rearrange("p (h i) -> p h i", i=3)
                P.op("pool", I("tensor_scalar", out=CSv[:, :, 0], in0=CK[:, n, :], scalar1=-8.0,
                                                                      scalar2=None, op0=ALU.mult),
                     r=[B_CK[n]], w=[B_CS[n]])
                P.op("dve", I("scalar_tensor_tensor", out=r1, in0=CK[:, n, :], scalar=-8.0,
                                                                           in1=CSv[:, :, 0], op0=ALU.mult,
                                                                           op1=ALU.subtract),
                     r=[B_CK[n], B_CS[n]], w=[B_r1])
                P.op("pool", I("tensor_copy", out=CSv[:, :, 1], in_=r1), r=[B_r1], wj=[B_CS[n]])
                P.op("pool", I("tensor_tensor", out=r2, in0=r1, in1=CSv[:, :, 1], op=ALU.subtract),
                     r=[B_r1, B_CS[n]], w=[B_r2])
                P.op("pool", I("tensor_copy", out=CSv[:, :, 2], in_=r2), r=[B_r2], wj=[B_CS[n]])
            for c in range(8):
                bk = nb()
                proj(bk, wkv, B_wkv, c * 128, 128, xt, Bxt, 512)
                if c % 2 == 0:
                    P.op("act", I("activation", out=ksb[:, c, :], in_=bk.f[:, :], func=AF.Copy),
                         r=[bk.buf], w=[B_ksb] if c == 0 else [], wj=[] if c == 0 else [B_ksb])
                else:
                    P.op("dve", I("tensor_copy", out=ksb[:, c, :], in_=bk.f[:, :]),
                         r=[bk.buf], wj=[B_ksb])
            P.op("sp", I("dma_start", out=KT[:, :, g * 512:(g + 1) * 512].rearrange("c p t -> p c t"),
                                                  in_=ksb), r=[B_ksb], w=[B_KV[g]], dma=True, key="kst")
            for b in range(4):
                for hf in range(2):
                    bk = nb()
                    for k in range(8):
                        P.op("pe", I("matmul",
                            bk.f[:, :], xt[:, k, b * 128:(b + 1) * 128], wkv[:, k, 1024 + hf * 512:1024 + (hf + 1) * 512],
                            start=(k == 0), stop=(k == 7)), r=[Bxt, B_wkv], w=[bk.buf])
                    src_v = bk.f[:, :].rearrange("p (h d) -> p h d", h=8)
                    first = (b == 0 and hf == 0)
                    if (b + hf) % 2 == 0:
                        P.op("dve", I("tensor_copy",
                            out=vsb[:, b, hf * 8:(hf + 1) * 8, 0:64], in_=src_v),
                            r=[bk.buf], w=[B_vsb] if first else [], wj=[] if first else [B_vsb])
                    else:
                        P.op("act", I("activation",
                            out=vsb[:, b, hf * 8:(hf + 1) * 8, 0:64], in_=src_v, func=AF.Copy),
                            r=[bk.buf], wj=[B_vsb])
            P.op("sp", I("dma_start", out=VA[g * 4:(g + 1) * 4].rearrange("b p h e -> p b h e"), in_=vsb),
                 r=[B_vsb], wj=[B_KV[g]], dma=True, key="vst")
        if debug:
            P.op("sp", I("dma_start", out=CK_d, in_=CK.rearrange("p n h -> p (n h)")), r=B_CK, dma=True, key="dbg")
        P.barrier(B_KV + B_CK + B_CS + [B_mkv])
        A.release(a2_base)

    if "b" in phases:
        win, wout, B_win, B_wout = winB, woutB, B_winB, B_woutB
        masks = A.t(BF16, 2, 4, 512)
        B_masks = Buf("masks")
        am = A.t(F32, 2)
        P.op("dve", I("tensor_scalar", out=am, in0=selt, scalar1=float(ALPHA), scalar2=None, op0=ALU.mult),
             r=[B_const], wj=[B_const])
        if "a2" not in phases:
            P.barrier([B_mkv])

        xst = [A.t(BF16, 2, 2, 512)] * 2
        B_xst = [Buf("xst0")] * 2
        xsel = A.t(BF16, 8, 512)
        B_xsel = Buf("xsel")
        csel = A.t(BF16, 4, 48)
        B_csel = Buf("csel")
        csT = A.t(BF16, 512)
        B_csT = Buf("csT")
        gmain = [A.t(F32, 2, 512) for _ in range(2)]
        B_gmain = [Buf("gm0"), Buf("gm1")]
        Qa = [A.t(BF16, 4, 512) for _ in range(2)]
        B_Qa = [Buf("Qa0"), Buf("Qa1")]
        kbuf = [A.t(BF16, 4, 512) for _ in range(2)]
        B_kbuf = [Buf("kb0"), Buf("kb1")]
        vbuf = [A.t(BF16, 4, 4, 128) for _ in range(2)]
        B_vbuf = [Buf("vb0"), Buf("vb1")]
        PTb = [A.t(BF16, 512) for _ in range(3)]
        B_PTb = [Buf(f"ptb{i}") for i in range(3)]
        gsb = [A.t(F32, 512) for _ in range(2)]
        B_gsb = [Buf("gsbB0"), Buf("gsbB1")]
        qm = A.t(BF16, 4, 512)
        B_qm = Buf("qmB")
        PT = PTb[0:2]
        B_PT = B_PTb[0:2]
        rd = A.t(F32, 512)
        t1 = A.t(F32, 512)
        B_rd, B_t1 = Buf("rdB"), Buf("t1B")
        rden = [A.t(F32, 512) for _ in range(2)]
        B_rden = [Buf("rden0"), Buf("rden1")]
        gated = A.t(BF16, 12, 512)
        B_gated = Buf("gatedB")
        xfA = [A.t(F32, D)] * 2
        xfB = [A.t(F32, D)] * 2
        B_xfA = [Buf("xfA0")] * 2
        B_xfB = [Buf("xfB0")] * 2
        zz = [A.t(F32, D) for _ in range(2)]
        zn = [A.t(F32, D) for _ in range(2)]
        B_zz = [Buf("zB0"), Buf("zB1")]
        B_zn = [Buf("znB0"), Buf("znB1")]
        junk, B_junk = None, None
        tmpn = [rd, t1, gsb[0], gsb[1]]
        B_tmpn = [B_rd, B_t1, B_gsb[0], B_gsb[1]]
        B_out = Buf("out")
        for i in range(2):
            P.op("pool", I("memset", kbuf[i][64:67, :, :], 1.0), w=[B_kbuf[i]])

        Obk = banks[0:4]
        Sbk = banks[4:7]
        Mbk = banks[7]
        kv_rr = [0]
        s_rr = [0]
        pt_rr = [0]
        gs_rr = [0]
        xs_rr = [0]
        def do_select(g0, g1):
                for pc in range(4):
                    q = xs_rr[0] % 2
                    xs_rr[0] += 1
                    pq_load(xst[q][:, 0, :, :], X1T[2 * pc:2 * pc + 2, :, g0 * 512:(g0 + 1) * 512].rearrange("c p t -> p c t"),
                            [B_xst[q]], "xst")
                    P.op("pool", I("dma_start",
                        out=xst[q][:, 1, :, :], in_=X1T[2 * pc:2 * pc + 2, :, g1 * 512:(g1 + 1) * 512].rearrange("c p t -> p c t")),
                        wj=[B_xst[q]], dma=True, key="xst")
                    P.op("dve", I("tensor_scalar", out=xsel[:, 2 * pc:2 * pc + 2, :], in0=xst[q][:, 0, :, :],
                                                                     scalar1=selt[:, 0:1], scalar2=None, op0=ALU.mult),
                         r=[B_xst[q], B_const], w=[B_xsel] if pc == 0 else [], wj=[] if pc == 0 else [B_xsel])
                    P.op("dve", I("scalar_tensor_tensor",
                        out=xsel[:, 2 * pc:2 * pc + 2, :], in0=xst[q][:, 1, :, :], scalar=selt[:, 1:2],
                        in1=xsel[:, 2 * pc:2 * pc + 2, :], op0=ALU.mult, op1=ALU.add),
                        r=[B_xst[q], B_const], wj=[B_xsel])
                P.op("dve", I("tensor_scalar", out=csel, in0=CS[:, g0 * 4:(g0 + 1) * 4, :], scalar1=selt[:, 0:1],
                                                             scalar2=None, op0=ALU.mult),
                     r=B_CS[g0 * 4:(g0 + 1) * 4] + [B_const], w=[B_csel])
                P.op("dve", I("scalar_tensor_tensor", out=csel, in0=CS[:, g1 * 4:(g1 + 1) * 4, :],
                                                                    scalar=selt[:, 1:2], in1=csel, op0=ALU.mult, op1=ALU.add),
                     r=B_CS[g1 * 4:(g1 + 1) * 4] + [B_const], wj=[B_csel])

        slots = slot_tiles(NT512)
        def proj_items(G):
            gq = G % 2
            items = []
            for oc in range(2):
                def it(bkf, oc=oc):
                    gb = bkf()
                    proj(gb, win, B_win, DMIX + (2 * G + oc) * 128, 128, xsel, B_xsel, 512)
                    P.op("act", I("activation", out=gmain[gq][:, oc, :], in_=gb.f[:, :], func=AF.Silu),
                         r=[gb.buf], w=[B_gmain[gq]] if oc == 0 else [], wj=[] if oc == 0 else [B_gmain[gq]])
                items.append(it)
            for hl in range(4):
                def it(bkf, hl=hl):
                    h = 4 * G + hl
                    qb = bkf()

                    def efirst(bk):
                        P.op("pe", I("matmul", bk.f[0:67, :], Emat[0:48, h, :], csT[0:48, :], start=True, stop=False),
                             r=[B_const, B_csT], w=[bk.buf])
                    proj(qb, win, B_win, h * 64, 64, xsel, B_xsel, 512, extra_first=efirst)
                    if hl % 2 == 0:
                        P.op("act", I("activation", out=Qa[gq][0:67, hl, :], in_=qb.f[0:67, :], func=AF.Copy),
                             r=[qb.buf], w=[B_Qa[gq]] if hl == 0 else [], wj=[] if hl == 0 else [B_Qa[gq]])
                    else:
                        P.op("dve", I("tensor_copy", out=Qa[gq][0:67, hl, :], in_=qb.f[0:67, :]),
                             r=[qb.buf], wj=[B_Qa[gq]])
                items.append(it)
            return items

        def pre_A(j, g0, g1):
            sp_load(masks, masks_d[j % 2].rearrange("p (a b c) -> p a b c", a=2, b=4), [B_masks], "masks")
            if j == 0:
                do_select(g0, g1)
            cbk = nb()
            for b in range(4):
                P.op("pe", I("transpose", out=cbk.h[0:48, b * 128:(b + 1) * 128], in_=csel[:, b, :],
                                                               identity=ident), r=[B_csel, B_const], w=[cbk.buf])
            P.op("dve", I("tensor_copy", out=csT[0:48, :], in_=cbk.h[0:48, 0:512]), r=[cbk.buf], w=[B_csT])

            for h in range(4):
                bk = nb()
                proj(bk, win, B_win, D + h * 128, 128, xsel, B_xsel, 512)
                P.op("act", I("activation", out=qm[:, h, :], in_=bk.f[:, :], func=AF.Copy,
                                                               scale=128.0 ** -0.5),
                     r=[bk.buf], w=[B_qm] if h == 0 else [], wj=[] if h == 0 else [B_qm])
            for it in proj_items(0):
                it(nb)

        def pre_B(j):
            for h in range(4):
                ybk, dbk = nb(), nb()
                mem_attn(h, qm, B_qm, 512, PT, B_PT, ybk, dbk)
                gb = nb()
                proj(gb, win, B_win, DMIX + D + h * 128, 128, xsel, B_xsel, 512)
                gi = gs_rr[0] % 2
                gs_rr[0] += 1
                P.op("act", I("activation", out=gsb[gi], in_=gb.f[:, :], func=AF.Silu),
                     r=[gb.buf], w=[B_gsb[gi]])
                P.op("dve", I("reciprocal", out=rd, in_=dbk.f[:, :]), r=[dbk.buf], w=[B_rd])
                P.op("dve", I("tensor_tensor", out=t1, in0=ybk.f[:, :], in1=rd, op=ALU.mult),
                     r=[ybk.buf, B_rd], w=[B_t1])
                P.op("dve", I("tensor_tensor", out=gated[:, 8 + h, :], in0=t1, in1=gsb[gi], op=ALU.mult),
                     r=[B_t1, B_gsb[gi]], w=[B_gated] if h == 0 else [], wj=[] if h == 0 else [B_gated])

        def attention(j, nch):
            norm_pending = []
            for G in range(4):
                gq = G % 2
                nxt = proj_items(G + 1) if G < 3 else []
                if G == 3 and j + 1 < len(slots):
                    do_select(slots[j + 1][0], slots[j + 1][1])
                units = [(kc, kb, hl) for kc in range(nch) for kb in range(4) for hl in range(4)]
                pend = []
                kq = 0
                for idx, (kc, kb, hl) in enumerate(units):
                    if norm_pending and idx in (2, 4):
                        norm_pending.pop(0)()
                    h = 4 * G + hl
                    n = kc * 4 + kb
                    mtype = kc - (nch - 2)
                    if kb == 0 and hl == 0:
                        kq = kv_rr[0] % 2
                        kv_rr[0] += 1
                        sp_load(kbuf[kq][0:64, :, :],
                                KT.rearrange("c (two d) t -> (c two) d t", two=2)[4 * G:4 * G + 4, :, kc * 512:(kc + 1) * 512]
                                .rearrange("h d t -> d h t"), [B_kbuf[kq]], f"kb{kq}", r=[B_KV[kc]])
                        sp_load(vbuf[kq], VA[kc * 4:(kc + 1) * 4, :, 4 * G:4 * G + 4, :].rearrange("b p h e -> p b h e"),
                                [B_vbuf[kq]], f"vb{kq}", r=[B_KV[kc]])
                    sb = Sbk[s_rr[0] % 3]
                    s_rr[0] += 1
                    P.op("pe", I("matmul", sb.f[:, :], kbuf[kq][0:67, hl, kb * 128:(kb + 1) * 128], Qa[gq][0:67, hl, :],
                                 start=True, stop=(mtype < 0)), r=[B_kbuf[kq], B_Qa[gq]], w=[sb.buf])
                    if mtype >= 0:
                        P.op("pe", I("matmul", sb.f[:, :], ident, masks[:, mtype, kb, :], start=False, stop=True),
                             r=[B_const, B_masks], w=[sb.buf])
                    pi = pt_rr[0] % 3
                    pt_rr[0] += 1
                    P.op("act", I("activation", out=PTb[pi], in_=sb.f[:, :], func=AF.Exp, scale=0.125,
                                  bias=CK[:, n, h:h + 1]), r=[sb.buf, B_CK[n]], w=[B_PTb[pi]])

                    def pv(hl=hl, kq=kq, kb=kb, pi=pi, first=(idx < 4), last=(idx >= len(units) - 4)):
                        P.op("pe", I("matmul", Obk[hl].f[:, :], vbuf[kq][:, kb, hl, :], PTb[pi], start=first, stop=last),
                             r=[B_vbuf[kq], B_PTb[pi]], w=[Obk[hl].buf])
                    pend.append(pv)
                    if len(pend) > 2:
                        pend.pop(0)()
                    if nxt and idx >= 3 and idx % 2 == 1:
                        nxt.pop(0)(lambda: Mbk)
                for f in pend:
                    f()
                for it in nxt:
                    it(lambda: Mbk)
                def mk_norm(G, pair, gq):
                    def f():
                        hls = (2 * pair, 2 * pair + 1)
                        for hl in hls:
                            po = (hl % 2) * 64
                            P.op("dve", I("tensor_tensor", out=tmpn[hl][0:64, :], in0=Obk[hl].f[0:64, :],
                                          in1=gmain[gq][po:po + 64, hl // 2, :], op=ALU.mult),
                                 r=[Obk[hl].buf, B_gmain[gq]], w=[B_tmpn[hl]])
                        for hl in hls:
                            P.op("act", I("activation", out=rden[hl % 2][0:64, :], in_=Obk[hl].f[64:128, :], func=AF.Ln),
                                 r=[Obk[hl].buf], w=[B_rden[hl % 2]])
                        for hl in hls:
                            P.op("act", I("activation", out=rden[hl % 2][0:64, :], in_=rden[hl % 2][0:64, :], func=AF.Exp,
                                          scale=-1.0), r=[B_rden[hl % 2]], w=[B_rden[hl % 2]])
                        for hl in hls:
                            h = 4 * G + hl
                            po = (hl % 2) * 64
                            P.op("dve", I("tensor_tensor", out=gated[po:po + 64, h // 2, :], in0=tmpn[hl][0:64, :],
                                          in1=rden[hl % 2][0:64, :], op=ALU.mult), r=[B_tmpn[hl], B_rden[hl % 2]],
                                 wj=[B_gated])
                    return f
                norm_pending = [mk_norm(G, pair, gq) for pair in range(2)]
                if G == 3:
                    for f in norm_pending:
                        f()
                    norm_pending = []

        def post(j, g0, g1):
            for b in (0, 1, -1, 2, 3, -2):
                if b < 0:
                    for b2 in ((0, 1) if b == -1 else (2, 3)):
                        q = b2 % 2
                        row0 = j * 512 + b2 * 128
                        ln_block(None, None, [(xfA[q], B_xfA[q], am[:, 0:1]), (xfB[q], B_xfB[q], am[:, 1:2])], None,
                                 zz[q], B_zz[q], zn[q], B_zn[q], junk, B_junk, q, None, None,
                                 out_d[row0:row0 + 128, :], B_out, f"outst{q}", B_dst_join=True, st_eng="pool",
                                 part="rest", ln_act=True)
                    continue
                q = b % 2
                pq_load(xfA[q], X1F[g0 * 512 + b * 128:g0 * 512 + (b + 1) * 128, :], [B_xfA[q]], "xfA", r=[B_X1F[g0]])
                pq_load(xfB[q], X1F[g1 * 512 + b * 128:g1 * 512 + (b + 1) * 128, :], [B_xfB[q]], "xfB", r=[B_X1F[g1]])
                yb = [nb(), nb()]
                for hf in range(2):
                    for k in range(12):
                        P.op("pe", I("matmul",
                            yb[hf].f[:, :], gated[:, k, b * 128:(b + 1) * 128], wout[:, k, hf * 512:(hf + 1) * 512],
                            start=(k == 0), stop=(k == 11)), r=[B_gated, B_wout], w=[yb[hf].buf])
                row0 = j * 512 + b * 128
                ln_block(yb[0], yb[1], [(xfA[q], B_xfA[q], am[:, 0:1]), (xfB[q], B_xfB[q], am[:, 1:2])], None,
                         zz[q], B_zz[q], zn[q], B_zn[q], junk, B_junk, q, None, None,
                         out_d[row0:row0 + 128, :], B_out, f"outst{q}", B_dst_join=True, st_eng="pool", part="z", ln_act=True)
        for j, (g0, g1, nch) in enumerate(slots):
            if j == 0:
                pre_A(0, g0, g1)
                pre_B(0)
            attention(j, nch)
            if j + 1 < len(slots):
                pre_A(j + 1, slots[j + 1][0], slots[j + 1][1])
            post(j, g0, g1)
            if j + 1 < len(slots):
                pre_B(j + 1)
        A.release(base_mark)

    P.emit()
    return nc


def _consts():
    bf = ml_dtypes.bfloat16
    cb = np.zeros((128, 3 * 128 + 16 * 67), np.float32)
    cb[:, 0:128] = np.eye(128)
    cb[:, 128:256] = 1.0
    E = np.zeros((128, 16, 67), np.float32)
    for h in range(16):
        for i in range(3):
            E[h * 3 + i, h, 64 + i] = 1.0
    cb[:, 384:] = E.reshape(128, -1)
    cf = np.zeros((128, 384), np.float32)
    s = np.arange(128)
    cf[:, 0:128] = (s[:, None] <= s[None, :]).astype(np.float32)
    cf[:, 128:256] = 1.0
    invc = np.zeros((4, 16), np.float32)
    for g, w in enumerate(POOLW):
        invc[g] = 1.0 / np.minimum(np.arange(16) + 1, w)
    cf[:, 256:320] = invc.reshape(1, 64)
    cf[:, 320:384] = 1.0
    return cb.astype(bf), cf


def _masks(hh):
    bf = ml_dtypes.bfloat16
    s = np.arange(128)[:, None, None]
    kb = np.arange(4)[None, :, None]
    t = np.arange(512)[None, None, :]
    diag = np.where(kb * 128 + s > t, MASKNEG, 0.0).astype(np.float32)
    full = np.full((128, 4, 512), MASKNEG, np.float32)
    none = np.zeros((128, 4, 512), np.float32)
    a = np.stack([diag, full], axis=1).reshape(128, -1)
    b = np.stack([none, diag], axis=1).reshape(128, -1)
    m = np.stack([a, b] if hh == 0 else [b, a], axis=0)
    return m.astype(bf)


def make_in_maps(inputs, S):
    cb, cf = _consts()
    x = np.asarray(inputs["x"], np.float32)
    B = x.shape[0]
    lnp = np.ascontiguousarray(np.stack([inputs["ln_g"], inputs["ln_b"]], axis=1).astype(np.float32))
    pscale = np.ascontiguousarray(np.asarray(inputs["pool_scale"], np.float32).reshape(8, 128).T)
    maps = []
    for c in range(2 * B):
        b, hh = c // 2, c % 2
        sel = np.zeros((128, 2), np.float32)
        sel[:, hh] = 1.0
        maps.append({
            "x": np.ascontiguousarray(x[b]),
            "mem": np.ascontiguousarray(np.asarray(inputs["mem"], np.float32)[b]),
            "w_in": np.asarray(inputs["w_in"], np.float32),
            "w_mem_kv": np.asarray(inputs["w_mem_kv"], np.float32),
            "w_out": np.asarray(inputs["w_out"], np.float32),
            "pool_w": np.ascontiguousarray(np.asarray(inputs["pool_w"], np.float32)[0]),
            "w_kv": np.asarray(inputs["w_kv_shared"], np.float32),
            "lnp": lnp,
            "pscale": pscale,
            "bfor": np.asarray(inputs["b_forget"], np.float32),
            "cb16": cb,
            "cf32": cf,
            "sel": sel,
            "masks": _masks(hh),
        })
    return maps


_NC_CACHE = {}


def kernel(x, mem, w_in, w_mem_kv, w_out, ln_g, ln_b, pool_w, pool_scale, w_kv_shared, b_forget):
    inputs = dict(x=x, mem=mem, w_in=w_in, w_mem_kv=w_mem_kv, w_out=w_out, ln_g=ln_g, ln_b=ln_b,
                  pool_w=pool_w, pool_scale=pool_scale, w_kv_shared=w_kv_shared, b_forget=b_forget)
    B, S, _ = np.asarray(x).shape
    if S not in _NC_CACHE:
        _NC_CACHE[S] = build(S)
    nc = _NC_CACHE[S]
    maps = make_in_maps(inputs, S)
    res = run_bass_kernel_spmd(nc, maps, core_ids=list(range(2 * B)))
    out = np.zeros((B, S, D), np.float32)
    st = slot_tiles(S // 512)
    for c in range(2 * B):
        b, hh = c // 2, c % 2
        o = np.asarray(res.results[c]["out"], np.float32)
        for j, (g0, g1, _) in enumerate(st):
            g = g0 if hh == 0 else g1
            out[b, g * 512:(g + 1) * 512, :] = o[j * 512:(j + 1) * 512, :]
    return out
```

```python
import numpy as np
import ml_dtypes
import concourse.bass as bass
import concourse.mybir as mybir
from concourse.bass_utils import run_bass_kernel_spmd

dt = mybir.dt
F32, BF16 = dt.float32, dt.bfloat16
AF = mybir.ActivationFunctionType
ALU = mybir.AluOpType

D = 1024
NMEM = 256
DIN = 3072
DMIX = 1536
NH = 16
DH = 64
ALPHA = 4.0 ** 0.25
EPS = 1e-5
POOLW = (2, 4, 8, 16)
MASKNEG = -30000.0

SAME_ENGINE_SYNC = True


class Buf:
    __slots__ = ("name", "w", "r", "jd")

    def __init__(self, name):
        self.name = name
        self.w = []
        self.r = []
        self.jd = []


class Op:
    __slots__ = ("eng", "fn", "deps", "sig", "sem", "val", "dma", "key", "idx")


ENGS = ("pe", "act", "dve", "pool", "sp")


class Prog:
    def __init__(self, nc):
        self.nc = nc
        self.ops = {e: [] for e in ENGS}
        self.n = 0

    def op(self, eng, fn, r=(), w=(), wj=(), dma=False, key=None):
        o = Op()
        o.eng, o.fn, o.dma, o.key = eng, fn, dma, key
        o.sig = dma
        o.sem = None
        o.val = 0
        o.idx = self.n
        self.n += 1
        deps = []
        for b in r:
            deps += b.w
        for b in w:
            deps += b.w
            deps += b.r
        for b in wj:
            deps += b.jd
        for b in r:
            b.r.append(o)
        for b in w:
            b.jd = b.w + b.r
            b.w = [o]
            b.r = []
        for b in wj:
            b.w.append(o)
        seen = set()
        fd = []
        for d in deps:
            if id(d) in seen or d is o:
                continue
            seen.add(id(d))
            if (not d.dma) and d.eng == eng and (eng == "pe" or not SAME_ENGINE_SYNC):
                continue
            fd.append(d)
            d.sig = True
        o.deps = fd
        self.ops[eng].append(o)
        return o

    def barrier(self, bufs):
        b = Buf("barrier")
        lasts = []
        for e in ENGS:
            if self.ops[e]:
                lasts.append(self.ops[e][-1])
        for x in bufs:
            lasts += x.w + x.r
        for e in ENGS:
            o = Op()
            o.eng, o.fn, o.dma, o.key = e, None, False, None
            o.sig = False
            o.sem = None
            o.val = 0
            o.idx = self.n
            self.n += 1
            o.deps = []
            seen = set()
            for d in lasts:
                if id(d) in seen:
                    continue
                seen.add(id(d))
                if (not d.dma) and d.eng == e:
                    continue
                o.deps.append(d)
                d.sig = True
            self.ops[e].append(o)

    def emit(self):
        nc = self.nc
        engsem = {e: nc.alloc_semaphore("s_" + e) for e in ("pe", "act", "dve", "pool")}
        keysem = {}
        cnt = {}
        for e in ENGS:
            for o in self.ops[e]:
                if o.dma:
                    k = o.key
                    if k not in keysem:
                        keysem[k] = nc.alloc_semaphore("d_" + k)
                        cnt[k] = 0
                    cnt[k] += 16
                    o.sem, o.val = keysem[k], cnt[k]
                elif o.sig:
                    cnt[e] = cnt.get(e, 0) + 1
                    o.sem, o.val = engsem[e], cnt[e]
        finals = [(keysem[k], cnt[k]) for k in keysem]
        ops = self.ops

        def mk(e):
            def body(eng):
                wm = {}
                for o in ops[e]:
                    need = {}
                    for d in o.deps:
                        k = d.sem.num
                        if wm.get(k, 0) < d.val and need.get(k, (None, 0))[1] < d.val:
                            need[k] = (d.sem, d.val)
                    for k, (s, v) in need.items():
                        eng.wait_ge(s, v)
                        wm[k] = v
                    if o.fn is None:
                        continue
                    inst = o.fn(eng)
                    if o.sig:
                        inst.then_inc(o.sem, 16 if o.dma else 1)
                if e == "sp":
                    for s, v in finals:
                        eng.wait_ge(s, v)
            return body

        with nc.Block() as blk:
            blk.tensor(mk("pe"))
            blk.scalar(mk("act"))
            blk.vector(mk("dve"))
            blk.gpsimd(mk("pool"))
            blk.sync(mk("sp"))


class Arena:
    def __init__(self, nc, nbytes):
        self.h32 = nc.alloc_sbuf_tensor("arena", [128, nbytes // 4], F32)
        self.h16 = self.h32.bitcast(BF16)
        self.top = 0
        self.cap = nbytes

    def mark(self):
        return self.top

    def release(self, m):
        self.top = m

    def t(self, dtype, *shape):
        n = 1
        for s in shape:
            n *= s
        nb = n * (4 if dtype == F32 else 2)
        nb = (nb + 63) // 64 * 64
        off = self.top
        self.top += nb
        assert self.top <= self.cap, f"SBUF arena overflow {self.top} > {self.cap}"
        if dtype == F32:
            ap = self.h32[:, off // 4: off // 4 + n]
        else:
            ap = self.h16[:, off // 2: off // 2 + n]
        if len(shape) == 2:
            ap = ap.rearrange("p (a b) -> p a b", a=shape[0])
        elif len(shape) == 3:
            ap = ap.rearrange("p (a b c) -> p a b c", a=shape[0], b=shape[1])
        return ap


def own_tiles(hh, n512):
    return [g for g in range(n512) if ((g % 4) in (0, 3)) == (hh == 0)]


def I(name, *a, **k):
    return lambda e: getattr(e, name)(*a, **k)


class Bank:
    def __init__(self, P32, P16, k):
        self.buf = Buf(f"bank{k}")
        self.f = P32[:, k * 512:(k + 1) * 512]
        self.h = P16[:, k * 1024:(k + 1) * 1024]


def slot_tiles(n512):
    res = []
    for j in range(n512 // 2):
        p, e = j // 2, j % 2
        g0 = 4 * p + (0, 3)[e]
        g1 = 4 * p + (1, 2)[e]
        res.append((g0, g1, max(g0, g1) + 1))
    return res


def build(S, phases=("a1", "a2", "b"), debug=False, TA=256):
    assert S % 2048 == 0
    NT512 = S // 512
    NBLK = S // 128
    NSLOT = NT512 // 2
    nc = bass.Bass("TRN2", target_bir_lowering=False)
    P = Prog(nc)

    def din(name, shape, dtype=F32):
        return nc.dram_tensor(name, list(shape), dtype, kind="ExternalInput").ap()

    x_d = din("x", [S, D])
    mem_d = din("mem", [NMEM, D])
    w_in_d = din("w_in", [2, D, DIN])
    w_mkv_d = din("w_mem_kv", [2, D, D])
    w_out_d = din("w_out", [2, DMIX, D])
    pool_w_d = din("pool_w", [4, 256, 256])
    w_kv_d = din("w_kv", [D, 2064])
    lnp_d = din("lnp", [2, 2, D])
    pscale_d = din("pscale", [128, 8])
    bfor_d = din("bfor", [16])
    cb16_d = din("cb16", [128, 3 * 128 + 16 * 67], BF16)
    cf32_d = din("cf32", [128, 128 + 128 + 64 + 64], F32)
    sel_d = din("sel", [128, 2])
    masks_d = din("masks", [2, 128, 2 * 4 * 512], BF16)
    out_d = nc.dram_tensor("out", [NSLOT * 512, D], F32, kind="ExternalOutput").ap()

    kind_scr = "ExternalOutput" if debug else "Internal"
    X1F = nc.dram_tensor("x1f_scr", [S, D], F32, kind=kind_scr).ap()
    X1T = nc.dram_tensor("x1t_scr", [8, 128, S], BF16, kind=kind_scr).ap()
    KT = nc.dram_tensor("kt_scr", [8, 128, S], BF16, kind=kind_scr).ap()
    VA = nc.dram_tensor("va_scr", [NBLK, 128, NH, 128], BF16, kind=kind_scr).ap()
    KC = nc.dram_tensor("kc_scr", [NH * 3, S], BF16, kind=kind_scr).ap()
    if debug:
        CK_d = nc.dram_tensor("ck_dbg", [128, NBLK * 16], F32, kind="ExternalOutput").ap()

    A = Arena(nc, 212736)
    P32 = nc.alloc_psum_tensor("ps", [128, 4096], F32)
    P16 = P32.bitcast(BF16)
    banks = [Bank(P32, P16, k) for k in range(8)]
    bank_rr = [0]

    def nb():
        b = banks[bank_rr[0] % 8]
        bank_rr[0] += 1
        return b

    cb16 = A.t(BF16, 3 * 128 + 16 * 67)
    ident = cb16[:, 0:128]
    ones16 = cb16[:, 128:256]
    Emat = cb16[:, 384:384 + 16 * 67].rearrange("p (h m) -> p h m", h=16)
    cf32 = A.t(F32, 384)
    tri = cf32[:, 0:128]
    allones = cf32[:, 128:256]
    invc = cf32[:, 256:320].rearrange("p (w t) -> p w t", w=4)
    ones64 = cf32[:, 320:384]
    pscale = A.t(F32, 8)
    bfor = A.t(F32, 16)
    selt = A.t(F32, 2)
    CK = A.t(F32, NBLK, 16)
    CS = A.t(BF16, NBLK, 48)
    Tn = A.t(F32, 16)
    lng = A.t(F32, D)
    lnb = A.t(F32, D)
    mkT = A.t(BF16, 4, NMEM)
    mv = A.t(BF16, 2, 512)
    small = A.t(F32, 64)

    B_const = Buf("const")
    B_ln = Buf("lnp")
    B_mkv = Buf("mkv")
    B_CK = [Buf(f"ck{n}") for n in range(NBLK)]
    B_CS = [Buf(f"cs{n}") for n in range(NBLK)]
    B_Tn = Buf("Tn")
    B_X1F = [Buf(f"x1f{g}") for g in range(NT512)]
    B_X1T = [Buf(f"x1t{g}") for g in range(NT512)]
    B_KV = [Buf(f"kv{g}") for g in range(NT512)]

    def sp_load(out, in_, w, key, r=()):
        return P.op("sp", I("dma_start", out=out, in_=in_), r=r, w=w, dma=True, key=key)

    def pq_load(out, in_, w, key, r=(), wj=()):
        return P.op("pool", I("dma_start", out=out, in_=in_), r=r, w=w, wj=wj, dma=True, key=key)

    def cast_load(out, in_, w, key, r=(), wj=()):
        return P.op("pool", I("dma_start", out=out, in_=in_), r=r, w=w, wj=wj, dma=True, key=key)

    sp_load(cb16, cb16_d, [B_const], "c0")
    P.op("sp", I("dma_start", out=cf32, in_=cf32_d), wj=[B_const], dma=True, key="c0")
    P.op("sp", I("dma_start", out=pscale, in_=pscale_d), wj=[B_const], dma=True, key="c0")
    P.op("sp", I("dma_start", out=bfor, in_=bfor_d.partition_broadcast(128)), wj=[B_const], dma=True, key="c0")
    P.op("sp", I("dma_start", out=selt, in_=sel_d), wj=[B_const], dma=True, key="c0")

    def load_ln(layer):
        sp_load(lng, lnp_d[layer, 0].partition_broadcast(128), [B_ln], "ln")
        P.op("sp", I("dma_start", out=lnb, in_=lnp_d[layer, 1].partition_broadcast(128)),
             wj=[B_ln], dma=True, key="ln")

    def load_w(dst, src_rows_by_cols, nk, ncols, buf, key, first=True):
        srcv = src_rows_by_cols.rearrange("(k p) n -> p k n", p=128)
        for k in range(nk):
            for c0 in range(0, ncols, 1024):
                step = min(1024, ncols - c0)
                o = dst[:, k, c0:c0 + step]
                i = srcv[:, k, c0:c0 + step]
                if first:
                    cast_load(o, i, [buf], key)
                    first = False
                else:
                    cast_load(o, i, [], key, wj=[buf])

    def transposes_to(dstT, src_tok, nblk, r_src, w_dst, alt, all_act=False):
        for c in range(8):
            bk = nb()
            for b in range(nblk):
                P.op("pe", I("transpose",
                    out=bk.h[:, b * 128:(b + 1) * 128], in_=src_tok[:, b, c * 128:(c + 1) * 128],
                    identity=ident), r=[r_src[b] if isinstance(r_src, list) else r_src, B_const], w=[bk.buf])
            n = nblk * 128
            if (c + alt) % 2 == 0 and not all_act:
                P.op("dve", I("tensor_copy", out=dstT[:, c, 0:n], in_=bk.h[:, 0:n]),
                     r=[bk.buf], w=[] if c else [w_dst], wj=[w_dst] if c else [])
            else:
                P.op("act", I("activation", out=dstT[:, c, 0:n], in_=bk.h[:, 0:n], func=AF.Copy),
                     r=[bk.buf], w=[] if c else [w_dst], wj=[w_dst] if c else [])

    def proj(bk, W, wbuf, col0, m, xT, xbuf, n, extra_first=None):
        first = True
        if extra_first is not None:
            extra_first(bk)
            first = False
        for k in range(8):
            P.op("pe", I("matmul",
                bk.f[0:m, 0:n], W[:, k, col0:col0 + m], xT[:, k, 0:n], start=first, stop=(k == 7)),
                r=[wbuf, xbuf], w=[bk.buf])
            first = False

    def mem_kv(layer, wtmp, B_wtmp, memb, B_memb, memT, B_memT):
        load_w(wtmp, w_mkv_d[layer], 8, D, B_wtmp, "wtmp")
        cast_load(memb, mem_d.rearrange("(b p) d -> p b d", p=128), [B_memb], "memb")
        transposes_to(memT, memb, 2, B_memb, B_memT, 0)
        for h in range(4):
            bk = nb()
            proj(bk, wtmp, B_wtmp, h * 128, 128, memT, B_memT, NMEM)
            P.op("act", I("activation", out=mkT[:, h, :], in_=bk.f[:, 0:NMEM], func=AF.Copy),
                 r=[bk.buf], w=[B_mkv] if h == 0 else [], wj=[] if h == 0 else [B_mkv])
        for j in range(2):
            bk = nb()
            for k in range(8):
                P.op("pe", I("matmul",
                    bk.f[:, :], memT[:, k, j * 128:(j + 1) * 128], wtmp[:, k, 512:1024],
                    start=(k == 0), stop=(k == 7)), r=[B_wtmp, B_memT], w=[bk.buf])
            P.op("dve", I("tensor_copy", out=mv[:, j, :], in_=bk.f[:, :]),
                 r=[bk.buf], wj=[B_mkv])

    def mem_attn(h, qm, B_qm, n, PT, B_PT, ybk, dbk):
        for j in range(2):
            lb = nb()
            P.op("pe", I("matmul",
                lb.f[:, 0:n], mkT[:, h, j * 128:(j + 1) * 128], qm[:, h, 0:n], start=True, stop=True),
                r=[B_mkv, B_qm], w=[lb.buf])
            P.op("act", I("activation", out=PT[j][:, 0:n], in_=lb.f[:, 0:n], func=AF.Exp),
                 r=[lb.buf], w=[B_PT[j]])
        for j in range(2):
            P.op("pe", I("matmul",
                ybk.f[:, 0:n], mv[:, j, h * 128:(h + 1) * 128], PT[j][:, 0:n], start=(j == 0), stop=(j == 1)),
                r=[B_mkv, B_PT[j]], w=[ybk.buf])
        for j in range(2):
            P.op("pe", I("matmul",
                dbk.f[:, 0:n], ones16, PT[j][:, 0:n], start=(j == 0), stop=(j == 1)),
                r=[B_const, B_PT[j]], w=[dbk.buf])

    def ln_block(ybk0, ybk1, xres, B_xres, z, B_z, zn, B_zn, junk, B_junk, slot, x1b, B_x1b,
                 dst_dram, B_dst, key, B_dst_join=False, res_scale=ALPHA, st_eng="sp", ln_act=False, part="all"):
        sm = small[:, slot * 8:(slot + 1) * 8]
        B_sm = B_small[slot]
        if not isinstance(xres, list):
            xres_l = [(xres, B_xres, float(res_scale))]
        else:
            xres_l = xres
        for hf, yb in enumerate((ybk0, ybk1) if part in ("all", "z") else ()):
            for ci, (xr, Bxr, sc) in enumerate(xres_l):
                last = ci == len(xres_l) - 1
                firstw = (hf == 0 and ci == 0)
                kw = dict(accum_out=sm[:, hf:hf + 1]) if last else {}
                in1 = yb.f[:, :] if ci == 0 else z[:, hf * 512:(hf + 1) * 512]
                rr = [Bxr, yb.buf] if ci == 0 else [Bxr, B_z]
                if not isinstance(sc, float):
                    rr = rr + [B_const]
                P.op("dve", I("scalar_tensor_tensor",
                    out=z[:, hf * 512:(hf + 1) * 512], in0=xr[:, hf * 512:(hf + 1) * 512], scalar=sc,
                    in1=in1, op0=ALU.mult, op1=ALU.add, **kw),
                    r=rr, w=[B_z, B_sm] if firstw else [], wj=[] if firstw else [B_z, B_sm])
        if part == "z":
            return
        if ln_act:
            P.op("act", I("activation", out=zn, in_=z, func=AF.Square, accum_out=sm[:, 2:3]), r=[B_z], w=[B_zn], wj=[B_sm])
        else:
            P.op("dve", I("scalar_tensor_tensor", out=zn, in0=z, scalar=1.0, in1=z, op0=ALU.mult, op1=ALU.mult,
                          accum_out=sm[:, 2:3]), r=[B_z], w=[B_zn], wj=[B_sm])
        P.op("dve", I("tensor_scalar", out=sm[:, 3:4], in0=sm[:, 0:1], scalar1=sm[:, 1:2], scalar2=1.0 / D,
                                              op0=ALU.add, op1=ALU.mult), r=[B_sm], wj=[B_sm])
        P.op("dve", I("tensor_tensor", out=sm[:, 4:5], in0=sm[:, 3:4], in1=sm[:, 3:4], op=ALU.mult),
             r=[B_sm], wj=[B_sm])
        P.op("dve", I("scalar_tensor_tensor", out=sm[:, 5:6], in0=sm[:, 2:3], scalar=1.0 / D, in1=sm[:, 4:5],
                                                     op0=ALU.mult, op1=ALU.subtract), r=[B_sm], wj=[B_sm])
        P.op("dve", I("tensor_scalar", out=sm[:, 5:6], in0=sm[:, 5:6], scalar1=EPS, scalar2=None, op0=ALU.add),
             r=[B_sm], wj=[B_sm])
        P.op("pool", I("tensor_tensor", out=sm[:, 6:7], in0=sm[:, 5:6], in1=epsc[:, 1:2], op=ALU.pow),
             r=[B_sm, B_const], wj=[B_sm])
        P.op("dve", I("scalar_tensor_tensor", out=sm[:, 7:8], in0=sm[:, 3:4], scalar=-1.0, in1=sm[:, 6:7],
                                                     op0=ALU.mult, op1=ALU.mult), r=[B_sm], wj=[B_sm])
        P.op("dve", I("tensor_scalar", out=zn, in0=z, scalar1=sm[:, 6:7], scalar2=sm[:, 7:8], op0=ALU.mult, op1=ALU.add),
             r=[B_z, B_sm], w=[B_zn])
        P.op("dve", I("tensor_tensor", out=z, in0=zn, in1=lng, op=ALU.mult), r=[B_zn, B_ln], w=[B_z])
        P.op("pool" if ln_act else "dve", I("tensor_tensor", out=zn, in0=z, in1=lnb, op=ALU.add), r=[B_z, B_ln], w=[B_zn])
        if x1b is not None:
            P.op("pool", I("tensor_copy", out=x1b, in_=zn), r=[B_zn], w=[B_x1b])
        if dst_dram is not None:
            if B_dst_join:
                P.op(st_eng, I("dma_start", out=dst_dram, in_=zn), r=[B_zn], wj=[B_dst], dma=True, key=key)
            else:
                P.op(st_eng, I("dma_start", out=dst_dram, in_=zn), r=[B_zn], w=[B_dst], dma=True, key=key)

    epsc = A.t(F32, 2)
    P.op("dve", I("memset", epsc[:, 0:1], EPS), wj=[B_const])
    P.op("dve", I("memset", epsc[:, 1:2], -0.5), wj=[B_const])
    B_small = [Buf(f"small{i}") for i in range(8)]
    P.op("dve", I("memset", Tn, 0.0), w=[B_Tn])

    base_mark = A.mark()

    if "a1" in phases:
        NB_A = TA // 128
        NTA = S // TA
        win = A.t(BF16, 8, DIN)
        wout = A.t(BF16, 12, D)
        poolw = A.t(BF16, 8, 256)
        B_win, B_wout, B_poolw = Buf("win"), Buf("wout"), Buf("poolw")
        m_tmp = A.mark()
        wtmp = A.t(BF16, 8, D)
        memb = A.t(BF16, 2, D)
        memT = A.t(BF16, 8, NMEM)
        B_wtmp, B_memb, B_memT = Buf("wtmp"), Buf("memb"), Buf("memT")
        load_ln(0)
        mem_kv(0, wtmp, B_wtmp, memb, B_memb, memT, B_memT)
        load_w(win, w_in_d[0], 8, DIN, B_win, "win")
        load_w(poolw, pool_w_d.rearrange("g r c -> (g r) c"), 8, 256, B_poolw, "poolw")
        load_w(wout, w_out_d[0], 12, D, B_wout, "wout")
        P.barrier([B_mkv])
        A.release(m_tmp)

        xb = [A.t(BF16, NB_A, D)] * 2
        B_xb = [Buf("xb0")] * 2
        xc = A.t(F32, NB_A, D)
        B_xc = Buf("xc")
        xf = [A.t(F32, D) for _ in range(2)]
        B_xf = [Buf("xf0"), Buf("xf1")]
        xT = [A.t(BF16, 8, TA) for _ in range(2)]
        B_xT = [Buf("xT0"), Buf("xT1")]
        HW_ = TA + 16
        U = [A.t(F32, 2, HW_) for _ in range(4)]
        B_U = [Buf(f"U{g}") for g in range(4)]
        T1 = A.t(F32, 2, HW_)
        T2 = A.t(F32, 2, HW_)
        B_T1, B_T2 = Buf("T1"), Buf("T2")
        Hh = A.t(F32, 8, 16)
        B_H = [Buf(f"H{c}") for c in range(8)]
        pm = [A.t(BF16, 2, TA) for _ in range(4)]
        B_pm = [Buf(f"pm{g}") for g in range(4)]
        gsb = A.t(F32, 12, TA)
        B_gsb = [Buf(f"gsb{i}") for i in range(12)]
        qm = A.t(BF16, 4, TA)
        B_qm = Buf("qm")
        PT8 = [[A.t(BF16, TA) for _ in range(2)] for _ in range(4)]
        B_PT8 = [[Buf(f"PT{h}{j}") for j in range(2)] for h in range(4)]
        rd = A.t(F32, TA)
        t1 = A.t(F32, TA)
        B_rd, B_t1 = Buf("rd"), Buf("t1")
        tfix = A.t(F32, 16)
        B_tfix = Buf("tfix")
        gated = A.t(BF16, 12, TA)
        B_gated = Buf("gated")
        zz = [A.t(F32, D) for _ in range(2)]
        zn = [A.t(F32, D) for _ in range(2)]
        B_zz = [Buf("z0"), Buf("z1")]
        B_zn = [Buf("zn0"), Buf("zn1")]
        junk, B_junk = None, None
        x1b = A.t(BF16, NB_A, D)
        B_x1b = [Buf(f"x1b{b}") for b in range(NB_A)]
        x1T = A.t(BF16, 8, TA)
        B_x1T = Buf("x1T")
        P.op("pool", I("memset", Hh, 0.0), w=B_H)

        def load_xc(i):
            if i < NTA:
                sp_load(xc, x_d[i * TA:(i + 1) * TA, :].rearrange("(b p) d -> p b d", p=128), [B_xc], "xc")

        def st_T(i):
            s = i % 2
            for b in range(NB_A):
                P.op("act", I("activation", out=xb[s][:, b, :], in_=xc[:, b, :], func=AF.Copy), r=[B_xc],
                     w=[B_xb[s]] if b == 0 else [], wj=[] if b == 0 else [B_xb[s]])
            transposes_to(xT[s], xb[s], NB_A, B_xb[s], B_xT[s], i, all_act=True)
            load_xc(i + 1)

        def st_U(i):
            s = i % 2
            for g in range(4):
                for ch in range(2):
                    c = 2 * g + ch
                    bk = nb()
                    proj(bk, win, B_win, c * 128, 128, xT[s], B_xT[s], TA)
                    P.op("pool", I("tensor_copy", out=U[g][:, ch, 0:16], in_=Hh[:, c, :]),
                         r=[B_H[c]], w=[B_U[g]] if ch == 0 else [], wj=[] if ch == 0 else [B_U[g]])
                    P.op("act", I("activation", out=U[g][:, ch, 16:16 + TA], in_=bk.f[:, 0:TA], func=AF.Copy),
                         r=[bk.buf], wj=[B_U[g]])
                    P.op("act", I("activation", out=Hh[:, c, :], in_=bk.f[:, TA - 16:TA], func=AF.Copy),
                         r=[bk.buf], w=[B_H[c]])

        def st_Upool(i):
            for g in range(4):
                src_, Bs = U[g], B_U[g]
                dsts = [(T1, B_T1), (T2, B_T2)]
                sh = 1
                for lvl in range(g + 1):
                    dstt, Bd = dsts[lvl % 2]
                    lo = 2 * sh - 1
                    P.op("dve", I("tensor_tensor", out=dstt[:, :, lo:HW_], in0=src_[:, :, lo:HW_],
                                  in1=src_[:, :, lo - sh:HW_ - sh], op=ALU.add), r=[Bs], w=[Bd])
                    src_, Bs = dstt, Bd
                    sh *= 2
                wv = POOLW[g]
                P.op("dve", I("scalar_tensor_tensor", out=pm[g][:, :, :], in0=src_[:, :, 16:HW_], scalar=1.0 / wv,
                              in1=U[g][:, :, 16:HW_], op0=ALU.mult, op1=ALU.subtract), r=[Bs, B_U[g]], w=[B_pm[g]])
                if i == 0:
                    for ch in range(2):
                        P.op("dve", I("tensor_tensor", out=tfix, in0=src_[:, ch, 16:32], in1=invc[:, g, :], op=ALU.mult),
                             r=[Bs, B_const], w=[B_tfix])
                        P.op("dve", I("tensor_tensor", out=pm[g][:, ch, 0:16], in0=tfix, in1=U[g][:, ch, 16:32],
                                      op=ALU.subtract), r=[B_tfix, B_U[g]], wj=[B_pm[g]])

        def st_S2(i):
            s = i % 2
            xTs, BxT = xT[s], B_xT[s]
            for h in range(4):
                bk = nb()
                proj(bk, win, B_win, D + h * 128, 128, xTs, BxT, TA)
                P.op("act", I("activation", out=qm[:, h, :], in_=bk.f[:, 0:TA], func=AF.Copy, scale=128.0 ** -0.5),
                     r=[bk.buf], w=[B_qm] if h == 0 else [], wj=[] if h == 0 else [B_qm])

            def gate(c):
                gb = nb()
                proj(gb, win, B_win, DMIX + c * 128, 128, xTs, BxT, TA)
                P.op("act", I("activation", out=gsb[:, c, :], in_=gb.f[:, 0:TA], func=AF.Silu),
                     r=[gb.buf], w=[B_gsb[c]])
            for c in range(4):
                gate(c)
            for h in range(4):
                for j in range(2):
                    lb = nb()
                    P.op("pe", I("matmul", lb.f[:, 0:TA], mkT[:, h, j * 128:(j + 1) * 128], qm[:, h, :], start=True, stop=True),
                         r=[B_mkv, B_qm], w=[lb.buf])
                    P.op("act", I("activation", out=PT8[h][j], in_=lb.f[:, 0:TA], func=AF.Exp),
                         r=[lb.buf], w=[B_PT8[h][j]])
            for c in range(4, 12):
                gate(c)
            ybks, dbks = [], []
            for h in range(4):
                ybk = nb()
                if h % 2 == 0:
                    dbk2 = nb()
                ybks.append(ybk)
                dbks.append((dbk2, (h % 2) * TA))
                for j in range(2):
                    P.op("pe", I("matmul", ybk.f[:, 0:TA], mv[:, j, h * 128:(h + 1) * 128], PT8[h][j], start=(j == 0),
                                 stop=(j == 1)), r=[B_mkv, B_PT8[h][j]], w=[ybk.buf])
                for j in range(2):
                    P.op("pe", I("matmul", dbk2.f[:, (h % 2) * TA:(h % 2) * TA + TA], ones16, PT8[h][j], start=(j == 0),
                                 stop=(j == 1)), r=[B_const, B_PT8[h][j]], w=[dbk2.buf])
            for h in range(4):
                ybk = ybks[h]
                dbk, doff = dbks[h]
                P.op("dve", I("reciprocal", out=rd, in_=dbk.f[:, doff:doff + TA]), r=[dbk.buf], w=[B_rd])
                P.op("dve", I("tensor_tensor", out=t1, in0=ybk.f[:, 0:TA], in1=rd, op=ALU.mult),
                     r=[ybk.buf, B_rd], w=[B_t1])
                P.op("dve", I("tensor_tensor", out=gated[:, 8 + h, :], in0=t1, in1=gsb[:, 8 + h, :], op=ALU.mult),
                     r=[B_t1, B_gsb[8 + h]], w=[B_gated] if h == 0 else [], wj=[] if h == 0 else [B_gated])

        def st_S3(i):
            for g in range(4):
                for oc in range(2):
                    c = 2 * g + oc
                    mb = nb()
                    for kc in range(2):
                        P.op("pe", I("matmul", mb.f[:, 0:TA], poolw[:, g * 2 + kc, oc * 128:(oc + 1) * 128], pm[g][:, kc, :],
                                     start=(kc == 0), stop=(kc == 1)), r=[B_poolw, B_pm[g]], w=[mb.buf])
                    P.op("dve", I("scalar_tensor_tensor", out=gated[:, c, :], in0=mb.f[:, 0:TA], scalar=pscale[:, c:c + 1],
                                  in1=gsb[:, c, :], op0=ALU.mult, op1=ALU.mult),
                         r=[mb.buf, B_gsb[c], B_const], wj=[B_gated])

        def st_S4(i):
            t0 = i * TA
            g512 = t0 // 512
            for b in range(NB_A):
                q = (i * NB_A + b) % 2
                sp_load(xf[q], x_d[t0 + b * 128:t0 + (b + 1) * 128, :], [B_xf[q]], f"xf{q}")
            for b in range(NB_A):
                q = (i * NB_A + b) % 2
                yb = [nb(), nb()]
                for hf in range(2):
                    for k in range(12):
                        P.op("pe", I("matmul", yb[hf].f[:, :], gated[:, k, b * 128:(b + 1) * 128],
                                     wout[:, k, hf * 512:(hf + 1) * 512], start=(k == 0), stop=(k == 11)),
                             r=[B_gated, B_wout], w=[yb[hf].buf])
                tok0 = t0 + b * 128
                ln_block(yb[0], yb[1], xf[q], B_xf[q], zz[q], B_zz[q], zn[q], B_zn[q], junk, B_junk, q,
                         x1b[:, b, :], B_x1b[b], X1F[tok0:tok0 + 128, :], B_X1F[g512], f"x1f{q}",
                         B_dst_join=(tok0 % 512 != 0), ln_act=True, part="z")
            for b in range(NB_A):
                q = (i * NB_A + b) % 2
                tok0 = t0 + b * 128
                ln_block(None, None, xf[q], B_xf[q], zz[q], B_zz[q], zn[q], B_zn[q], junk, B_junk, q,
                         x1b[:, b, :], B_x1b[b], X1F[tok0:tok0 + 128, :], B_X1F[g512], f"x1f{q}",
                         B_dst_join=(tok0 % 512 != 0), ln_act=True, part="rest")

        def st_X1(i):
            t0 = i * TA
            g512 = t0 // 512
            transposes_to(x1T, x1b, NB_A, B_x1b, B_x1T, i, all_act=True)
            if t0 % 512 == 0:
                P.op("sp", I("dma_start", out=X1T[:, :, t0:t0 + TA].rearrange("c p t -> p c t"), in_=x1T),
                     r=[B_x1T], w=[B_X1T[g512]], dma=True, key="x1t")
            else:
                P.op("sp", I("dma_start", out=X1T[:, :, t0:t0 + TA].rearrange("c p t -> p c t"), in_=x1T),
                     r=[B_x1T], wj=[B_X1T[g512]], dma=True, key="x1t")

        load_xc(0)
        st_T(0)
        st_U(0)
        st_Upool(0)
        for i in range(NTA):
            if i + 1 < NTA:
                st_T(i + 1)
            st_S2(i)
            st_S3(i)
            if i + 1 < NTA:
                st_U(i + 1)
            if i >= 1:
                st_X1(i - 1)
            st_S4(i)
            if i + 1 < NTA:
                st_Upool(i + 1)
        st_X1(NTA - 1)
        P.barrier(B_X1F + B_X1T)
        A.release(base_mark)

    pre_b = None
    if "b" in phases:
        winB = A.t(BF16, 8, DIN)
        woutB = A.t(BF16, 12, D)
        B_winB, B_woutB = Buf("win1"), Buf("wout1")
        a2_base = A.mark()
        m_tmpB = A.mark()
        wtmpB = A.t(BF16, 8, D)
        membB = A.t(BF16, 2, D)
        memTB = A.t(BF16, 8, NMEM)
        B_wtmpB, B_membB, B_memTB = Buf("wtmp1"), Buf("memb1"), Buf("memT1")
        pre_b = True
    else:
        a2_base = base_mark
    if "a2" in phases:
        wkv = A.t(BF16, 8, 2064)
        B_wkv = Buf("wkv")
        load_w(wkv, w_kv_d, 8, 2064, B_wkv, "wkv")
    if pre_b:
        load_ln(1)
        mem_kv(1, wtmpB, B_wtmpB, membB, B_membB, memTB, B_memTB)
        load_w(winB, w_in_d[1], 8, DIN, B_winB, "win")
        load_w(woutB, w_out_d[1], 12, D, B_woutB, "wout")
    if "a2" in phases:
        xt2 = [A.t(BF16, 8, 512) for _ in range(2)]
        B_xt2 = [Buf("xt2a"), Buf("xt2b")]
        ksb = A.t(BF16, 8, 512)
        B_ksb = Buf("ksb")
        kcsb = A.t(BF16, 512)
        B_kcsb = Buf("kcsb")
        vsb = A.t(BF16, 4, NH, 128)
        B_vsb = Buf("vsb")
        bfor4 = A.t(F32, 4, 16)
        fl = A.t(F32, 4, 16)
        e1 = A.t(F32, 4, 16)
        lp = A.t(F32, 4, 16)
        r1 = A.t(F32, 16)
        r2 = A.t(F32, 16)
        B_fl, B_e1, B_lp, B_r1, B_r2 = Buf("fl"), Buf("e1"), Buf("lp"), Buf("r1"), Buf("r2")
        B_bf4 = Buf("bf4")
        for b in range(4):
            P.op("pool", I("tensor_copy", out=bfor4[:, b, :], in_=bfor), r=[B_const],
                 w=[B_bf4] if b == 0 else [], wj=[] if b == 0 else [B_bf4])
        P.op("pool", I("memset", vsb[:, :, :, 64:128], 1.0), w=[B_vsb])
        sp_load(xt2[0], X1T[:, :, 0:512].rearrange("c p t -> p c t"), [B_xt2[0]], "xt2a", r=[B_X1T[0]])
        for g in range(NT512):
            s = g % 2
            if g + 1 < NT512:
                sp_load(xt2[1 - s], X1T[:, :, (g + 1) * 512:(g + 2) * 512].rearrange("c p t -> p c t"),
                        [B_xt2[1 - s]], "xt2b" if 1 - s else "xt2a", r=[B_X1T[g + 1]])
            xt = xt2[s]
            Bxt = B_xt2[s]
            fb = nb()
            for b in range(4):
                for k in range(8):
                    P.op("pe", I("matmul",
                        fb.f[:, b * 16:(b + 1) * 16], xt[:, k, b * 128:(b + 1) * 128], wkv[:, k, 2048:2064],
                        start=(k == 0), stop=(k == 7)), r=[Bxt, B_wkv], w=[fb.buf])
            P.op("dve", I("tensor_tensor", out=fl, in0=fb.f[:, 0:64].rearrange("p (b h) -> p b h", b=4),
                                                         in1=bfor4, op=ALU.add), r=[fb.buf, B_bf4], w=[B_fl])
            P.op("act", I("activation", out=e1, in_=fl, func=AF.Exp, scale=-1.0), r=[B_fl], w=[B_e1])
            P.op("act", I("activation", out=lp, in_=e1, func=AF.Ln, bias=1.0), r=[B_e1], w=[B_lp])
            for b in range(4):
                n = g * 4 + b
                cb = nb()
                P.op("pe", I("matmul", cb.f[:, 0:16], tri, lp[:, b, :], start=True, stop=True),
                     r=[B_const, B_lp], w=[cb.buf])
                P.op("pe", I("matmul", cb.f[:, 16:32], allones, lp[:, b, :], start=True, stop=True),
                     r=[B_const, B_lp], w=[cb.buf])
                P.op("dve", I("tensor_tensor", out=CK[:, n, :], in0=cb.f[:, 0:16], in1=Tn, op=ALU.add),
                     r=[cb.buf, B_Tn], w=[B_CK[n]])
                P.op("dve", I("tensor_tensor", out=Tn, in0=cb.f[:, 16:32], in1=Tn, op=ALU.add),
                     r=[cb.buf], w=[B_Tn])
                CSv = CS[:, n, :].rearrange("p (h i) -> p h i", i=3)
                P.op("pool", I("tensor_scalar", out=CSv[:, :, 0], in0=CK[:, n, :], scalar1=-8.0,
                                                                      scalar2=None, op0=ALU.mult),
                     r=[B_CK[n]], w=[B_CS[n]])
                P.op("dve", I("scalar_tensor_tensor", out=r1, in0=CK[:, n, :], scalar=-8.0,
                                                                           in1=CSv[:, :, 0], op0=ALU.mult,
                                                                           op1=ALU.subtract),
                     r=[B_CK[n], B_CS[n]], w=[B_r1])
                P.op("pool", I("tensor_copy", out=CSv[:, :, 1], in_=r1), r=[B_r1], wj=[B_CS[n]])
                P.op("pool", I("tensor_tensor", out=r2, in0=r1, in1=CSv[:, :, 1], op=ALU.subtract),
                     r=[B_r1, B_CS[n]], w=[B_r2])
                P.op("pool", I("tensor_copy", out=CSv[:, :, 2], in_=r2), r=[B_r2], wj=[B_CS[n]])
            kcb = nb()
            for b in range(4):
                P.op("pe", I("transpose", out=kcb.h[0:48, b * 128:(b + 1) * 128], in_=CS[:, g * 4 + b, :], identity=ident),
                     r=[B_CS[g * 4 + b], B_const], w=[kcb.buf])
            P.op("act", I("activation", out=kcsb[0:48, :], in_=kcb.h[0:48, 0:512], func=AF.Copy, scale=-1.0),
                 r=[kcb.buf], w=[B_kcsb])
            P.op("sp", I("dma_start", out=KC[:, g * 512:(g + 1) * 512], in_=kcsb[0:48, :]), r=[B_kcsb], w=[B_KV[g]],
                 dma=True, key="kcst")
            for c in range(8):
                bk = nb()
                proj(bk, wkv, B_wkv, c * 128, 128, xt, Bxt, 512)
                if c % 2 == 0:
                    P.op("act", I("activation", out=ksb[:, c, :], in_=bk.f[:, :], func=AF.Copy),
                         r=[bk.buf], w=[B_ksb] if c == 0 else [], wj=[] if c == 0 else [B_ksb])
                else:
                    P.op("dve", I("tensor_copy", out=ksb[:, c, :], in_=bk.f[:, :]),
                         r=[bk.buf], wj=[B_ksb])
            P.op("sp", I("dma_start", out=KT[:, :, g * 512:(g + 1) * 512].rearrange("c p t -> p c t"),
                                                  in_=ksb), r=[B_ksb], wj=[B_KV[g]], dma=True, key="kst")
            for b in range(4):
                for hf in range(2):
                    bk = nb()
                    for k in range(8):
                        P.op("pe", I("matmul",
                            bk.f[:, :], xt[:, k, b * 128:(b + 1) * 128], wkv[:, k, 1024 + hf * 512:1024 + (hf + 1) * 512],
                            start=(k == 0), stop=(k == 7)), r=[Bxt, B_wkv], w=[bk.buf])
                    src_v = bk.f[:, :].rearrange("p (h d) -> p h d", h=8)
                    first = (b == 0 and hf == 0)
                    if (b + hf) % 2 == 0:
                        P.op("dve", I("tensor_copy",
                            out=vsb[:, b, hf * 8:(hf + 1) * 8, 0:64], in_=src_v),
                            r=[bk.buf], w=[B_vsb] if first else [], wj=[] if first else [B_vsb])
                    else:
                        P.op("act", I("activation",
                            out=vsb[:, b, hf * 8:(hf + 1) * 8, 0:64], in_=src_v, func=AF.Copy),
                            r=[bk.buf], wj=[B_vsb])
            P.op("sp", I("dma_start", out=VA[g * 4:(g + 1) * 4].rearrange("b p h e -> p b h e"), in_=vsb),
                 r=[B_vsb], wj=[B_KV[g]], dma=True, key="vst")
        if debug:
            P.op("sp", I("dma_start", out=CK_d, in_=CK.rearrange("p n h -> p (n h)")), r=B_CK, dma=True, key="dbg")
        P.barrier(B_KV + B_CK + B_CS + [B_mkv])
        A.release(a2_base)

    if "b" in phases:
        win, wout, B_win, B_wout = winB, woutB, B_winB, B_woutB
        masks = A.t(BF16, 2, 4, 512)
        B_masks = Buf("masks")
        am = A.t(F32, 2)
        P.op("dve", I("tensor_scalar", out=am, in0=selt, scalar1=float(ALPHA), scalar2=None, op0=ALU.mult),
             r=[B_const], wj=[B_const])
        if "a2" not in phases:
            P.barrier([B_mkv])

        xst = [A.t(BF16, 2, 2, 512)] * 2
        B_xst = [Buf("xst0")] * 2
        xsel = A.t(BF16, 8, 512)
        B_xsel = Buf("xsel")
        csel = A.t(BF16, 4, 48)
        B_csel = Buf("csel")
        csT = A.t(BF16, 512)
        B_csT = Buf("csT")
        gmain = [A.t(F32, 2, 512) for _ in range(2)]
        B_gmain = [Buf("gm0"), Buf("gm1")]
        Qa = [A.t(BF16, 4, 512) for _ in range(2)]
        B_Qa = [Buf("Qa0"), Buf("Qa1")]
        kbuf = [A.t(BF16, 4, 512) for _ in range(2)]
        B_kbuf = [Buf("kb0"), Buf("kb1")]
        vbuf = [A.t(BF16, 4, 4, 128) for _ in range(2)]
        B_vbuf = [Buf("vb0"), Buf("vb1")]
        PTb = [A.t(BF16, 512) for _ in range(3)]
        B_PTb = [Buf(f"ptb{i}") for i in range(3)]
        gsb = [A.t(F32, 512) for _ in range(2)]
        B_gsb = [Buf("gsbB0"), Buf("gsbB1")]
        qm = A.t(BF16, 4, 512)
        B_qm = Buf("qmB")
        PT = PTb[0:2]
        B_PT = B_PTb[0:2]
        rd = A.t(F32, 512)
        t1 = A.t(F32, 512)
        B_rd, B_t1 = Buf("rdB"), Buf("t1B")
        rden = [A.t(F32, 512) for _ in range(2)]
        B_rden = [Buf("rden0"), Buf("rden1")]
        gated = A.t(BF16, 12, 512)
        B_gated = Buf("gatedB")
        xfA = [A.t(F32, D)] * 2
        xfB = [A.t(F32, D)] * 2
        B_xfA = [Buf("xfA0")] * 2
        B_xfB = [Buf("xfB0")] * 2
        zz = [A.t(F32, D) for _ in range(2)]
        zn = [A.t(F32, D) for _ in range(2)]
        B_zz = [Buf("zB0"), Buf("zB1")]
        B_zn = [Buf("znB0"), Buf("znB1")]
        junk, B_junk = None, None
        tmpn = [rd, t1, gsb[0], gsb[1]]
        B_tmpn = [B_rd, B_t1, B_gsb[0], B_gsb[1]]
        B_out = Buf("out")
        for i in range(2):
            P.op("pool", I("memset", kbuf[i][64:70, :, :], 1.0), w=[B_kbuf[i]])
            P.op("pool", I("memset", Qa[i][64:70, :, :], 1.0), w=[B_Qa[i]])

        Obk = banks[0:4]
        Sbk = banks[4:7]
        Mbk = banks[7]
        kv_rr = [0]
        s_rr = [0]
        pt_rr = [0]
        gs_rr = [0]
        xs_rr = [0]
        def do_select(g0, g1):
                for pc in range(4):
                    q = xs_rr[0] % 2
                    xs_rr[0] += 1
                    pq_load(xst[q][:, 0, :, :], X1T[2 * pc:2 * pc + 2, :, g0 * 512:(g0 + 1) * 512].rearrange("c p t -> p c t"),
                            [B_xst[q]], "xst")
                    P.op("pool", I("dma_start",
                        out=xst[q][:, 1, :, :], in_=X1T[2 * pc:2 * pc + 2, :, g1 * 512:(g1 + 1) * 512].rearrange("c p t -> p c t")),
                        wj=[B_xst[q]], dma=True, key="xst")
                    P.op("dve", I("tensor_scalar", out=xsel[:, 2 * pc:2 * pc + 2, :], in0=xst[q][:, 0, :, :],
                                                                     scalar1=selt[:, 0:1], scalar2=None, op0=ALU.mult),
                         r=[B_xst[q], B_const], w=[B_xsel] if pc == 0 else [], wj=[] if pc == 0 else [B_xsel])
                    P.op("dve", I("scalar_tensor_tensor",
                        out=xsel[:, 2 * pc:2 * pc + 2, :], in0=xst[q][:, 1, :, :], scalar=selt[:, 1:2],
                        in1=xsel[:, 2 * pc:2 * pc + 2, :], op0=ALU.mult, op1=ALU.add),
                        r=[B_xst[q], B_const], wj=[B_xsel])
                P.op("dve", I("tensor_scalar", out=csel, in0=CS[:, g0 * 4:(g0 + 1) * 4, :], scalar1=selt[:, 0:1],
                                                             scalar2=None, op0=ALU.mult),
                     r=B_CS[g0 * 4:(g0 + 1) * 4] + [B_const], w=[B_csel])
                P.op("dve", I("scalar_tensor_tensor", out=csel, in0=CS[:, g1 * 4:(g1 + 1) * 4, :],
                                                                    scalar=selt[:, 1:2], in1=csel, op0=ALU.mult, op1=ALU.add),
                     r=B_CS[g1 * 4:(g1 + 1) * 4] + [B_const], wj=[B_csel])

        slots = slot_tiles(NT512)
        def proj_items(G):
            gq = G % 2
            items = []
            for oc in range(2):
                def it(bkf, oc=oc):
                    gb = bkf()
                    proj(gb, win, B_win, DMIX + (2 * G + oc) * 128, 128, xsel, B_xsel, 512)
                    P.op("act", I("activation", out=gmain[gq][:, oc, :], in_=gb.f[:, :], func=AF.Silu),
                         r=[gb.buf], w=[B_gmain[gq]] if oc == 0 else [], wj=[] if oc == 0 else [B_gmain[gq]])
                items.append(it)
            for hl in range(4):
                def it(bkf, hl=hl):
                    h = 4 * G + hl
                    qb = bkf()

                    def efirst(bk):
                        P.op("pe", I("matmul", bk.f[0:67, :], Emat[0:48, h, :], csT[0:48, :], start=True, stop=False),
                             r=[B_const, B_csT], w=[bk.buf])
                    proj(qb, win, B_win, h * 64, 64, xsel, B_xsel, 512, extra_first=efirst)
                    if hl % 2 == 0:
                        P.op("act", I("activation", out=Qa[gq][0:67, hl, :], in_=qb.f[0:67, :], func=AF.Copy),
                             r=[qb.buf], w=[B_Qa[gq]] if hl == 0 else [], wj=[] if hl == 0 else [B_Qa[gq]])
                    else:
                        P.op("dve", I("tensor_copy", out=Qa[gq][0:67, hl, :], in_=qb.f[0:67, :]),
                             r=[qb.buf], wj=[B_Qa[gq]])
                items.append(it)
            return items

        def pre_A(j, g0, g1):
            sp_load(masks, masks_d[j % 2].rearrange("p (a b c) -> p a b c", a=2, b=4), [B_masks], "masks")
            if j == 0:
                do_select(g0, g1)
            cbk = nb()
            for b in range(4):
                P.op("pe", I("transpose", out=cbk.h[0:48, b * 128:(b + 1) * 128], in_=csel[:, b, :],
                                                               identity=ident), r=[B_csel, B_const], w=[cbk.buf])
            P.op("dve", I("tensor_copy", out=csT[0:48, :], in_=cbk.h[0:48, 0:512]), r=[cbk.buf], w=[B_csT])

            for h in range(4):
                bk = nb()
                proj(bk, win, B_win, D + h * 128, 128, xsel, B_xsel, 512)
                P.op("act", I("activation", out=qm[:, h, :], in_=bk.f[:, :], func=AF.Copy,
                                                               scale=128.0 ** -0.5),
                     r=[bk.buf], w=[B_qm] if h == 0 else [], wj=[] if h == 0 else [B_qm])
            for it in proj_items(0):
                it(nb)

        def pre_B(j):
            for h in range(4):
                ybk, dbk = nb(), nb()
                mem_attn(h, qm, B_qm, 512, PT, B_PT, ybk, dbk)
                gb = nb()
                proj(gb, win, B_win, DMIX + D + h * 128, 128, xsel, B_xsel, 512)
                gi = gs_rr[0] % 2
                gs_rr[0] += 1
                P.op("act", I("activation", out=gsb[gi], in_=gb.f[:, :], func=AF.Silu),
                     r=[gb.buf], w=[B_gsb[gi]])
                P.op("dve", I("reciprocal", out=rd, in_=dbk.f[:, :]), r=[dbk.buf], w=[B_rd])
                P.op("dve", I("tensor_tensor", out=t1, in0=ybk.f[:, :], in1=rd, op=ALU.mult),
                     r=[ybk.buf, B_rd], w=[B_t1])
                P.op("dve", I("tensor_tensor", out=gated[:, 8 + h, :], in0=t1, in1=gsb[gi], op=ALU.mult),
                     r=[B_t1, B_gsb[gi]], w=[B_gated] if h == 0 else [], wj=[] if h == 0 else [B_gated])

        def attention(j, nch):
            norm_pending = []
            for G in range(4):
                gq = G % 2
                nxt = proj_items(G + 1) if G < 3 else []
                if G == 3 and j + 1 < len(slots):
                    do_select(slots[j + 1][0], slots[j + 1][1])
                units = [(kc, kb, hl) for kc in range(nch) for kb in range(4) for hl in range(4)]
                pend = []
                kq = 0
                for idx, (kc, kb, hl) in enumerate(units):
                    if norm_pending and idx in (2, 4):
                        norm_pending.pop(0)()
                    h = 4 * G + hl
                    n = kc * 4 + kb
                    mtype = kc - (nch - 2)
                    if kb == 0 and hl == 0:
                        kq = kv_rr[0] % 2
                        kv_rr[0] += 1
                        sp_load(kbuf[kq][0:64, :, :],
                                KT.rearrange("c (two d) t -> (c two) d t", two=2)[4 * G:4 * G + 4, :, kc * 512:(kc + 1) * 512]
                                .rearrange("h d t -> d h t"), [B_kbuf[kq]], f"kb{kq}", r=[B_KV[kc]])
                        P.op("sp", I("dma_start", out=kbuf[kq][67:70, :, :],
                                     in_=KC.rearrange("(h i) t -> i h t", i=3)[:, 4 * G:4 * G + 4, kc * 512:(kc + 1) * 512]),
                             r=[B_KV[kc]], wj=[B_kbuf[kq]], dma=True, key=f"kb{kq}")
                        sp_load(vbuf[kq], VA[kc * 4:(kc + 1) * 4, :, 4 * G:4 * G + 4, :].rearrange("b p h e -> p b h e"),
                                [B_vbuf[kq]], f"vb{kq}", r=[B_KV[kc]])
                    sb = Sbk[s_rr[0] % 3]
                    s_rr[0] += 1
                    P.op("pe", I("matmul", sb.f[:, :], kbuf[kq][0:70, hl, kb * 128:(kb + 1) * 128], Qa[gq][0:70, hl, :],
                                 start=True, stop=(mtype < 0)), r=[B_kbuf[kq], B_Qa[gq]], w=[sb.buf])
                    if mtype >= 0:
                        P.op("pe", I("matmul", sb.f[:, :], ident, masks[:, mtype, kb, :], start=False, stop=True),
                             r=[B_const, B_masks], w=[sb.buf])
                    pi = pt_rr[0] % 3
                    pt_rr[0] += 1
                    P.op("act", I("activation", out=PTb[pi], in_=sb.f[:, :], func=AF.Exp, scale=0.125),
                         r=[sb.buf], w=[B_PTb[pi]])

                    def pv(hl=hl, kq=kq, kb=kb, pi=pi, first=(idx < 4), last=(idx >= len(units) - 4)):
                        P.op("pe", I("matmul", Obk[hl].f[:, :], vbuf[kq][:, kb, hl, :], PTb[pi], start=first, stop=last),
                             r=[B_vbuf[kq], B_PTb[pi]], w=[Obk[hl].buf])
                    pend.append(pv)
                    if len(pend) > 2:
                        pend.pop(0)()
                    if nxt and idx >= 3 and idx % 2 == 1:
                        nxt.pop(0)(lambda: Mbk)
                for f in pend:
                    f()
                for it in nxt:
                    it(lambda: Mbk)
                def mk_norm(G, pair, gq):
                    def f():
                        hls = (2 * pair, 2 * pair + 1)
                        for hl in hls:
                            po = (hl % 2) * 64
                            P.op("dve", I("tensor_tensor", out=tmpn[hl][0:64, :], in0=Obk[hl].f[0:64, :],
                                          in1=gmain[gq][po:po + 64, hl // 2, :], op=ALU.mult),
                                 r=[Obk[hl].buf, B_gmain[gq]], w=[B_tmpn[hl]])
                        for hl in hls:
                            P.op("act", I("activation", out=rden[hl % 2][0:64, :], in_=Obk[hl].f[64:128, :], func=AF.Ln),
                                 r=[Obk[hl].buf], w=[B_rden[hl % 2]])
                        for hl in hls:
                            P.op("act", I("activation", out=rden[hl % 2][0:64, :], in_=rden[hl % 2][0:64, :], func=AF.Exp,
                                          scale=-1.0), r=[B_rden[hl % 2]], w=[B_rden[hl % 2]])
                        for hl in hls:
                            h = 4 * G + hl
                            po = (hl % 2) * 64
                            P.op("dve", I("tensor_tensor", out=gated[po:po + 64, h // 2, :], in0=tmpn[hl][0:64, :],
                                          in1=rden[hl % 2][0:64, :], op=ALU.mult), r=[B_tmpn[hl], B_rden[hl % 2]],
                                 wj=[B_gated])
                    return f
                norm_pending = [mk_norm(G, pair, gq) for pair in range(2)]
                if G == 3:
                    for f in norm_pending:
                        f()
                    norm_pending = []

        def post(j, g0, g1):
            for b in (0, 1, -1, 2, 3, -2):
                if b < 0:
                    for b2 in ((0, 1) if b == -1 else (2, 3)):
                        q = b2 % 2
                        row0 = j * 512 + b2 * 128
                        ln_block(None, None, [(xfA[q], B_xfA[q], am[:, 0:1]), (xfB[q], B_xfB[q], am[:, 1:2])], None,
                                 zz[q], B_zz[q], zn[q], B_zn[q], junk, B_junk, q, None, None,
                                 out_d[row0:row0 + 128, :], B_out, f"outst{q}", B_dst_join=True, st_eng="pool",
                                 part="rest", ln_act=True)
                    continue
                q = b % 2
                pq_load(xfA[q], X1F[g0 * 512 + b * 128:g0 * 512 + (b + 1) * 128, :], [B_xfA[q]], "xfA", r=[B_X1F[g0]])
                pq_load(xfB[q], X1F[g1 * 512 + b * 128:g1 * 512 + (b + 1) * 128, :], [B_xfB[q]], "xfB", r=[B_X1F[g1]])
                yb = [nb(), nb()]
                for hf in range(2):
                    for k in range(12):
                        P.op("pe", I("matmul",
                            yb[hf].f[:, :], gated[:, k, b * 128:(b + 1) * 128], wout[:, k, hf * 512:(hf + 1) * 512],
                            start=(k == 0), stop=(k == 11)), r=[B_gated, B_wout], w=[yb[hf].buf])
                row0 = j * 512 + b * 128
                ln_block(yb[0], yb[1], [(xfA[q], B_xfA[q], am[:, 0:1]), (xfB[q], B_xfB[q], am[:, 1:2])], None,
                         zz[q], B_zz[q], zn[q], B_zn[q], junk, B_junk, q, None, None,
                         out_d[row0:row0 + 128, :], B_out, f"outst{q}", B_dst_join=True, st_eng="pool", part="z", ln_act=True)
        for j, (g0, g1, nch) in enumerate(slots):
            if j == 0:
                pre_A(0, g0, g1)
                pre_B(0)
            attention(j, nch)
            if j + 1 < len(slots):
                pre_A(j + 1, slots[j + 1][0], slots[j + 1][1])
            post(j, g0, g1)
            if j + 1 < len(slots):
                pre_B(j + 1)
        A.release(base_mark)

    P.emit()
    return nc


def _consts():
    bf = ml_dtypes.bfloat16
    cb = np.zeros((128, 3 * 128 + 16 * 67), np.float32)
    cb[:, 0:128] = np.eye(128)
    cb[:, 128:256] = 1.0
    E = np.zeros((128, 16, 67), np.float32)
    for h in range(16):
        for i in range(3):
            E[h * 3 + i, h, 64 + i] = 1.0
    cb[:, 384:] = E.reshape(128, -1)
    cf = np.zeros((128, 384), np.float32)
    s = np.arange(128)
    cf[:, 0:128] = (s[:, None] <= s[None, :]).astype(np.float32)
    cf[:, 128:256] = 1.0
    invc = np.zeros((4, 16), np.float32)
    for g, w in enumerate(POOLW):
        invc[g] = 1.0 / np.minimum(np.arange(16) + 1, w)
    cf[:, 256:320] = invc.reshape(1, 64)
    cf[:, 320:384] = 1.0
    return cb.astype(bf), cf


def _masks(hh):
    bf = ml_dtypes.bfloat16
    s = np.arange(128)[:, None, None]
    kb = np.arange(4)[None, :, None]
    t = np.arange(512)[None, None, :]
    diag = np.where(kb * 128 + s > t, MASKNEG, 0.0).astype(np.float32)
    full = np.full((128, 4, 512), MASKNEG, np.float32)
    none = np.zeros((128, 4, 512), np.float32)
    a = np.stack([diag, full], axis=1).reshape(128, -1)
    b = np.stack([none, diag], axis=1).reshape(128, -1)
    m = np.stack([a, b] if hh == 0 else [b, a], axis=0)
    return m.astype(bf)


def make_in_maps(inputs, S):
    cb, cf = _consts()
    x = np.asarray(inputs["x"], np.float32)
    B = x.shape[0]
    lnp = np.ascontiguousarray(np.stack([inputs["ln_g"], inputs["ln_b"]], axis=1).astype(np.float32))
    pscale = np.ascontiguousarray(np.asarray(inputs["pool_scale"], np.float32).reshape(8, 128).T)
    maps = []
    for c in range(2 * B):
        b, hh = c // 2, c % 2
        sel = np.zeros((128, 2), np.float32)
        sel[:, hh] = 1.0
        maps.append({
            "x": np.ascontiguousarray(x[b]),
            "mem": np.ascontiguousarray(np.asarray(inputs["mem"], np.float32)[b]),
            "w_in": np.asarray(inputs["w_in"], np.float32),
            "w_mem_kv": np.asarray(inputs["w_mem_kv"], np.float32),
            "w_out": np.asarray(inputs["w_out"], np.float32),
            "pool_w": np.ascontiguousarray(np.asarray(inputs["pool_w"], np.float32)[0]),
            "w_kv": np.asarray(inputs["w_kv_shared"], np.float32),
            "lnp": lnp,
            "pscale": pscale,
            "bfor": np.asarray(inputs["b_forget"], np.float32),
            "cb16": cb,
            "cf32": cf,
            "sel": sel,
            "masks": _masks(hh),
        })
    return maps


_NC_CACHE = {}


def kernel(x, mem, w_in, w_mem_kv, w_out, ln_g, ln_b, pool_w, pool_scale, w_kv_shared, b_forget):
    inputs = dict(x=x, mem=mem, w_in=w_in, w_mem_kv=w_mem_kv, w_out=w_out, ln_g=ln_g, ln_b=ln_b,
                  pool_w=pool_w, pool_scale=pool_scale, w_kv_shared=w_kv_shared, b_forget=b_forget)
    B, S, _ = np.asarray(x).shape
    if S not in _NC_CACHE:
        _NC_CACHE[S] = build(S)
    nc = _NC_CACHE[S]
    maps = make_in_maps(inputs, S)
    res = run_bass_kernel_spmd(nc, maps, core_ids=list(range(2 * B)))
    out = np.zeros((B, S, D), np.float32)
    st = slot_tiles(S // 512)
    for c in range(2 * B):
        b, hh = c // 2, c % 2
        o = np.asarray(res.results[c]["out"], np.float32)
        for j, (g0, g1, _) in enumerate(st):
            g = g0 if hh == 0 else g1
            out[b, g * 512:(g + 1) * 512, :] = o[j * 512:(j + 1) * 512, :]
    return out
```

```python
import numpy as np
import ml_dtypes
import concourse.bass as bass
import concourse.mybir as mybir
from concourse.bass_utils import run_bass_kernel_spmd

dt = mybir.dt
F32, BF16 = dt.float32, dt.bfloat16
AF = mybir.ActivationFunctionType
ALU = mybir.AluOpType

D = 1024
NMEM = 256
DIN = 3072
DMIX = 1536
NH = 16
DH = 64
ALPHA = 4.0 ** 0.25
EPS = 1e-5
POOLW = (2, 4, 8, 16)
MASKNEG = -30000.0

SAME_ENGINE_SYNC = True


class Buf:
    __slots__ = ("name", "w", "r", "jd")

    def __init__(self, name):
        self.name = name
        self.w = []
        self.r = []
        self.jd = []


class Op:
    __slots__ = ("eng", "fn", "deps", "sig", "sem", "val", "dma", "key", "idx")


ENGS = ("pe", "act", "dve", "pool", "sp")


class Prog:
    def __init__(self, nc):
        self.nc = nc
        self.ops = {e: [] for e in ENGS}
        self.n = 0

    def op(self, eng, fn, r=(), w=(), wj=(), dma=False, key=None):
        o = Op()
        o.eng, o.fn, o.dma, o.key = eng, fn, dma, key
        o.sig = dma
        o.sem = None
        o.val = 0
        o.idx = self.n
        self.n += 1
        deps = []
        for b in r:
            deps += b.w
        for b in w:
            deps += b.w
            deps += b.r
        for b in wj:
            deps += b.jd
        for b in r:
            b.r.append(o)
        for b in w:
            b.jd = b.w + b.r
            b.w = [o]
            b.r = []
        for b in wj:
            b.w.append(o)
        seen = set()
        fd = []
        for d in deps:
            if id(d) in seen or d is o:
                continue
            seen.add(id(d))
            if (not d.dma) and d.eng == eng and (eng == "pe" or not SAME_ENGINE_SYNC):
                continue
            fd.append(d)
            d.sig = True
        o.deps = fd
        self.ops[eng].append(o)
        return o

    def barrier(self, bufs):
        b = Buf("barrier")
        lasts = []
        for e in ENGS:
            if self.ops[e]:
                lasts.append(self.ops[e][-1])
        for x in bufs:
            lasts += x.w + x.r
        for e in ENGS:
            o = Op()
            o.eng, o.fn, o.dma, o.key = e, None, False, None
            o.sig = False
            o.sem = None
            o.val = 0
            o.idx = self.n
            self.n += 1
            o.deps = []
            seen = set()
            for d in lasts:
                if id(d) in seen:
                    continue
                seen.add(id(d))
                if (not d.dma) and d.eng == e:
                    continue
                o.deps.append(d)
                d.sig = True
            self.ops[e].append(o)

    def emit(self):
        nc = self.nc
        engsem = {e: nc.alloc_semaphore("s_" + e) for e in ("pe", "act", "dve", "pool")}
        keysem = {}
        cnt = {}
        for e in ENGS:
            for o in self.ops[e]:
                if o.dma:
                    k = o.key
                    if k not in keysem:
                        keysem[k] = nc.alloc_semaphore("d_" + k)
                        cnt[k] = 0
                    cnt[k] += 16
                    o.sem, o.val = keysem[k], cnt[k]
                elif o.sig:
                    cnt[e] = cnt.get(e, 0) + 1
                    o.sem, o.val = engsem[e], cnt[e]
        finals = [(keysem[k], cnt[k]) for k in keysem]
        ops = self.ops

        def mk(e):
            def body(eng):
                wm = {}
                for o in ops[e]:
                    need = {}
                    for d in o.deps:
                        k = d.sem.num
                        if wm.get(k, 0) < d.val and need.get(k, (None, 0))[1] < d.val:
                            need[k] = (d.sem, d.val)
                    for k, (s, v) in need.items():
                        eng.wait_ge(s, v)
                        wm[k] = v
                    if o.fn is None:
                        continue
                    inst = o.fn(eng)
                    if o.sig:
                        inst.then_inc(o.sem, 16 if o.dma else 1)
                if e == "sp":
                    for s, v in finals:
                        eng.wait_ge(s, v)
            return body

        with nc.Block() as blk:
            blk.tensor(mk("pe"))
            blk.scalar(mk("act"))
            blk.vector(mk("dve"))
            blk.gpsimd(mk("pool"))
            blk.sync(mk("sp"))


class Arena:
    def __init__(self, nc, nbytes):
        self.h32 = nc.alloc_sbuf_tensor("arena", [128, nbytes // 4], F32)
        self.h16 = self.h32.bitcast(BF16)
        self.top = 0
        self.cap = nbytes

    def mark(self):
        return self.top

    def release(self, m):
        self.top = m

    def t(self, dtype, *shape):
        n = 1
        for s in shape:
            n *= s
        nb = n * (4 if dtype == F32 else 2)
        nb = (nb + 63) // 64 * 64
        off = self.top
        self.top += nb
        assert self.top <= self.cap, f"SBUF arena overflow {self.top} > {self.cap}"
        if dtype == F32:
            ap = self.h32[:, off // 4: off // 4 + n]
        else:
            ap = self.h16[:, off // 2: off // 2 + n]
        if len(shape) == 2:
            ap = ap.rearrange("p (a b) -> p a b", a=shape[0])
        elif len(shape) == 3:
            ap = ap.rearrange("p (a b c) -> p a b c", a=shape[0], b=shape[1])
        return ap


def own_tiles(hh, n512):
    return [g for g in range(n512) if ((g % 4) in (0, 3)) == (hh == 0)]


def I(name, *a, **k):
    return lambda e: getattr(e, name)(*a, **k)


class Bank:
    def __init__(self, P32, P16, k):
        self.buf = Buf(f"bank{k}")
        self.f = P32[:, k * 512:(k + 1) * 512]
        self.h = P16[:, k * 1024:(k + 1) * 1024]


def slot_tiles(n512):
    res = []
    for j in range(n512 // 2):
        p, e = j // 2, j % 2
        g0 = 4 * p + (0, 3)[e]
        g1 = 4 * p + (1, 2)[e]
        res.append((g0, g1, max(g0, g1) + 1))
    return res


def build(S, phases=("a1", "a2", "b"), debug=False, TA=256):
    assert S % 2048 == 0
    NT512 = S // 512
    NBLK = S // 128
    NSLOT = NT512 // 2
    nc = bass.Bass("TRN2", target_bir_lowering=False)
    P = Prog(nc)

    def din(name, shape, dtype=F32):
        return nc.dram_tensor(name, list(shape), dtype, kind="ExternalInput").ap()

    x_d = din("x", [S, D])
    mem_d = din("mem", [NMEM, D])
    w_in_d = din("w_in", [2, D, DIN])
    w_mkv_d = din("w_mem_kv", [2, D, D])
    w_out_d = din("w_out", [2, DMIX, D])
    pool_w_d = din("pool_w", [4, 256, 256])
    w_kv_d = din("w_kv", [D, 2064])
    lnp_d = din("lnp", [2, 2, D])
    pscale_d = din("pscale", [128, 8])
    bfor_d = din("bfor", [16])
    cb16_d = din("cb16", [128, 3 * 128 + 16 * 67], BF16)
    cf32_d = din("cf32", [128, 128 + 128 + 64 + 64], F32)
    sel_d = din("sel", [128, 2])
    masks_d = din("masks", [2, 128, 2 * 4 * 512], BF16)
    out_d = nc.dram_tensor("out", [NSLOT * 512, D], F32, kind="ExternalOutput").ap()

    kind_scr = "ExternalOutput" if debug else "Internal"
    X1F = nc.dram_tensor("x1f_scr", [S, D], F32, kind=kind_scr).ap()
    X1T = nc.dram_tensor("x1t_scr", [8, 128, S], BF16, kind=kind_scr).ap()
    KT = nc.dram_tensor("kt_scr", [8, 128, S], BF16, kind=kind_scr).ap()
    VA = nc.dram_tensor("va_scr", [NBLK, 128, NH, 128], BF16, kind=kind_scr).ap()
    KC = nc.dram_tensor("kc_scr", [NH * 3, S], BF16, kind=kind_scr).ap()
    if debug:
        CK_d = nc.dram_tensor("ck_dbg", [128, NBLK * 16], F32, kind="ExternalOutput").ap()

    A = Arena(nc, 212736)
    P32 = nc.alloc_psum_tensor("ps", [128, 4096], F32)
    P16 = P32.bitcast(BF16)
    banks = [Bank(P32, P16, k) for k in range(8)]
    bank_rr = [0]

    def nb():
        b = banks[bank_rr[0] % 8]
        bank_rr[0] += 1
        return b

    cb16 = A.t(BF16, 3 * 128 + 16 * 67)
    ident = cb16[:, 0:128]
    ones16 = cb16[:, 128:256]
    Emat = cb16[:, 384:384 + 16 * 67].rearrange("p (h m) -> p h m", h=16)
    cf32 = A.t(F32, 384)
    tri = cf32[:, 0:128]
    allones = cf32[:, 128:256]
    invc = cf32[:, 256:320].rearrange("p (w t) -> p w t", w=4)
    ones64 = cf32[:, 320:384]
    pscale = A.t(F32, 8)
    bfor = A.t(F32, 16)
    selt = A.t(F32, 2)
    CK = A.t(F32, NBLK, 16)
    CS = A.t(BF16, NBLK, 48)
    Tn = A.t(F32, 16)
    lng = A.t(F32, D)
    lnb = A.t(F32, D)
    mkT = A.t(BF16, 4, NMEM)
    mv = A.t(BF16, 2, 512)
    small = A.t(F32, 64)

    B_const = Buf("const")
    B_ln = Buf("lnp")
    B_mkv = Buf("mkv")
    B_CK = [Buf(f"ck{n}") for n in range(NBLK)]
    B_CS = [Buf(f"cs{n}") for n in range(NBLK)]
    B_Tn = Buf("Tn")
    B_X1F = [Buf(f"x1f{g}") for g in range(NT512)]
    B_X1T = [Buf(f"x1t{g}") for g in range(NT512)]
    B_KV = [Buf(f"kv{g}") for g in range(NT512)]

    def sp_load(out, in_, w, key, r=()):
        return P.op("sp", I("dma_start", out=out, in_=in_), r=r, w=w, dma=True, key=key)

    def pq_load(out, in_, w, key, r=(), wj=()):
        return P.op("pool", I("dma_start", out=out, in_=in_), r=r, w=w, wj=wj, dma=True, key=key)

    def cast_load(out, in_, w, key, r=(), wj=()):
        return P.op("pool", I("dma_start", out=out, in_=in_), r=r, w=w, wj=wj, dma=True, key=key)

    sp_load(cb16, cb16_d, [B_const], "c0")
    P.op("sp", I("dma_start", out=cf32, in_=cf32_d), wj=[B_const], dma=True, key="c0")
    P.op("sp", I("dma_start", out=pscale, in_=pscale_d), wj=[B_const], dma=True, key="c0")
    P.op("sp", I("dma_start", out=bfor, in_=bfor_d.partition_broadcast(128)), wj=[B_const], dma=True, key="c0")
    P.op("sp", I("dma_start", out=selt, in_=sel_d), wj=[B_const], dma=True, key="c0")

    def load_ln(layer):
        sp_load(lng, lnp_d[layer, 0].partition_broadcast(128), [B_ln], "ln")
        P.op("sp", I("dma_start", out=lnb, in_=lnp_d[layer, 1].partition_broadcast(128)),
             wj=[B_ln], dma=True, key="ln")

    def load_w(dst, src_rows_by_cols, nk, ncols, buf, key, first=True):
        srcv = src_rows_by_cols.rearrange("(k p) n -> p k n", p=128)
        for k in range(nk):
            for c0 in range(0, ncols, 1024):
                step = min(1024, ncols - c0)
                o = dst[:, k, c0:c0 + step]
                i = srcv[:, k, c0:c0 + step]
                if first:
                    cast_load(o, i, [buf], key)
                    first = False
                else:
                    cast_load(o, i, [], key, wj=[buf])

    def transposes_to(dstT, src_tok, nblk, r_src, w_dst, alt, all_act=False):
        for c in range(8):
            bk = nb()
            for b in range(nblk):
                P.op("pe", I("transpose",
                    out=bk.h[:, b * 128:(b + 1) * 128], in_=src_tok[:, b, c * 128:(c + 1) * 128],
                    identity=ident), r=[r_src[b] if isinstance(r_src, list) else r_src, B_const], w=[bk.buf])
            n = nblk * 128
            if (c + alt) % 2 == 0 and not all_act:
                P.op("dve", I("tensor_copy", out=dstT[:, c, 0:n], in_=bk.h[:, 0:n]),
                     r=[bk.buf], w=[] if c else [w_dst], wj=[w_dst] if c else [])
            else:
                P.op("act", I("activation", out=dstT[:, c, 0:n], in_=bk.h[:, 0:n], func=AF.Copy),
                     r=[bk.buf], w=[] if c else [w_dst], wj=[w_dst] if c else [])

    def proj(bk, W, wbuf, col0, m, xT, xbuf, n, extra_first=None):
        first = True
        if extra_first is not None:
            extra_first(bk)
            first = False
        for k in range(8):
            P.op("pe", I("matmul",
                bk.f[0:m, 0:n], W[:, k, col0:col0 + m], xT[:, k, 0:n], start=first, stop=(k == 7)),
                r=[wbuf, xbuf], w=[bk.buf])
            first = False

    def mem_kv(layer, wtmp, B_wtmp, memb, B_memb, memT, B_memT):
        load_w(wtmp, w_mkv_d[layer], 8, D, B_wtmp, "wtmp")
        cast_load(memb, mem_d.rearrange("(b p) d -> p b d", p=128), [B_memb], "memb")
        transposes_to(memT, memb, 2, B_memb, B_memT, 0)
        for h in range(4):
            bk = nb()
            proj(bk, wtmp, B_wtmp, h * 128, 128, memT, B_memT, NMEM)
            P.op("act", I("activation", out=mkT[:, h, :], in_=bk.f[:, 0:NMEM], func=AF.Copy),
                 r=[bk.buf], w=[B_mkv] if h == 0 else [], wj=[] if h == 0 else [B_mkv])
        for j in range(2):
            bk = nb()
            for k in range(8):
                P.op("pe", I("matmul",
                    bk.f[:, :], memT[:, k, j * 128:(j + 1) * 128], wtmp[:, k, 512:1024],
                    start=(k == 0), stop=(k == 7)), r=[B_wtmp, B_memT], w=[bk.buf])
            P.op("dve", I("tensor_copy", out=mv[:, j, :], in_=bk.f[:, :]),
                 r=[bk.buf], wj=[B_mkv])

    def mem_attn(h, qm, B_qm, n, PT, B_PT, ybk, dbk):
        for j in range(2):
            lb = nb()
            P.op("pe", I("matmul",
                lb.f[:, 0:n], mkT[:, h, j * 128:(j + 1) * 128], qm[:, h, 0:n], start=True, stop=True),
                r=[B_mkv, B_qm], w=[lb.buf])
            P.op("act", I("activation", out=PT[j][:, 0:n], in_=lb.f[:, 0:n], func=AF.Exp),
                 r=[lb.buf], w=[B_PT[j]])
        for j in range(2):
            P.op("pe", I("matmul",
                ybk.f[:, 0:n], mv[:, j, h * 128:(h + 1) * 128], PT[j][:, 0:n], start=(j == 0), stop=(j == 1)),
                r=[B_mkv, B_PT[j]], w=[ybk.buf])
        for j in range(2):
            P.op("pe", I("matmul",
                dbk.f[:, 0:n], ones16, PT[j][:, 0:n], start=(j == 0), stop=(j == 1)),
                r=[B_const, B_PT[j]], w=[dbk.buf])

    def ln_block(ybk0, ybk1, xres, B_xres, z, B_z, zn, B_zn, junk, B_junk, slot, x1b, B_x1b,
                 dst_dram, B_dst, key, B_dst_join=False, res_scale=ALPHA, st_eng="sp", ln_act=False, part="all"):
        sm = small[:, slot * 8:(slot + 1) * 8]
        B_sm = B_small[slot]
        if not isinstance(xres, list):
            xres_l = [(xres, B_xres, float(res_scale))]
        else:
            xres_l = xres
        for hf, yb in enumerate((ybk0, ybk1) if part in ("all", "z") else ()):
            for ci, (xr, Bxr, sc) in enumerate(xres_l):
                last = ci == len(xres_l) - 1
                firstw = (hf == 0 and ci == 0)
                kw = dict(accum_out=sm[:, hf:hf + 1]) if last else {}
                in1 = yb.f[:, :] if ci == 0 else z[:, hf * 512:(hf + 1) * 512]
                rr = [Bxr, yb.buf] if ci == 0 else [Bxr, B_z]
                if not isinstance(sc, float):
                    rr = rr + [B_const]
                P.op("dve", I("scalar_tensor_tensor",
                    out=z[:, hf * 512:(hf + 1) * 512], in0=xr[:, hf * 512:(hf + 1) * 512], scalar=sc,
                    in1=in1, op0=ALU.mult, op1=ALU.add, **kw),
                    r=rr, w=[B_z, B_sm] if firstw else [], wj=[] if firstw else [B_z, B_sm])
        if part == "z":
            return
        if ln_act:
            P.op("act", I("activation", out=zn, in_=z, func=AF.Square, accum_out=sm[:, 2:3]), r=[B_z], w=[B_zn], wj=[B_sm])
        else:
            P.op("dve", I("scalar_tensor_tensor", out=zn, in0=z, scalar=1.0, in1=z, op0=ALU.mult, op1=ALU.mult,
                          accum_out=sm[:, 2:3]), r=[B_z], w=[B_zn], wj=[B_sm])
        P.op("dve", I("tensor_scalar", out=sm[:, 3:4], in0=sm[:, 0:1], scalar1=sm[:, 1:2], scalar2=1.0 / D,
                                              op0=ALU.add, op1=ALU.mult), r=[B_sm], wj=[B_sm])
        P.op("dve", I("tensor_tensor", out=sm[:, 4:5], in0=sm[:, 3:4], in1=sm[:, 3:4], op=ALU.mult),
             r=[B_sm], wj=[B_sm])
        P.op("dve", I("scalar_tensor_tensor", out=sm[:, 5:6], in0=sm[:, 2:3], scalar=1.0 / D, in1=sm[:, 4:5],
                                                     op0=ALU.mult, op1=ALU.subtract), r=[B_sm], wj=[B_sm])
        P.op("dve", I("tensor_scalar", out=sm[:, 5:6], in0=sm[:, 5:6], scalar1=EPS, scalar2=None, op0=ALU.add),
             r=[B_sm], wj=[B_sm])
        P.op("pool", I("tensor_tensor", out=sm[:, 6:7], in0=sm[:, 5:6], in1=epsc[:, 1:2], op=ALU.pow),
             r=[B_sm, B_const], wj=[B_sm])
        P.op("dve", I("scalar_tensor_tensor", out=sm[:, 7:8], in0=sm[:, 3:4], scalar=-1.0, in1=sm[:, 6:7],
                                                     op0=ALU.mult, op1=ALU.mult), r=[B_sm], wj=[B_sm])
        P.op("dve", I("tensor_scalar", out=zn, in0=z, scalar1=sm[:, 6:7], scalar2=sm[:, 7:8], op0=ALU.mult, op1=ALU.add),
             r=[B_z, B_sm], w=[B_zn])
        P.op("dve", I("tensor_tensor", out=z, in0=zn, in1=lng, op=ALU.mult), r=[B_zn, B_ln], w=[B_z])
        P.op("pool" if ln_act else "dve", I("tensor_tensor", out=zn, in0=z, in1=lnb, op=ALU.add), r=[B_z, B_ln], w=[B_zn])
        if x1b is not None:
            P.op("pool", I("tensor_copy", out=x1b, in_=zn), r=[B_zn], w=[B_x1b])
        if dst_dram is not None:
            if B_dst_join:
                P.op(st_eng, I("dma_start", out=dst_dram, in_=zn), r=[B_zn], wj=[B_dst], dma=True, key=key)
            else:
                P.op(st_eng, I("dma_start", out=dst_dram, in_=zn), r=[B_zn], w=[B_dst], dma=True, key=key)

    epsc = A.t(F32, 2)
    P.op("dve", I("memset", epsc[:, 0:1], EPS), wj=[B_const])
    P.op("dve", I("memset", epsc[:, 1:2], -0.5), wj=[B_const])
    B_small = [Buf(f"small{i}") for i in range(8)]
    P.op("dve", I("memset", Tn, 0.0), w=[B_Tn])

    base_mark = A.mark()

    if "a1" in phases:
        NB_A = TA // 128
        NTA = S // TA
        win = A.t(BF16, 8, DIN)
        wout = A.t(BF16, 12, D)
        poolw = A.t(BF16, 8, 256)
        B_win, B_wout, B_poolw = Buf("win"), Buf("wout"), Buf("poolw")
        m_tmp = A.mark()
        wtmp = A.t(BF16, 8, D)
        memb = A.t(BF16, 2, D)
        memT = A.t(BF16, 8, NMEM)
        B_wtmp, B_memb, B_memT = Buf("wtmp"), Buf("memb"), Buf("memT")
        load_ln(0)
        mem_kv(0, wtmp, B_wtmp, memb, B_memb, memT, B_memT)
        load_w(win, w_in_d[0], 8, DIN, B_win, "win")
        load_w(poolw, pool_w_d.rearrange("g r c -> (g r) c"), 8, 256, B_poolw, "poolw")
        load_w(wout, w_out_d[0], 12, D, B_wout, "wout")
        P.barrier([B_mkv])
        A.release(m_tmp)

        xb = [A.t(BF16, NB_A, D)] * 2
        B_xb = [Buf("xb0")] * 2
        xc = A.t(F32, NB_A, D)
        B_xc = Buf("xc")
        xf = [A.t(F32, D) for _ in range(2)]
        B_xf = [Buf("xf0"), Buf("xf1")]
        xT = [A.t(BF16, 8, TA) for _ in range(2)]
        B_xT = [Buf("xT0"), Buf("xT1")]
        HW_ = TA + 16
        U = [A.t(F32, 2, HW_) for _ in range(4)]
        B_U = [Buf(f"U{g}") for g in range(4)]
        T1 = A.t(F32, 2, HW_)
        T2 = A.t(F32, 2, HW_)
        B_T1, B_T2 = Buf("T1"), Buf("T2")
        Hh = A.t(F32, 8, 16)
        B_H = [Buf(f"H{c}") for c in range(8)]
        pm = [A.t(BF16, 2, TA) for _ in range(4)]
        B_pm = [Buf(f"pm{g}") for g in range(4)]
        gsb = A.t(F32, 12, TA)
        B_gsb = [Buf(f"gsb{i}") for i in range(12)]
        qm = A.t(BF16, 4, TA)
        B_qm = Buf("qm")
        PT8 = [[A.t(BF16, TA) for _ in range(2)] for _ in range(4)]
        B_PT8 = [[Buf(f"PT{h}{j}") for j in range(2)] for h in range(4)]
        rd = A.t(F32, TA)
        t1 = A.t(F32, TA)
        B_rd, B_t1 = Buf("rd"), Buf("t1")
        tfix = A.t(F32, 16)
        B_tfix = Buf("tfix")
        gated = A.t(BF16, 12, TA)
        B_gated = Buf("gated")
        zz = [A.t(F32, D) for _ in range(2)]
        zn = [A.t(F32, D) for _ in range(2)]
        B_zz = [Buf("z0"), Buf("z1")]
        B_zn = [Buf("zn0"), Buf("zn1")]
        junk, B_junk = None, None
        x1b = A.t(BF16, NB_A, D)
        B_x1b = [Buf(f"x1b{b}") for b in range(NB_A)]
        x1T = A.t(BF16, 8, TA)
        B_x1T = Buf("x1T")
        P.op("pool", I("memset", Hh, 0.0), w=B_H)

        def load_xc(i):
            if i < NTA:
                sp_load(xc, x_d[i * TA:(i + 1) * TA, :].rearrange("(b p) d -> p b d", p=128), [B_xc], "xc")

        def st_T(i):
            s = i % 2
            for b in range(NB_A):
                P.op("act", I("activation", out=xb[s][:, b, :], in_=xc[:, b, :], func=AF.Copy), r=[B_xc],
                     w=[B_xb[s]] if b == 0 else [], wj=[] if b == 0 else [B_xb[s]])
            transposes_to(xT[s], xb[s], NB_A, B_xb[s], B_xT[s], i, all_act=True)
            load_xc(i + 1)

        def st_U(i):
            s = i % 2
            for g in range(4):
                for ch in range(2):
                    c = 2 * g + ch
                    bk = nb()
                    proj(bk, win, B_win, c * 128, 128, xT[s], B_xT[s], TA)
                    P.op("pool", I("tensor_copy", out=U[g][:, ch, 0:16], in_=Hh[:, c, :]),
                         r=[B_H[c]], w=[B_U[g]] if ch == 0 else [], wj=[] if ch == 0 else [B_U[g]])
                    P.op("act", I("activation", out=U[g][:, ch, 16:16 + TA], in_=bk.f[:, 0:TA], func=AF.Copy),
                         r=[bk.buf], wj=[B_U[g]])
                    P.op("act", I("activation", out=Hh[:, c, :], in_=bk.f[:, TA - 16:TA], func=AF.Copy),
                         r=[bk.buf], w=[B_H[c]])

        def st_Upool(i):
            for g in range(4):
                src_, Bs = U[g], B_U[g]
                dsts = [(T1, B_T1), (T2, B_T2)]
                sh = 1
                for lvl in range(g + 1):
                    dstt, Bd = dsts[lvl % 2]
                    lo = 2 * sh - 1
                    P.op("dve", I("tensor_tensor", out=dstt[:, :, lo:HW_], in0=src_[:, :, lo:HW_],
                                  in1=src_[:, :, lo - sh:HW_ - sh], op=ALU.add), r=[Bs], w=[Bd])
                    src_, Bs = dstt, Bd
                    sh *= 2
                wv = POOLW[g]
                P.op("dve", I("scalar_tensor_tensor", out=pm[g][:, :, :], in0=src_[:, :, 16:HW_], scalar=1.0 / wv,
                              in1=U[g][:, :, 16:HW_], op0=ALU.mult, op1=ALU.subtract), r=[Bs, B_U[g]], w=[B_pm[g]])
                if i == 0:
                    for ch in range(2):
                        P.op("dve", I("tensor_tensor", out=tfix, in0=src_[:, ch, 16:32], in1=invc[:, g, :], op=ALU.mult),
                             r=[Bs, B_const], w=[B_tfix])
                        P.op("dve", I("tensor_tensor", out=pm[g][:, ch, 0:16], in0=tfix, in1=U[g][:, ch, 16:32],
                                      op=ALU.subtract), r=[B_tfix, B_U[g]], wj=[B_pm[g]])

        def st_S2(i):
            s = i % 2
            xTs, BxT = xT[s], B_xT[s]
            for h in range(4):
                bk = nb()
                proj(bk, win, B_win, D + h * 128, 128, xTs, BxT, TA)
                P.op("act", I("activation", out=qm[:, h, :], in_=bk.f[:, 0:TA], func=AF.Copy, scale=128.0 ** -0.5),
                     r=[bk.buf], w=[B_qm] if h == 0 else [], wj=[] if h == 0 else [B_qm])

            def gate(c):
                gb = nb()
                proj(gb, win, B_win, DMIX + c * 128, 128, xTs, BxT, TA)
                P.op("act", I("activation", out=gsb[:, c, :], in_=gb.f[:, 0:TA], func=AF.Silu),
                     r=[gb.buf], w=[B_gsb[c]])
            for c in range(4):
                gate(c)
            for h in range(4):
                for j in range(2):
                    lb = nb()
                    P.op("pe", I("matmul", lb.f[:, 0:TA], mkT[:, h, j * 128:(j + 1) * 128], qm[:, h, :], start=True, stop=True),
                         r=[B_mkv, B_qm], w=[lb.buf])
                    P.op("act", I("activation", out=PT8[h][j], in_=lb.f[:, 0:TA], func=AF.Exp),
                         r=[lb.buf], w=[B_PT8[h][j]])
            for c in range(4, 12):
                gate(c)
            ybks, dbks = [], []
            for h in range(4):
                ybk = nb()
                if h % 2 == 0:
                    dbk2 = nb()
                ybks.append(ybk)
                dbks.append((dbk2, (h % 2) * TA))
                for j in range(2):
                    P.op("pe", I("matmul", ybk.f[:, 0:TA], mv[:, j, h * 128:(h + 1) * 128], PT8[h][j], start=(j == 0),
                                 stop=(j == 1)), r=[B_mkv, B_PT8[h][j]], w=[ybk.buf])
                for j in range(2):
                    P.op("pe", I("matmul", dbk2.f[:, (h % 2) * TA:(h % 2) * TA + TA], ones16, PT8[h][j], start=(j == 0),
                                 stop=(j == 1)), r=[B_const, B_PT8[h][j]], w=[dbk2.buf])
            for h in range(4):
                ybk = ybks[h]
                dbk, doff = dbks[h]
                P.op("dve", I("reciprocal", out=rd, in_=dbk.f[:, doff:doff + TA]), r=[dbk.buf], w=[B_rd])
                P.op("dve", I("tensor_tensor", out=t1, in0=ybk.f[:, 0:TA], in1=rd, op=ALU.mult),
                     r=[ybk.buf, B_rd], w=[B_t1])
                P.op("dve", I("tensor_tensor", out=gated[:, 8 + h, :], in0=t1, in1=gsb[:, 8 + h, :], op=ALU.mult),
                     r=[B_t1, B_gsb[8 + h]], w=[B_gated] if h == 0 else [], wj=[] if h == 0 else [B_gated])

        def st_S3(i):
            for g in range(4):
                for oc in range(2):
                    c = 2 * g + oc
                    mb = nb()
                    for kc in range(2):
                        P.op("pe", I("matmul", mb.f[:, 0:TA], poolw[:, g * 2 + kc, oc * 128:(oc + 1) * 128], pm[g][:, kc, :],
                                     start=(kc == 0), stop=(kc == 1)), r=[B_poolw, B_pm[g]], w=[mb.buf])
                    P.op("dve", I("scalar_tensor_tensor", out=gated[:, c, :], in0=mb.f[:, 0:TA], scalar=pscale[:, c:c + 1],
                                  in1=gsb[:, c, :], op0=ALU.mult, op1=ALU.mult),
                         r=[mb.buf, B_gsb[c], B_const], wj=[B_gated])

        def st_S4(i):
            t0 = i * TA
            g512 = t0 // 512
            for b in range(NB_A):
                q = (i * NB_A + b) % 2
                sp_load(xf[q], x_d[t0 + b * 128:t0 + (b + 1) * 128, :], [B_xf[q]], f"xf{q}")
            for b in range(NB_A):
                q = (i * NB_A + b) % 2
                yb = [nb(), nb()]
                for hf in range(2):
                    for k in range(12):
                        P.op("pe", I("matmul", yb[hf].f[:, :], gated[:, k, b * 128:(b + 1) * 128],
                                     wout[:, k, hf * 512:(hf + 1) * 512], start=(k == 0), stop=(k == 11)),
                             r=[B_gated, B_wout], w=[yb[hf].buf])
                tok0 = t0 + b * 128
                ln_block(yb[0], yb[1], xf[q], B_xf[q], zz[q], B_zz[q], zn[q], B_zn[q], junk, B_junk, q,
                         x1b[:, b, :], B_x1b[b], X1F[tok0:tok0 + 128, :], B_X1F[g512], f"x1f{q}",
                         B_dst_join=(tok0 % 512 != 0), ln_act=True, part="z")
            for b in range(NB_A):
                q = (i * NB_A + b) % 2
                tok0 = t0 + b * 128
                ln_block(None, None, xf[q], B_xf[q], zz[q], B_zz[q], zn[q], B_zn[q], junk, B_junk, q,
                         x1b[:, b, :], B_x1b[b], X1F[tok0:tok0 + 128, :], B_X1F[g512], f"x1f{q}",
                         B_dst_join=(tok0 % 512 != 0), ln_act=True, part="rest")

        def st_X1(i):
            t0 = i * TA
            g512 = t0 // 512
            transposes_to(x1T, x1b, NB_A, B_x1b, B_x1T, i, all_act=True)
            if t0 % 512 == 0:
                P.op("sp", I("dma_start", out=X1T[:, :, t0:t0 + TA].rearrange("c p t -> p c t"), in_=x1T),
                     r=[B_x1T], w=[B_X1T[g512]], dma=True, key="x1t")
            else:
                P.op("sp", I("dma_start", out=X1T[:, :, t0:t0 + TA].rearrange("c p t -> p c t"), in_=x1T),
                     r=[B_x1T], wj=[B_X1T[g512]], dma=True, key="x1t")

        load_xc(0)
        st_T(0)
        st_U(0)
        st_Upool(0)
        for i in range(NTA):
            if i + 1 < NTA:
                st_T(i + 1)
            st_S2(i)
            st_S3(i)
            if i + 1 < NTA:
                st_U(i + 1)
            if i >= 1:
                st_X1(i - 1)
            st_S4(i)
            if i + 1 < NTA:
                st_Upool(i + 1)
        st_X1(NTA - 1)
        P.barrier(B_X1F + B_X1T)
        A.release(base_mark)

    pre_b = None
    if "b" in phases:
        winB = A.t(BF16, 8, DIN)
        woutB = A.t(BF16, 12, D)
        B_winB, B_woutB = Buf("win1"), Buf("wout1")
        a2_base = A.mark()
        m_tmpB = A.mark()
        wtmpB = A.t(BF16, 8, D)
        membB = A.t(BF16, 2, D)
        memTB = A.t(BF16, 8, NMEM)
        B_wtmpB, B_membB, B_memTB = Buf("wtmp1"), Buf("memb1"), Buf("memT1")
        pre_b = True
    else:
        a2_base = base_mark
    if "a2" in phases:
        wkv = A.t(BF16, 8, 2064)
        B_wkv = Buf("wkv")
        load_w(wkv, w_kv_d, 8, 2064, B_wkv, "wkv")
    if pre_b:
        load_ln(1)
        mem_kv(1, wtmpB, B_wtmpB, membB, B_membB, memTB, B_memTB)
        load_w(winB, w_in_d[1], 8, DIN, B_winB, "win")
        load_w(woutB, w_out_d[1], 12, D, B_woutB, "wout")
    if "a2" in phases:
        xt2 = [A.t(BF16, 8, 512) for _ in range(2)]
        B_xt2 = [Buf("xt2a"), Buf("xt2b")]
        ksb = A.t(BF16, 8, 512)
        B_ksb = Buf("ksb")
        kcsb = A.t(BF16, 512)
        B_kcsb = Buf("kcsb")
        vsb = A.t(BF16, 4, NH, 128)
        B_vsb = Buf("vsb")
        bfor4 = A.t(F32, 4, 16)
        fl = A.t(F32, 4, 16)
        e1 = A.t(F32, 4, 16)
        lp = A.t(F32, 4, 16)
        r1 = A.t(F32, 16)
        r2 = A.t(F32, 16)
        B_fl, B_e1, B_lp, B_r1, B_r2 = Buf("fl"), Buf("e1"), Buf("lp"), Buf("r1"), Buf("r2")
        B_bf4 = Buf("bf4")
        for b in range(4):
            P.op("pool", I("tensor_copy", out=bfor4[:, b, :], in_=bfor), r=[B_const],
                 w=[B_bf4] if b == 0 else [], wj=[] if b == 0 else [B_bf4])
        P.op("pool", I("memset", vsb[:, :, :, 64:128], 1.0), w=[B_vsb])
        sp_load(xt2[0], X1T[:, :, 0:512].rearrange("c p t -> p c t"), [B_xt2[0]], "xt2a", r=[B_X1T[0]])
        for g in range(NT512):
            s = g % 2
            if g + 1 < NT512:
                sp_load(xt2[1 - s], X1T[:, :, (g + 1) * 512:(g + 2) * 512].rearrange("c p t -> p c t"),
                        [B_xt2[1 - s]], "xt2b" if 1 - s else "xt2a", r=[B_X1T[g + 1]])
            xt = xt2[s]
            Bxt = B_xt2[s]
            fb = nb()
            for b in range(4):
                for k in range(8):
                    P.op("pe", I("matmul",
                        fb.f[:, b * 16:(b + 1) * 16], xt[:, k, b * 128:(b + 1) * 128], wkv[:, k, 2048:2064],
                        start=(k == 0), stop=(k == 7)), r=[Bxt, B_wkv], w=[fb.buf])
            P.op("dve", I("tensor_tensor", out=fl, in0=fb.f[:, 0:64].rearrange("p (b h) -> p b h", b=4),
                                                         in1=bfor4, op=ALU.add), r=[fb.buf, B_bf4], w=[B_fl])
            P.op("act", I("activation", out=e1, in_=fl, func=AF.Exp, scale=-1.0), r=[B_fl], w=[B_e1])
            P.op("act", I("activation", out=lp, in_=e1, func=AF.Ln, bias=1.0), r=[B_e1], w=[B_lp])
            for b in range(4):
                n = g * 4 + b
                cb = nb()
                P.op("pe", I("matmul", cb.f[:, 0:16], tri, lp[:, b, :], start=True, stop=True),
                     r=[B_const, B_lp], w=[cb.buf])
                P.op("pe", I("matmul", cb.f[:, 16:32], allones, lp[:, b, :], start=True, stop=True),
                     r=[B_const, B_lp], w=[cb.buf])
                P.op("dve", I("tensor_tensor", out=CK[:, n, :], in0=cb.f[:, 0:16], in1=Tn, op=ALU.add),
                     r=[cb.buf, B_Tn], w=[B_CK[n]])
                P.op("dve", I("tensor_tensor", out=Tn, in0=cb.f[:, 16:32], in1=Tn, op=ALU.add),
                     r=[cb.buf], w=[B_Tn])
                CSv = CS[:, n, :].rearrange("p (h i) -> p h i", i=3)
                P.op("pool", I("tensor_scalar", out=CSv[:, :, 0], in0=CK[:, n, :], scalar1=-8.0,
                                                                      scalar2=None, op0=ALU.mult),
                     r=[B_CK[n]], w=[B_CS[n]])
                P.op("dve", I("scalar_tensor_tensor", out=r1, in0=CK[:, n, :], scalar=-8.0,
                                                                           in1=CSv[:, :, 0], op0=ALU.mult,
                                                                           op1=ALU.subtract),
                     r=[B_CK[n], B_CS[n]], w=[B_r1])
                P.op("pool", I("tensor_copy", out=CSv[:, :, 1], in_=r1), r=[B_r1], wj=[B_CS[n]])
                P.op("pool", I("tensor_tensor", out=r2, in0=r1, in1=CSv[:, :, 1], op=ALU.subtract),
                     r=[B_r1, B_CS[n]], w=[B_r2])
                P.op("pool", I("tensor_copy", out=CSv[:, :, 2], in_=r2), r=[B_r2], wj=[B_CS[n]])
            for c in range(8):
                bk = nb()
                proj(bk, wkv, B_wkv, c * 128, 128, xt, Bxt, 512)
                if c % 2 == 0:
                    P.op("act", I("activation", out=ksb[:, c, :], in_=bk.f[:, :], func=AF.Copy),
                         r=[bk.buf], w=[B_ksb] if c == 0 else [], wj=[] if c == 0 else [B_ksb])
                else:
                    P.op("dve", I("tensor_copy", out=ksb[:, c, :], in_=bk.f[:, :]),
                         r=[bk.buf], wj=[B_ksb])
            P.op("sp", I("dma_start", out=KT[:, :, g * 512:(g + 1) * 512].rearrange("c p t -> p c t"),
                                                  in_=ksb), r=[B_ksb], w=[B_KV[g]], dma=True, key="kst")
            for b in range(4):
                for hf in range(2):
                    bk = nb()
                    for k in range(8):
                        P.op("pe", I("matmul",
                            bk.f[:, :], xt[:, k, b * 128:(b + 1) * 128], wkv[:, k, 1024 + hf * 512:1024 + (hf + 1) * 512],
                            start=(k == 0), stop=(k == 7)), r=[Bxt, B_wkv], w=[bk.buf])
                    src_v = bk.f[:, :].rearrange("p (h d) -> p h d", h=8)
                    first = (b == 0 and hf == 0)
                    if (b + hf) % 2 == 0:
                        P.op("dve", I("tensor_copy",
                            out=vsb[:, b, hf * 8:(hf + 1) * 8, 0:64], in_=src_v),
                            r=[bk.buf], w=[B_vsb] if first else [], wj=[] if first else [B_vsb])
                    else:
                        P.op("act", I("activation",
                            out=vsb[:, b, hf * 8:(hf + 1) * 8, 0:64], in_=src_v, func=AF.Copy),
                            r=[bk.buf], wj=[B_vsb])
            P.op("sp", I("dma_start", out=VA[g * 4:(g + 1) * 4].rearrange("b p h e -> p b h e"), in_=vsb),
                 r=[B_vsb], wj=[B_KV[g]], dma=True, key="vst")
            kcb = nb()
            for b in range(4):
                P.op("pe", I("transpose", out=kcb.h[0:48, b * 128:(b + 1) * 128], in_=CS[:, g * 4 + b, :], identity=ident),
                     r=[B_CS[g * 4 + b], B_const], w=[kcb.buf])
            P.op("act", I("activation", out=kcsb[0:48, :], in_=kcb.h[0:48, 0:512], func=AF.Copy, scale=-1.0),
                 r=[kcb.buf], w=[B_kcsb])
            P.op("sp", I("dma_start", out=KC[:, g * 512:(g + 1) * 512], in_=kcsb[0:48, :]), r=[B_kcsb], wj=[B_KV[g]],
                 dma=True, key="kcst")
        if debug:
            P.op("sp", I("dma_start", out=CK_d, in_=CK.rearrange("p n h -> p (n h)")), r=B_CK, dma=True, key="dbg")
        P.barrier(B_KV + B_CK + B_CS + [B_mkv])
        A.release(a2_base)

    if "b" in phases:
        win, wout, B_win, B_wout = winB, woutB, B_winB, B_woutB
        masks = A.t(BF16, 2, 4, 512)
        B_masks = Buf("masks")
        am = A.t(F32, 2)
        P.op("dve", I("tensor_scalar", out=am, in0=selt, scalar1=float(ALPHA), scalar2=None, op0=ALU.mult),
             r=[B_const], wj=[B_const])
        if "a2" not in phases:
            P.barrier([B_mkv])

        xst = [A.t(BF16, 2, 2, 512)] * 2
        B_xst = [Buf("xst0")] * 2
        xsel = A.t(BF16, 8, 512)
        B_xsel = Buf("xsel")
        csel = A.t(BF16, 4, 48)
        B_csel = Buf("csel")
        csT = A.t(BF16, 512)
        B_csT = Buf("csT")
        gmain = [A.t(F32, 2, 512) for _ in range(2)]
        B_gmain = [Buf("gm0"), Buf("gm1")]
        Qa = [A.t(BF16, 4, 512) for _ in range(2)]
        B_Qa = [Buf("Qa0"), Buf("Qa1")]
        kbuf = [A.t(BF16, 4, 512) for _ in range(2)]
        B_kbuf = [Buf("kb0"), Buf("kb1")]
        vbuf = [A.t(BF16, 4, 4, 128) for _ in range(2)]
        B_vbuf = [Buf("vb0"), Buf("vb1")]
        PTb = [A.t(BF16, 512) for _ in range(3)]
        B_PTb = [Buf(f"ptb{i}") for i in range(3)]
        gsb = [A.t(F32, 512) for _ in range(2)]
        B_gsb = [Buf("gsbB0"), Buf("gsbB1")]
        qm = A.t(BF16, 4, 512)
        B_qm = Buf("qmB")
        PT = PTb[0:2]
        B_PT = B_PTb[0:2]
        rd = A.t(F32, 512)
        t1 = A.t(F32, 512)
        B_rd, B_t1 = Buf("rdB"), Buf("t1B")
        rden = [A.t(F32, 512) for _ in range(2)]
        B_rden = [Buf("rden0"), Buf("rden1")]
        gated = A.t(BF16, 12, 512)
        B_gated = Buf("gatedB")
        xfA = [A.t(F32, D)] * 2
        xfB = [A.t(F32, D)] * 2
        B_xfA = [Buf("xfA0")] * 2
        B_xfB = [Buf("xfB0")] * 2
        zz = [A.t(F32, D) for _ in range(2)]
        zn = [A.t(F32, D) for _ in range(2)]
        B_zz = [Buf("zB0"), Buf("zB1")]
        B_zn = [Buf("znB0"), Buf("znB1")]
        junk, B_junk = None, None
        tmpn = [rd, t1, gsb[0], gsb[1]]
        B_tmpn = [B_rd, B_t1, B_gsb[0], B_gsb[1]]
        B_out = Buf("out")
        for i in range(2):
            P.op("pool", I("memset", kbuf[i][64:70, :, :], 1.0), w=[B_kbuf[i]])
            P.op("pool", I("memset", Qa[i][64:70, :, :], 1.0), w=[B_Qa[i]])

        Obk = banks[0:4]
        Sbk = banks[4:7]
        Mbk = banks[7]
        kv_rr = [0]
        s_rr = [0]
        pt_rr = [0]
        gs_rr = [0]
        xs_rr = [0]
        def do_select(g0, g1):
                for pc in range(4):
                    q = xs_rr[0] % 2
                    xs_rr[0] += 1
                    pq_load(xst[q][:, 0, :, :], X1T[2 * pc:2 * pc + 2, :, g0 * 512:(g0 + 1) * 512].rearrange("c p t -> p c t"),
                            [B_xst[q]], "xst")
                    P.op("pool", I("dma_start",
                        out=xst[q][:, 1, :, :], in_=X1T[2 * pc:2 * pc + 2, :, g1 * 512:(g1 + 1) * 512].rearrange("c p t -> p c t")),
                        wj=[B_xst[q]], dma=True, key="xst")
                    P.op("dve", I("tensor_scalar", out=xsel[:, 2 * pc:2 * pc + 2, :], in0=xst[q][:, 0, :, :],
                                                                     scalar1=selt[:, 0:1], scalar2=None, op0=ALU.mult),
                         r=[B_xst[q], B_const], w=[B_xsel] if pc == 0 else [], wj=[] if pc == 0 else [B_xsel])
                    P.op("dve", I("scalar_tensor_tensor",
                        out=xsel[:, 2 * pc:2 * pc + 2, :], in0=xst[q][:, 1, :, :], scalar=selt[:, 1:2],
                        in1=xsel[:, 2 * pc:2 * pc + 2, :], op0=ALU.mult, op1=ALU.add),
                        r=[B_xst[q], B_const], wj=[B_xsel])
                P.op("dve", I("tensor_scalar", out=csel, in0=CS[:, g0 * 4:(g0 + 1) * 4, :], scalar1=selt[:, 0:1],
                                                             scalar2=None, op0=ALU.mult),
                     r=B_CS[g0 * 4:(g0 + 1) * 4] + [B_const], w=[B_csel])
                P.op("dve", I("scalar_tensor_tensor", out=csel, in0=CS[:, g1 * 4:(g1 + 1) * 4, :],
                                                                    scalar=selt[:, 1:2], in1=csel, op0=ALU.mult, op1=ALU.add),
                     r=B_CS[g1 * 4:(g1 + 1) * 4] + [B_const], wj=[B_csel])

        slots = slot_tiles(NT512)
        def proj_items(G):
            gq = G % 2
            items = []
            for oc in range(2):
                def it(bkf, oc=oc):
                    gb = bkf()
                    proj(gb, win, B_win, DMIX + (2 * G + oc) * 128, 128, xsel, B_xsel, 512)
                    P.op("act", I("activation", out=gmain[gq][:, oc, :], in_=gb.f[:, :], func=AF.Silu),
                         r=[gb.buf], w=[B_gmain[gq]] if oc == 0 else [], wj=[] if oc == 0 else [B_gmain[gq]])
                items.append(it)
            for hl in range(4):
                def it(bkf, hl=hl):
                    h = 4 * G + hl
                    qb = bkf()

                    def efirst(bk):
                        P.op("pe", I("matmul", bk.f[0:67, :], Emat[0:48, h, :], csT[0:48, :], start=True, stop=False),
                             r=[B_const, B_csT], w=[bk.buf])
                    proj(qb, win, B_win, h * 64, 64, xsel, B_xsel, 512, extra_first=efirst)
                    if hl % 2 == 0:
                        P.op("act", I("activation", out=Qa[gq][0:67, hl, :], in_=qb.f[0:67, :], func=AF.Copy),
                             r=[qb.buf], w=[B_Qa[gq]] if hl == 0 else [], wj=[] if hl == 0 else [B_Qa[gq]])
                    else:
                        P.op("dve", I("tensor_copy", out=Qa[gq][0:67, hl, :], in_=qb.f[0:67, :]),
                             r=[qb.buf], wj=[B_Qa[gq]])
                items.append(it)
            return items

        def pre_A(j, g0, g1):
            sp_load(masks, masks_d[j % 2].rearrange("p (a b c) -> p a b c", a=2, b=4), [B_masks], "masks")
            if j == 0:
                do_select(g0, g1)
            cbk = nb()
            for b in range(4):
                P.op("pe", I("transpose", out=cbk.h[0:48, b * 128:(b + 1) * 128], in_=csel[:, b, :],
                                                               identity=ident), r=[B_csel, B_const], w=[cbk.buf])
            P.op("dve", I("tensor_copy", out=csT[0:48, :], in_=cbk.h[0:48, 0:512]), r=[cbk.buf], w=[B_csT])

            for h in range(4):
                bk = nb()
                proj(bk, win, B_win, D + h * 128, 128, xsel, B_xsel, 512)
                P.op("act", I("activation", out=qm[:, h, :], in_=bk.f[:, :], func=AF.Copy,
                                                               scale=128.0 ** -0.5),
                     r=[bk.buf], w=[B_qm] if h == 0 else [], wj=[] if h == 0 else [B_qm])
            for it in proj_items(0):
                it(nb)

        def pre_B(j):
            for h in range(4):
                ybk, dbk = nb(), nb()
                mem_attn(h, qm, B_qm, 512, PT, B_PT, ybk, dbk)
                gb = nb()
                proj(gb, win, B_win, DMIX + D + h * 128, 128, xsel, B_xsel, 512)
                gi = gs_rr[0] % 2
                gs_rr[0] += 1
                P.op("act", I("activation", out=gsb[gi], in_=gb.f[:, :], func=AF.Silu),
                     r=[gb.buf], w=[B_gsb[gi]])
                P.op("dve", I("reciprocal", out=rd, in_=dbk.f[:, :]), r=[dbk.buf], w=[B_rd])
                P.op("dve", I("tensor_tensor", out=t1, in0=ybk.f[:, :], in1=rd, op=ALU.mult),
                     r=[ybk.buf, B_rd], w=[B_t1])
                P.op("dve", I("tensor_tensor", out=gated[:, 8 + h, :], in0=t1, in1=gsb[gi], op=ALU.mult),
                     r=[B_t1, B_gsb[gi]], w=[B_gated] if h == 0 else [], wj=[] if h == 0 else [B_gated])

        def attention(j, nch):
            norm_pending = []
            for G in range(4):
                gq = G % 2
                nxt = proj_items(G + 1) if G < 3 else []
                if G == 3 and j + 1 < len(slots):
                    do_select(slots[j + 1][0], slots[j + 1][1])
                units = [(kc, kb, hl) for kc in range(nch) for kb in range(4) for hl in range(4)]
                pend = []
                kq = 0
                for idx, (kc, kb, hl) in enumerate(units):
                    if norm_pending and idx in (2, 4):
                        norm_pending.pop(0)()
                    h = 4 * G + hl
                    n = kc * 4 + kb
                    mtype = kc - (nch - 2)
                    if kb == 0 and hl == 0:
                        kq = kv_rr[0] % 2
                        kv_rr[0] += 1
                        sp_load(kbuf[kq][0:64, :, :],
                                KT.rearrange("c (two d) t -> (c two) d t", two=2)[4 * G:4 * G + 4, :, kc * 512:(kc + 1) * 512]
                                .rearrange("h d t -> d h t"), [B_kbuf[kq]], f"kb{kq}", r=[B_KV[kc]])
                        P.op("sp", I("dma_start", out=kbuf[kq][67:70, :, :],
                                     in_=KC.rearrange("(h i) t -> i h t", i=3)[:, 4 * G:4 * G + 4, kc * 512:(kc + 1) * 512]),
                             r=[B_KV[kc]], wj=[B_kbuf[kq]], dma=True, key=f"kb{kq}")
                        sp_load(vbuf[kq], VA[kc * 4:(kc + 1) * 4, :, 4 * G:4 * G + 4, :].rearrange("b p h e -> p b h e"),
                                [B_vbuf[kq]], f"vb{kq}", r=[B_KV[kc]])
                    sb = Sbk[s_rr[0] % 3]
                    s_rr[0] += 1
                    P.op("pe", I("matmul", sb.f[:, :], kbuf[kq][0:70, hl, kb * 128:(kb + 1) * 128], Qa[gq][0:70, hl, :],
                                 start=True, stop=(mtype < 0)), r=[B_kbuf[kq], B_Qa[gq]], w=[sb.buf])
                    if mtype >= 0:
                        P.op("pe", I("matmul", sb.f[:, :], ident, masks[:, mtype, kb, :], start=False, stop=True),
                             r=[B_const, B_masks], w=[sb.buf])
                    pi = pt_rr[0] % 3
                    pt_rr[0] += 1
                    P.op("act", I("activation", out=PTb[pi], in_=sb.f[:, :], func=AF.Exp, scale=0.125),
                         r=[sb.buf], w=[B_PTb[pi]])

                    def pv(hl=hl, kq=kq, kb=kb, pi=pi, first=(idx < 4), last=(idx >= len(units) - 4)):
                        P.op("pe", I("matmul", Obk[hl].f[:, :], vbuf[kq][:, kb, hl, :], PTb[pi], start=first, stop=last),
                             r=[B_vbuf[kq], B_PTb[pi]], w=[Obk[hl].buf])
                    pend.append(pv)
                    if len(pend) > 2:
                        pend.pop(0)()
                    if nxt and idx >= 3 and idx % 2 == 1:
                        nxt.pop(0)(lambda: Mbk)
                for f in pend:
                    f()
                for it in nxt:
                    it(lambda: Mbk)
                def mk_norm(G, pair, gq):
                    def f():
                        hls = (2 * pair, 2 * pair + 1)
                        for hl in hls:
                            po = (hl % 2) * 64
                            P.op("dve", I("tensor_tensor", out=tmpn[hl][0:64, :], in0=Obk[hl].f[0:64, :],
                                          in1=gmain[gq][po:po + 64, hl // 2, :], op=ALU.mult),
                                 r=[Obk[hl].buf, B_gmain[gq]], w=[B_tmpn[hl]])
                        for hl in hls:
                            P.op("act", I("activation", out=rden[hl % 2][0:64, :], in_=Obk[hl].f[64:128, :], func=AF.Ln),
                                 r=[Obk[hl].buf], w=[B_rden[hl % 2]])
                        for hl in hls:
                            P.op("act", I("activation", out=rden[hl % 2][0:64, :], in_=rden[hl % 2][0:64, :], func=AF.Exp,
                                          scale=-1.0), r=[B_rden[hl % 2]], w=[B_rden[hl % 2]])
                        for hl in hls:
                            h = 4 * G + hl
                            po = (hl % 2) * 64
                            P.op("dve", I("tensor_tensor", out=gated[po:po + 64, h // 2, :], in0=tmpn[hl][0:64, :],
                                          in1=rden[hl % 2][0:64, :], op=ALU.mult), r=[B_tmpn[hl], B_rden[hl % 2]],
                                 wj=[B_gated])
                    return f
                norm_pending = [mk_norm(G, pair, gq) for pair in range(2)]
                if G == 3:
                    for f in norm_pending:
                        f()
                    norm_pending = []

        def post(j, g0, g1):
            for b in (0, 1, -1, 2, 3, -2):
                if b < 0:
                    for b2 in ((0, 1) if b == -1 else (2, 3)):
                        q = b2 % 2
                        row0 = j * 512 + b2 * 128
                        ln_block(None, None, [(xfA[q], B_xfA[q], am[:, 0:1]), (xfB[q], B_xfB[q], am[:, 1:2])], None,
                                 zz[q], B_zz[q], zn[q], B_zn[q], junk, B_junk, q, None, None,
                                 out_d[row0:row0 + 128, :], B_out, f"outst{q}", B_dst_join=True, st_eng="pool",
                                 part="rest", ln_act=True)
                    continue
                q = b % 2
                pq_load(xfA[q], X1F[g0 * 512 + b * 128:g0 * 512 + (b + 1) * 128, :], [B_xfA[q]], "xfA", r=[B_X1F[g0]])
                pq_load(xfB[q], X1F[g1 * 512 + b * 128:g1 * 512 + (b + 1) * 128, :], [B_xfB[q]], "xfB", r=[B_X1F[g1]])
                yb = [nb(), nb()]
                for hf in range(2):
                    for k in range(12):
                        P.op("pe", I("matmul",
                            yb[hf].f[:, :], gated[:, k, b * 128:(b + 1) * 128], wout[:, k, hf * 512:(hf + 1) * 512],
                            start=(k == 0), stop=(k == 11)), r=[B_gated, B_wout], w=[yb[hf].buf])
                row0 = j * 512 + b * 128
                ln_block(yb[0], yb[1], [(xfA[q], B_xfA[q], am[:, 0:1]), (xfB[q], B_xfB[q], am[:, 1:2])], None,
                         zz[q], B_zz[q], zn[q], B_zn[q], junk, B_junk, q, None, None,
                         out_d[row0:row0 + 128, :], B_out, f"outst{q}", B_dst_join=True, st_eng="pool", part="z", ln_act=True)
        for j, (g0, g1, nch) in enumerate(slots):
            if j == 0:
                pre_A(0, g0, g1)
                pre_B(0)
            attention(j, nch)
            if j + 1 < len(slots):
                pre_A(j + 1, slots[j + 1][0], slots[j + 1][1])
            post(j, g0, g1)
            if j + 1 < len(slots):
                pre_B(j + 1)
        A.release(base_mark)

    P.emit()
    return nc


def _consts():
    bf = ml_dtypes.bfloat16
    cb = np.zeros((128, 3 * 128 + 16 * 67), np.float32)
    cb[:, 0:128] = np.eye(128)
    cb[:, 128:256] = 1.0
    E = np.zeros((128, 16, 67), np.float32)
    for h in range(16):
        for i in range(3):
            E[h * 3 + i, h, 64 + i] = 1.0
    cb[:, 384:] = E.reshape(128, -1)
    cf = np.zeros((128, 384), np.float32)
    s = np.arange(128)
    cf[:, 0:128] = (s[:, None] <= s[None, :]).astype(np.float32)
    cf[:, 128:256] = 1.0
    invc = np.zeros((4, 16), np.float32)
    for g, w in enumerate(POOLW):
        invc[g] = 1.0 / np.minimum(np.arange(16) + 1, w)
    cf[:, 256:320] = invc.reshape(1, 64)
    cf[:, 320:384] = 1.0
    return cb.astype(bf), cf


def _masks(hh):
    bf = ml_dtypes.bfloat16
    s = np.arange(128)[:, None, None]
    kb = np.arange(4)[None, :, None]
    t = np.arange(512)[None, None, :]
    diag = np.where(kb * 128 + s > t, MASKNEG, 0.0).astype(np.float32)
    full = np.full((128, 4, 512), MASKNEG, np.float32)
    none = np.zeros((128, 4, 512), np.float32)
    a = np.stack([diag, full], axis=1).reshape(128, -1)
    b = np.stack([none, diag], axis=1).reshape(128, -1)
    m = np.stack([a, b] if hh == 0 else [b, a], axis=0)
    return m.astype(bf)


def make_in_maps(inputs, S):
    cb, cf = _consts()
    x = np.asarray(inputs["x"], np.float32)
    B = x.shape[0]
    lnp = np.ascontiguousarray(np.stack([inputs["ln_g"], inputs["ln_b"]], axis=1).astype(np.float32))
    pscale = np.ascontiguousarray(np.asarray(inputs["pool_scale"], np.float32).reshape(8, 128).T)
    maps = []
    for c in range(2 * B):
        b, hh = c // 2, c % 2
        sel = np.zeros((128, 2), np.float32)
        sel[:, hh] = 1.0
        maps.append({
            "x": np.ascontiguousarray(x[b]),
            "mem": np.ascontiguousarray(np.asarray(inputs["mem"], np.float32)[b]),
            "w_in": np.asarray(inputs["w_in"], np.float32),
            "w_mem_kv": np.asarray(inputs["w_mem_kv"], np.float32),
            "w_out": np.asarray(inputs["w_out"], np.float32),
            "pool_w": np.ascontiguousarray(np.asarray(inputs["pool_w"], np.float32)[0]),
            "w_kv": np.asarray(inputs["w_kv_shared"], np.float32),
            "lnp": lnp,
            "pscale": pscale,
            "bfor": np.asarray(inputs["b_forget"], np.float32),
            "cb16": cb,
            "cf32": cf,
            "sel": sel,
            "masks": _masks(hh),
        })
    return maps


_NC_CACHE = {}


def kernel(x, mem, w_in, w_mem_kv, w_out, ln_g, ln_b, pool_w, pool_scale, w_kv_shared, b_forget):
    inputs = dict(x=x, mem=mem, w_in=w_in, w_mem_kv=w_mem_kv, w_out=w_out, ln_g=ln_g, ln_b=ln_b,
                  pool_w=pool_w, pool_scale=pool_scale, w_kv_shared=w_kv_shared, b_forget=b_forget)
    B, S, _ = np.asarray(x).shape
    if S not in _NC_CACHE:
        _NC_CACHE[S] = build(S)
    nc = _NC_CACHE[S]
    maps = make_in_maps(inputs, S)
    res = run_bass_kernel_spmd(nc, maps, core_ids=list(range(2 * B)))
    out = np.zeros((B, S, D), np.float32)
    st = slot_tiles(S // 512)
    for c in range(2 * B):
        b, hh = c // 2, c % 2
        o = np.asarray(res.results[c]["out"], np.float32)
        for j, (g0, g1, _) in enumerate(st):
            g = g0 if hh == 0 else g1
            out[b, g * 512:(g + 1) * 512, :] = o[j * 512:(j + 1) * 512, :]
    return out
```

```python
import numpy as np
import ml_dtypes
import concourse.bass as bass
import concourse.mybir as mybir
from concourse.bass_utils import run_bass_kernel_spmd

dt = mybir.dt
F32, BF16 = dt.float32, dt.bfloat16
AF = mybir.ActivationFunctionType
ALU = mybir.AluOpType

D = 1024
NMEM = 256
DIN = 3072
DMIX = 1536
NH = 16
DH = 64
ALPHA = 4.0 ** 0.25
EPS = 1e-5
POOLW = (2, 4, 8, 16)
MASKNEG = -30000.0

SAME_ENGINE_SYNC = True


class Buf:
    __slots__ = ("name", "w", "r", "jd")

    def __init__(self, name):
        self.name = name
        self.w = []
        self.r = []
        self.jd = []


class Op:
    __slots__ = ("eng", "fn", "deps", "sig", "sem", "val", "dma", "key", "idx")


ENGS = ("pe", "act", "dve", "pool", "sp")


class Prog:
    def __init__(self, nc):
        self.nc = nc
        self.ops = {e: [] for e in ENGS}
        self.n = 0

    def op(self, eng, fn, r=(), w=(), wj=(), dma=False, key=None):
        o = Op()
        o.eng, o.fn, o.dma, o.key = eng, fn, dma, key
        o.sig = dma
        o.sem = None
        o.val = 0
        o.idx = self.n
        self.n += 1
        deps = []
        for b in r:
            deps += b.w
        for b in w:
            deps += b.w
            deps += b.r
        for b in wj:
            deps += b.jd
        for b in r:
            b.r.append(o)
        for b in w:
            b.jd = b.w + b.r
            b.w = [o]
            b.r = []
        for b in wj:
            b.w.append(o)
        seen = set()
        fd = []
        for d in deps:
            if id(d) in seen or d is o:
                continue
            seen.add(id(d))
            if (not d.dma) and d.eng == eng and (eng == "pe" or not SAME_ENGINE_SYNC):
                continue
            fd.append(d)
            d.sig = True
        o.deps = fd
        self.ops[eng].append(o)
        return o

    def barrier(self, bufs):
        b = Buf("barrier")
        lasts = []
        for e in ENGS:
            if self.ops[e]:
                lasts.append(self.ops[e][-1])
        for x in bufs:
            lasts += x.w + x.r
        for e in ENGS:
            o = Op()
            o.eng, o.fn, o.dma, o.key = e, None, False, None
            o.sig = False
            o.sem = None
            o.val = 0
            o.idx = self.n
            self.n += 1
            o.deps = []
            seen = set()
            for d in lasts:
                if id(d) in seen:
                    continue
                seen.add(id(d))
                if (not d.dma) and d.eng == e:
                    continue
                o.deps.append(d)
                d.sig = True
            self.ops[e].append(o)

    def emit(self):
        nc = self.nc
        engsem = {e: nc.alloc_semaphore("s_" + e) for e in ("pe", "act", "dve", "pool")}
        keysem = {}
        cnt = {}
        for e in ENGS:
            for o in self.ops[e]:
                if o.dma:
                    k = o.key
                    if k not in keysem:
                        keysem[k] = nc.alloc_semaphore("d_" + k)
                        cnt[k] = 0
                    cnt[k] += 16
                    o.sem, o.val = keysem[k], cnt[k]
                elif o.sig:
                    cnt[e] = cnt.get(e, 0) + 1
                    o.sem, o.val = engsem[e], cnt[e]
        finals = [(keysem[k], cnt[k]) for k in keysem]
        ops = self.ops

        def mk(e):
            def body(eng):
                wm = {}
                for o in ops[e]:
                    need = {}
                    for d in o.deps:
                        k = d.sem.num
                        if wm.get(k, 0) < d.val and need.get(k, (None, 0))[1] < d.val:
                            need[k] = (d.sem, d.val)
                    for k, (s, v) in need.items():
                        eng.wait_ge(s, v)
                        wm[k] = v
                    if o.fn is None:
                        continue
                    inst = o.fn(eng)
                    if o.sig:
                        inst.then_inc(o.sem, 16 if o.dma else 1)
                if e == "sp":
                    for s, v in finals:
                        eng.wait_ge(s, v)
            return body

        with nc.Block() as blk:
            blk.tensor(mk("pe"))
            blk.scalar(mk("act"))
            blk.vector(mk("dve"))
            blk.gpsimd(mk("pool"))
            blk.sync(mk("sp"))


class Arena:
    def __init__(self, nc, nbytes):
        self.h32 = nc.alloc_sbuf_tensor("arena", [128, nbytes // 4], F32)
        self.h16 = self.h32.bitcast(BF16)
        self.top = 0
        self.cap = nbytes

    def mark(self):
        return self.top

    def release(self, m):
        self.top = m

    def t(self, dtype, *shape):
        n = 1
        for s in shape:
            n *= s
        nb = n * (4 if dtype == F32 else 2)
        nb = (nb + 63) // 64 * 64
        off = self.top
        self.top += nb
        assert self.top <= self.cap, f"SBUF arena overflow {self.top} > {self.cap}"
        if dtype == F32:
            ap = self.h32[:, off // 4: off // 4 + n]
        else:
            ap = self.h16[:, off // 2: off // 2 + n]
        if len(shape) == 2:
            ap = ap.rearrange("p (a b) -> p a b", a=shape[0])
        elif len(shape) == 3:
            ap = ap.rearrange("p (a b c) -> p a b c", a=shape[0], b=shape[1])
        return ap


def own_tiles(hh, n512):
    return [g for g in range(n512) if ((g % 4) in (0, 3)) == (hh == 0)]


def I(name, *a, **k):
    return lambda e: getattr(e, name)(*a, **k)


class Bank:
    def __init__(self, P32, P16, k):
        self.buf = Buf(f"bank{k}")
        self.f = P32[:, k * 512:(k + 1) * 512]
        self.h = P16[:, k * 1024:(k + 1) * 1024]


def slot_tiles(n512):
    res = []
    for j in range(n512 // 2):
        p, e = j // 2, j % 2
        g0 = 4 * p + (0, 3)[e]
        g1 = 4 * p + (1, 2)[e]
        res.append((g0, g1, max(g0, g1) + 1))
    return res


def build(S, phases=("a1", "a2", "b"), debug=False, TA=256):
    assert S % 2048 == 0
    NT512 = S // 512
    NBLK = S // 128
    NSLOT = NT512 // 2
    nc = bass.Bass("TRN2", target_bir_lowering=False)
    P = Prog(nc)

    def din(name, shape, dtype=F32):
        return nc.dram_tensor(name, list(shape), dtype, kind="ExternalInput").ap()

    x_d = din("x", [S, D])
    mem_d = din("mem", [NMEM, D])
    w_in_d = din("w_in", [2, D, DIN])
    w_mkv_d = din("w_mem_kv", [2, D, D])
    w_out_d = din("w_out", [2, DMIX, D])
    pool_w_d = din("pool_w", [4, 256, 256])
    w_kv_d = din("w_kv", [D, 2064])
    lnp_d = din("lnp", [2, 2, D])
    pscale_d = din("pscale", [128, 8])
    bfor_d = din("bfor", [16])
    cb16_d = din("cb16", [128, 3 * 128 + 16 * 67], BF16)
    cf32_d = din("cf32", [128, 128 + 128 + 64 + 64], F32)
    sel_d = din("sel", [128, 2])
    masks_d = din("masks", [2, 128, 2 * 4 * 512], BF16)
    out_d = nc.dram_tensor("out", [NSLOT * 512, D], F32, kind="ExternalOutput").ap()

    kind_scr = "ExternalOutput" if debug else "Internal"
    X1F = nc.dram_tensor("x1f_scr", [S, D], F32, kind=kind_scr).ap()
    X1T = nc.dram_tensor("x1t_scr", [8, 128, S], BF16, kind=kind_scr).ap()
    KT = nc.dram_tensor("kt_scr", [8, 128, S], BF16, kind=kind_scr).ap()
    VA = nc.dram_tensor("va_scr", [NBLK, 128, NH, 128], BF16, kind=kind_scr).ap()
    KC = nc.dram_tensor("kc_scr", [NH * 3, S], BF16, kind=kind_scr).ap()
    if debug:
        CK_d = nc.dram_tensor("ck_dbg", [128, NBLK * 16], F32, kind="ExternalOutput").ap()

    A = Arena(nc, 212736)
    P32 = nc.alloc_psum_tensor("ps", [128, 4096], F32)
    P16 = P32.bitcast(BF16)
    banks = [Bank(P32, P16, k) for k in range(8)]
    bank_rr = [0]

    def nb():
        b = banks[bank_rr[0] % 8]
        bank_rr[0] += 1
        return b

    cb16 = A.t(BF16, 3 * 128 + 16 * 67)
    ident = cb16[:, 0:128]
    ones16 = cb16[:, 128:256]
    Emat = cb16[:, 384:384 + 16 * 67].rearrange("p (h m) -> p h m", h=16)
    cf32 = A.t(F32, 384)
    tri = cf32[:, 0:128]
    allones = cf32[:, 128:256]
    invc = cf32[:, 256:320].rearrange("p (w t) -> p w t", w=4)
    ones64 = cf32[:, 320:384]
    pscale = A.t(F32, 8)
    bfor = A.t(F32, 16)
    selt = A.t(F32, 2)
    CK = A.t(F32, NBLK, 16)
    CS = A.t(BF16, NBLK, 48)
    Tn = A.t(F32, 16)
    lng = A.t(F32, D)
    lnb = A.t(F32, D)
    mkT = A.t(BF16, 4, NMEM)
    mv = A.t(BF16, 2, 512)
    small = A.t(F32, 64)

    B_const = Buf("const")
    B_ln = Buf("lnp")
    B_mkv = Buf("mkv")
    B_CK = [Buf(f"ck{n}") for n in range(NBLK)]
    B_CS = [Buf(f"cs{n}") for n in range(NBLK)]
    B_Tn = Buf("Tn")
    B_X1F = [Buf(f"x1f{g}") for g in range(NT512)]
    B_X1T = [Buf(f"x1t{g}") for g in range(NT512)]
    B_KV = [Buf(f"kv{g}") for g in range(NT512)]

    def sp_load(out, in_, w, key, r=()):
        return P.op("sp", I("dma_start", out=out, in_=in_), r=r, w=w, dma=True, key=key)

    def pq_load(out, in_, w, key, r=(), wj=()):
        return P.op("pool", I("dma_start", out=out, in_=in_), r=r, w=w, wj=wj, dma=True, key=key)

    def cast_load(out, in_, w, key, r=(), wj=()):
        return P.op("pool", I("dma_start", out=out, in_=in_), r=r, w=w, wj=wj, dma=True, key=key)

    sp_load(cb16, cb16_d, [B_const], "c0")
    P.op("sp", I("dma_start", out=cf32, in_=cf32_d), wj=[B_const], dma=True, key="c0")
    P.op("sp", I("dma_start", out=pscale, in_=pscale_d), wj=[B_const], dma=True, key="c0")
    P.op("sp", I("dma_start", out=bfor, in_=bfor_d.partition_broadcast(128)), wj=[B_const], dma=True, key="c0")
    P.op("sp", I("dma_start", out=selt, in_=sel_d), wj=[B_const], dma=True, key="c0")

    def load_ln(layer):
        sp_load(lng, lnp_d[layer, 0].partition_broadcast(128), [B_ln], "ln")
        P.op("sp", I("dma_start", out=lnb, in_=lnp_d[layer, 1].partition_broadcast(128)),
             wj=[B_ln], dma=True, key="ln")

    def load_w(dst, src_rows_by_cols, nk, ncols, buf, key, first=True):
        srcv = src_rows_by_cols.rearrange("(k p) n -> p k n", p=128)
        for k in range(nk):
            for c0 in range(0, ncols, 1024):
                step = min(1024, ncols - c0)
                o = dst[:, k, c0:c0 + step]
                i = srcv[:, k, c0:c0 + step]
                if first:
                    cast_load(o, i, [buf], key)
                    first = False
                else:
                    cast_load(o, i, [], key, wj=[buf])

    def transposes_to(dstT, src_tok, nblk, r_src, w_dst, alt, all_act=False):
        for c in range(8):
            bk = nb()
            for b in range(nblk):
                P.op("pe", I("transpose",
                    out=bk.h[:, b * 128:(b + 1) * 128], in_=src_tok[:, b, c * 128:(c + 1) * 128],
                    identity=ident), r=[r_src[b] if isinstance(r_src, list) else r_src, B_const], w=[bk.buf])
            n = nblk * 128
            if (c + alt) % 2 == 0 and not all_act:
                P.op("dve", I("tensor_copy", out=dstT[:, c, 0:n], in_=bk.h[:, 0:n]),
                     r=[bk.buf], w=[] if c else [w_dst], wj=[w_dst] if c else [])
            else:
                P.op("act", I("activation", out=dstT[:, c, 0:n], in_=bk.h[:, 0:n], func=AF.Copy),
                     r=[bk.buf], w=[] if c else [w_dst], wj=[w_dst] if c else [])

    def proj(bk, W, wbuf, col0, m, xT, xbuf, n, extra_first=None):
        first = True
        if extra_first is not None:
            extra_first(bk)
            first = False
        for k in range(8):
            P.op("pe", I("matmul",
                bk.f[0:m, 0:n], W[:, k, col0:col0 + m], xT[:, k, 0:n], start=first, stop=(k == 7)),
                r=[wbuf, xbuf], w=[bk.buf])
            first = False

    def mem_kv(layer, wtmp, B_wtmp, memb, B_memb, memT, B_memT):
        load_w(wtmp, w_mkv_d[layer], 8, D, B_wtmp, "wtmp")
        cast_load(memb, mem_d.rearrange("(b p) d -> p b d", p=128), [B_memb], "memb")
        transposes_to(memT, memb, 2, B_memb, B_memT, 0)
        for h in range(4):
            bk = nb()
            proj(bk, wtmp, B_wtmp, h * 128, 128, memT, B_memT, NMEM)
            P.op("act", I("activation", out=mkT[:, h, :], in_=bk.f[:, 0:NMEM], func=AF.Copy),
                 r=[bk.buf], w=[B_mkv] if h == 0 else [], wj=[] if h == 0 else [B_mkv])
        for j in range(2):
            bk = nb()
            for k in range(8):
                P.op("pe", I("matmul",
                    bk.f[:, :], memT[:, k, j * 128:(j + 1) * 128], wtmp[:, k, 512:1024],
                    start=(k == 0), stop=(k == 7)), r=[B_wtmp, B_memT], w=[bk.buf])
            P.op("dve", I("tensor_copy", out=mv[:, j, :], in_=bk.f[:, :]),
                 r=[bk.buf], wj=[B_mkv])

    def mem_attn(h, qm, B_qm, n, PT, B_PT, ybk, dbk):
        for j in range(2):
            lb = nb()
            P.op("pe", I("matmul",
                lb.f[:, 0:n], mkT[:, h, j * 128:(j + 1) * 128], qm[:, h, 0:n], start=True, stop=True),
                r=[B_mkv, B_qm], w=[lb.buf])
            P.op("act", I("activation", out=PT[j][:, 0:n], in_=lb.f[:, 0:n], func=AF.Exp),
                 r=[lb.buf], w=[B_PT[j]])
        for j in range(2):
            P.op("pe", I("matmul",
                ybk.f[:, 0:n], mv[:, j, h * 128:(h + 1) * 128], PT[j][:, 0:n], start=(j == 0), stop=(j == 1)),
                r=[B_mkv, B_PT[j]], w=[ybk.buf])
        for j in range(2):
            P.op("pe", I("matmul",
                dbk.f[:, 0:n], ones16, PT[j][:, 0:n], start=(j == 0), stop=(j == 1)),
                r=[B_const, B_PT[j]], w=[dbk.buf])

    def ln_block(ybk0, ybk1, xres, B_xres, z, B_z, zn, B_zn, junk, B_junk, slot, x1b, B_x1b,
                 dst_dram, B_dst, key, B_dst_join=False, res_scale=ALPHA, st_eng="sp", ln_act=False, part="all"):
        sm = small[:, slot * 8:(slot + 1) * 8]
        B_sm = B_small[slot]
        if not isinstance(xres, list):
            xres_l = [(xres, B_xres, float(res_scale))]
        else:
            xres_l = xres
        for hf, yb in enumerate((ybk0, ybk1) if part in ("all", "z") else ()):
            for ci, (xr, Bxr, sc) in enumerate(xres_l):
                last = ci == len(xres_l) - 1
                firstw = (hf == 0 and ci == 0)
                kw = dict(accum_out=sm[:, hf:hf + 1]) if last else {}
                in1 = yb.f[:, :] if ci == 0 else z[:, hf * 512:(hf + 1) * 512]
                rr = [Bxr, yb.buf] if ci == 0 else [Bxr, B_z]
                if not isinstance(sc, float):
                    rr = rr + [B_const]
                P.op("dve", I("scalar_tensor_tensor",
                    out=z[:, hf * 512:(hf + 1) * 512], in0=xr[:, hf * 512:(hf + 1) * 512], scalar=sc,
                    in1=in1, op0=ALU.mult, op1=ALU.add, **kw),
                    r=rr, w=[B_z, B_sm] if firstw else [], wj=[] if firstw else [B_z, B_sm])
        if part == "z":
            return
        if ln_act:
            P.op("act", I("activation", out=zn, in_=z, func=AF.Square, accum_out=sm[:, 2:3]), r=[B_z], w=[B_zn], wj=[B_sm])
        else:
            P.op("dve", I("scalar_tensor_tensor", out=zn, in0=z, scalar=1.0, in1=z, op0=ALU.mult, op1=ALU.mult,
                          accum_out=sm[:, 2:3]), r=[B_z], w=[B_zn], wj=[B_sm])
        P.op("dve", I("tensor_scalar", out=sm[:, 3:4], in0=sm[:, 0:1], scalar1=sm[:, 1:2], scalar2=1.0 / D,
                                              op0=ALU.add, op1=ALU.mult), r=[B_sm], wj=[B_sm])
        P.op("dve", I("tensor_tensor", out=sm[:, 4:5], in0=sm[:, 3:4], in1=sm[:, 3:4], op=ALU.mult),
             r=[B_sm], wj=[B_sm])
        P.op("dve", I("scalar_tensor_tensor", out=sm[:, 5:6], in0=sm[:, 2:3], scalar=1.0 / D, in1=sm[:, 4:5],
                                                     op0=ALU.mult, op1=ALU.subtract), r=[B_sm], wj=[B_sm])
        P.op("dve", I("tensor_scalar", out=sm[:, 5:6], in0=sm[:, 5:6], scalar1=EPS, scalar2=None, op0=ALU.add),
             r=[B_sm], wj=[B_sm])
        P.op("pool", I("tensor_tensor", out=sm[:, 6:7], in0=sm[:, 5:6], in1=epsc[:, 1:2], op=ALU.pow),
             r=[B_sm, B_const], wj=[B_sm])
        P.op("dve", I("scalar_tensor_tensor", out=sm[:, 7:8], in0=sm[:, 3:4], scalar=-1.0, in1=sm[:, 6:7],
                                                     op0=ALU.mult, op1=ALU.mult), r=[B_sm], wj=[B_sm])
        P.op("dve", I("tensor_scalar", out=zn, in0=z, scalar1=sm[:, 6:7], scalar2=sm[:, 7:8], op0=ALU.mult, op1=ALU.add),
             r=[B_z, B_sm], w=[B_zn])
        P.op("dve", I("tensor_tensor", out=z, in0=zn, in1=lng, op=ALU.mult), r=[B_zn, B_ln], w=[B_z])
        P.op("pool" if ln_act else "dve", I("tensor_tensor", out=zn, in0=z, in1=lnb, op=ALU.add), r=[B_z, B_ln], w=[B_zn])
        if x1b is not None:
            P.op("pool", I("tensor_copy", out=x1b, in_=zn), r=[B_zn], w=[B_x1b])
        if dst_dram is not None:
            if B_dst_join:
                P.op(st_eng, I("dma_start", out=dst_dram, in_=zn), r=[B_zn], wj=[B_dst], dma=True, key=key)
            else:
                P.op(st_eng, I("dma_start", out=dst_dram, in_=zn), r=[B_zn], w=[B_dst], dma=True, key=key)

    epsc = A.t(F32, 2)
    P.op("dve", I("memset", epsc[:, 0:1], EPS), wj=[B_const])
    P.op("dve", I("memset", epsc[:, 1:2], -0.5), wj=[B_const])
    B_small = [Buf(f"small{i}") for i in range(8)]
    P.op("dve", I("memset", Tn, 0.0), w=[B_Tn])

    base_mark = A.mark()

    if "a1" in phases:
        NB_A = TA // 128
        NTA = S // TA
        win = A.t(BF16, 8, DIN)
        wout = A.t(BF16, 12, D)
        poolw = A.t(BF16, 8, 256)
        B_win, B_wout, B_poolw = Buf("win"), Buf("wout"), Buf("poolw")
        m_tmp = A.mark()
        wtmp = A.t(BF16, 8, D)
        memb = A.t(BF16, 2, D)
        memT = A.t(BF16, 8, NMEM)
        B_wtmp, B_memb, B_memT = Buf("wtmp"), Buf("memb"), Buf("memT")
        load_ln(0)
        mem_kv(0, wtmp, B_wtmp, memb, B_memb, memT, B_memT)
        load_w(win, w_in_d[0], 8, DIN, B_win, "win")
        load_w(poolw, pool_w_d.rearrange("g r c -> (g r) c"), 8, 256, B_poolw, "poolw")
        load_w(wout, w_out_d[0], 12, D, B_wout, "wout")
        P.barrier([B_mkv])
        A.release(m_tmp)

        xb = [A.t(BF16, NB_A, D)] * 2
        B_xb = [Buf("xb0")] * 2
        xc = A.t(F32, NB_A, D)
        B_xc = Buf("xc")
        xf = [A.t(F32, D) for _ in range(2)]
        B_xf = [Buf("xf0"), Buf("xf1")]
        xT = [A.t(BF16, 8, TA) for _ in range(2)]
        B_xT = [Buf("xT0"), Buf("xT1")]
        HW_ = TA + 16
        U = [A.t(F32, 2, HW_) for _ in range(4)]
        B_U = [Buf(f"U{g}") for g in range(4)]
        T1 = A.t(F32, 2, HW_)
        T2 = A.t(F32, 2, HW_)
        B_T1, B_T2 = Buf("T1"), Buf("T2")
        Hh = A.t(F32, 8, 16)
        B_H = [Buf(f"H{c}") for c in range(8)]
        pm = [A.t(BF16, 2, TA) for _ in range(4)]
        B_pm = [Buf(f"pm{g}") for g in range(4)]
        gsb = A.t(F32, 12, TA)
        B_gsb = [Buf(f"gsb{i}") for i in range(12)]
        qm = A.t(BF16, 4, TA)
        B_qm = Buf("qm")
        PT8 = [[A.t(BF16, TA) for _ in range(2)] for _ in range(4)]
        B_PT8 = [[Buf(f"PT{h}{j}") for j in range(2)] for h in range(4)]
        rd = A.t(F32, TA)
        t1 = A.t(F32, TA)
        B_rd, B_t1 = Buf("rd"), Buf("t1")
        tfix = A.t(F32, 16)
        B_tfix = Buf("tfix")
        gated = A.t(BF16, 12, TA)
        B_gated = Buf("gated")
        zz = [A.t(F32, D) for _ in range(2)]
        zn = [A.t(F32, D) for _ in range(2)]
        B_zz = [Buf("z0"), Buf("z1")]
        B_zn = [Buf("zn0"), Buf("zn1")]
        junk, B_junk = None, None
        x1b = A.t(BF16, NB_A, D)
        B_x1b = [Buf(f"x1b{b}") for b in range(NB_A)]
        x1T = A.t(BF16, 8, TA)
        B_x1T = Buf("x1T")
        P.op("pool", I("memset", Hh, 0.0), w=B_H)

        def load_xc(i):
            if i < NTA:
                sp_load(xc, x_d[i * TA:(i + 1) * TA, :].rearrange("(b p) d -> p b d", p=128), [B_xc], "xc")

        def st_T(i):
            s = i % 2
            for b in range(NB_A):
                P.op("act", I("activation", out=xb[s][:, b, :], in_=xc[:, b, :], func=AF.Copy), r=[B_xc],
                     w=[B_xb[s]] if b == 0 else [], wj=[] if b == 0 else [B_xb[s]])
            transposes_to(xT[s], xb[s], NB_A, B_xb[s], B_xT[s], i, all_act=True)
            load_xc(i + 1)

        def st_U(i):
            s = i % 2
            for g in range(4):
                for ch in range(2):
                    c = 2 * g + ch
                    bk = nb()
                    proj(bk, win, B_win, c * 128, 128, xT[s], B_xT[s], TA)
                    P.op("pool", I("tensor_copy", out=U[g][:, ch, 0:16], in_=Hh[:, c, :]),
                         r=[B_H[c]], w=[B_U[g]] if ch == 0 else [], wj=[] if ch == 0 else [B_U[g]])
                    P.op("act", I("activation", out=U[g][:, ch, 16:16 + TA], in_=bk.f[:, 0:TA], func=AF.Copy),
                         r=[bk.buf], wj=[B_U[g]])
                    P.op("act", I("activation", out=Hh[:, c, :], in_=bk.f[:, TA - 16:TA], func=AF.Copy),
                         r=[bk.buf], w=[B_H[c]])

        def st_Upool(i):
            for g in range(4):
                src_, Bs = U[g], B_U[g]
                dsts = [(T1, B_T1), (T2, B_T2)]
                sh = 1
                for lvl in range(g + 1):
                    dstt, Bd = dsts[lvl % 2]
                    lo = 2 * sh - 1
                    P.op("dve", I("tensor_tensor", out=dstt[:, :, lo:HW_], in0=src_[:, :, lo:HW_],
                                  in1=src_[:, :, lo - sh:HW_ - sh], op=ALU.add), r=[Bs], w=[Bd])
                    src_, Bs = dstt, Bd
                    sh *= 2
                wv = POOLW[g]
                P.op("dve", I("scalar_tensor_tensor", out=pm[g][:, :, :], in0=src_[:, :, 16:HW_], scalar=1.0 / wv,
                              in1=U[g][:, :, 16:HW_], op0=ALU.mult, op1=ALU.subtract), r=[Bs, B_U[g]], w=[B_pm[g]])
                if i == 0:
                    for ch in range(2):
                        P.op("dve", I("tensor_tensor", out=tfix, in0=src_[:, ch, 16:32], in1=invc[:, g, :], op=ALU.mult),
                             r=[Bs, B_const], w=[B_tfix])
                        P.op("dve", I("tensor_tensor", out=pm[g][:, ch, 0:16], in0=tfix, in1=U[g][:, ch, 16:32],
                                      op=ALU.subtract), r=[B_tfix, B_U[g], B_pm[g]], wj=[B_pm[g]])

        def st_S2(i):
            s = i % 2
            xTs, BxT = xT[s], B_xT[s]
            for h in range(4):
                bk = nb()
                proj(bk, win, B_win, D + h * 128, 128, xTs, BxT, TA)
                P.op("act", I("activation", out=qm[:, h, :], in_=bk.f[:, 0:TA], func=AF.Copy, scale=128.0 ** -0.5),
                     r=[bk.buf], w=[B_qm] if h == 0 else [], wj=[] if h == 0 else [B_qm])

            def gate(c):
                gb = nb()
                proj(gb, win, B_win, DMIX + c * 128, 128, xTs, BxT, TA)
                P.op("act", I("activation", out=gsb[:, c, :], in_=gb.f[:, 0:TA], func=AF.Silu),
                     r=[gb.buf], w=[B_gsb[c]])
            for c in range(4):
                gate(c)
            for h in range(4):
                for j in range(2):
                    lb = nb()
                    P.op("pe", I("matmul", lb.f[:, 0:TA], mkT[:, h, j * 128:(j + 1) * 128], qm[:, h, :], start=True, stop=True),
                         r=[B_mkv, B_qm], w=[lb.buf])
                    P.op("act", I("activation", out=PT8[h][j], in_=lb.f[:, 0:TA], func=AF.Exp),
                         r=[lb.buf], w=[B_PT8[h][j]])
            for c in range(4, 12):
                gate(c)
            ybks, dbks = [], []
            for h in range(4):
                ybk = nb()
                if h % 2 == 0:
                    dbk2 = nb()
                ybks.append(ybk)
                dbks.append((dbk2, (h % 2) * TA))
                for j in range(2):
                    P.op("pe", I("matmul", ybk.f[:, 0:TA], mv[:, j, h * 128:(h + 1) * 128], PT8[h][j], start=(j == 0),
                                 stop=(j == 1)), r=[B_mkv, B_PT8[h][j]], w=[ybk.buf])
                for j in range(2):
                    P.op("pe", I("matmul", dbk2.f[:, (h % 2) * TA:(h % 2) * TA + TA], ones16, PT8[h][j], start=(j == 0),
                                 stop=(j == 1)), r=[B_const, B_PT8[h][j]], w=[dbk2.buf])
            for h in range(4):
                ybk = ybks[h]
                dbk, doff = dbks[h]
                P.op("dve", I("reciprocal", out=rd, in_=dbk.f[:, doff:doff + TA]), r=[dbk.buf], w=[B_rd])
                P.op("dve", I("tensor_tensor", out=t1, in0=ybk.f[:, 0:TA], in1=rd, op=ALU.mult),
                     r=[ybk.buf, B_rd], w=[B_t1])
                P.op("dve", I("tensor_tensor", out=gated[:, 8 + h, :], in0=t1, in1=gsb[:, 8 + h, :], op=ALU.mult),
                     r=[B_t1, B_gsb[8 + h]], w=[B_gated] if h == 0 else [], wj=[] if h == 0 else [B_gated])

        def st_S3(i):
            for g in range(4):
                for oc in range(2):
                    c = 2 * g + oc
                    mb = nb()
                    for kc in range(2):
                        P.op("pe", I("matmul", mb.f[:, 0:TA], poolw[:, g * 2 + kc, oc * 128:(oc + 1) * 128], pm[g][:, kc, :],
                                     start=(kc == 0), stop=(kc == 1)), r=[B_poolw, B_pm[g]], w=[mb.buf])
                    P.op("dve", I("scalar_tensor_tensor", out=gated[:, c, :], in0=mb.f[:, 0:TA], scalar=pscale[:, c:c + 1],
                                  in1=gsb[:, c, :], op0=ALU.mult, op1=ALU.mult),
                         r=[mb.buf, B_gsb[c], B_const], wj=[B_gated])

        def st_S4(i):
            t0 = i * TA
            g512 = t0 // 512
            for b in range(NB_A):
                q = (i * NB_A + b) % 2
                sp_load(xf[q], x_d[t0 + b * 128:t0 + (b + 1) * 128, :], [B_xf[q]], f"xf{q}")
            for b in range(NB_A):
                q = (i * NB_A + b) % 2
                yb = [nb(), nb()]
                for hf in range(2):
                    for k in range(12):
                        P.op("pe", I("matmul", yb[hf].f[:, :], gated[:, k, b * 128:(b + 1) * 128],
                                     wout[:, k, hf * 512:(hf + 1) * 512], start=(k == 0), stop=(k == 11)),
                             r=[B_gated, B_wout], w=[yb[hf].buf])
                tok0 = t0 + b * 128
                ln_block(yb[0], yb[1], xf[q], B_xf[q], zz[q], B_zz[q], zn[q], B_zn[q], junk, B_junk, q,
                         x1b[:, b, :], B_x1b[b], X1F[tok0:tok0 + 128, :], B_X1F[g512], f"x1f{q}",
                         B_dst_join=(tok0 % 512 != 0), ln_act=True, part="z")
            for b in range(NB_A):
                q = (i * NB_A + b) % 2
                tok0 = t0 + b * 128
                ln_block(None, None, xf[q], B_xf[q], zz[q], B_zz[q], zn[q], B_zn[q], junk, B_junk, q,
                         x1b[:, b, :], B_x1b[b], X1F[tok0:tok0 + 128, :], B_X1F[g512], f"x1f{q}",
                         B_dst_join=(tok0 % 512 != 0), ln_act=True, part="rest")

        def st_X1(i):
            t0 = i * TA
            g512 = t0 // 512
            transposes_to(x1T, x1b, NB_A, B_x1b, B_x1T, i, all_act=True)
            if t0 % 512 == 0:
                P.op("sp", I("dma_start", out=X1T[:, :, t0:t0 + TA].rearrange("c p t -> p c t"), in_=x1T),
                     r=[B_x1T], w=[B_X1T[g512]], dma=True, key="x1t")
            else:
                P.op("sp", I("dma_start", out=X1T[:, :, t0:t0 + TA].rearrange("c p t -> p c t"), in_=x1T),
                     r=[B_x1T], wj=[B_X1T[g512]], dma=True, key="x1t")

        load_xc(0)
        st_T(0)
        st_U(0)
        st_Upool(0)
        for i in range(NTA):
            if i + 1 < NTA:
                st_T(i + 1)
            st_S2(i)
            st_S3(i)
            if i + 1 < NTA:
                st_U(i + 1)
            if i >= 1:
                st_X1(i - 1)
            st_S4(i)
            if i + 1 < NTA:
                st_Upool(i + 1)
        st_X1(NTA - 1)
        P.barrier(B_X1F + B_X1T)
        A.release(base_mark)

    pre_b = None
    if "b" in phases:
        winB = A.t(BF16, 8, DIN)
        woutB = A.t(BF16, 12, D)
        B_winB, B_woutB = Buf("win1"), Buf("wout1")
        a2_base = A.mark()
        m_tmpB = A.mark()
        wtmpB = A.t(BF16, 8, D)
        membB = A.t(BF16, 2, D)
        memTB = A.t(BF16, 8, NMEM)
        B_wtmpB, B_membB, B_memTB = Buf("wtmp1"), Buf("memb1"), Buf("memT1")
        pre_b = True
    else:
        a2_base = base_mark
    if "a2" in phases:
        wkv = A.t(BF16, 8, 2064)
        B_wkv = Buf("wkv")
        load_w(wkv, w_kv_d, 8, 2064, B_wkv, "wkv")
    if pre_b:
        load_ln(1)
        mem_kv(1, wtmpB, B_wtmpB, membB, B_membB, memTB, B_memTB)
        load_w(winB, w_in_d[1], 8, DIN, B_winB, "win")
        load_w(woutB, w_out_d[1], 12, D, B_woutB, "wout")
    if "a2" in phases:
        xt2 = [A.t(BF16, 8, 512) for _ in range(2)]
        B_xt2 = [Buf("xt2a"), Buf("xt2b")]
        ksb = A.t(BF16, 8, 512)
        B_ksb = Buf("ksb")
        kcsb = A.t(BF16, 512)
        B_kcsb = Buf("kcsb")
        vsb = A.t(BF16, 4, NH, 128)
        B_vsb = Buf("vsb")
        bfor4 = A.t(F32, 4, 16)
        fl = A.t(F32, 4, 16)
        e1 = A.t(F32, 4, 16)
        lp = A.t(F32, 4, 16)
        r1 = A.t(F32, 16)
        r2 = A.t(F32, 16)
        B_fl, B_e1, B_lp, B_r1, B_r2 = Buf("fl"), Buf("e1"), Buf("lp"), Buf("r1"), Buf("r2")
        B_bf4 = Buf("bf4")
        for b in range(4):
            P.op("pool", I("tensor_copy", out=bfor4[:, b, :], in_=bfor), r=[B_const],
                 w=[B_bf4] if b == 0 else [], wj=[] if b == 0 else [B_bf4])
        P.op("pool", I("memset", vsb[:, :, :, 64:128], 1.0), w=[B_vsb])
        sp_load(xt2[0], X1T[:, :, 0:512].rearrange("c p t -> p c t"), [B_xt2[0]], "xt2a", r=[B_X1T[0]])
        for g in range(NT512):
            s = g % 2
            if g + 1 < NT512:
                sp_load(xt2[1 - s], X1T[:, :, (g + 1) * 512:(g + 2) * 512].rearrange("c p t -> p c t"),
                        [B_xt2[1 - s]], "xt2b" if 1 - s else "xt2a", r=[B_X1T[g + 1]])
            xt = xt2[s]
            Bxt = B_xt2[s]
            fb = nb()
            for b in range(4):
                for k in range(8):
                    P.op("pe", I("matmul",
                        fb.f[:, b * 16:(b + 1) * 16], xt[:, k, b * 128:(b + 1) * 128], wkv[:, k, 2048:2064],
                        start=(k == 0), stop=(k == 7)), r=[Bxt, B_wkv], w=[fb.buf])
            P.op("dve", I("tensor_tensor", out=fl, in0=fb.f[:, 0:64].rearrange("p (b h) -> p b h", b=4),
                                                         in1=bfor4, op=ALU.add), r=[fb.buf, B_bf4], w=[B_fl])
            P.op("act", I("activation", out=e1, in_=fl, func=AF.Exp, scale=-1.0), r=[B_fl], w=[B_e1])
            P.op("act", I("activation", out=lp, in_=e1, func=AF.Ln, bias=1.0), r=[B_e1], w=[B_lp])
            for b in range(4):
                n = g * 4 + b
                cb = nb()
                P.op("pe", I("matmul", cb.f[:, 0:16], tri, lp[:, b, :], start=True, stop=True),
                     r=[B_const, B_lp], w=[cb.buf])
                P.op("pe", I("matmul", cb.f[:, 16:32], allones, lp[:, b, :], start=True, stop=True),
                     r=[B_const, B_lp], w=[cb.buf])
                P.op("dve", I("tensor_tensor", out=CK[:, n, :], in0=cb.f[:, 0:16], in1=Tn, op=ALU.add),
                     r=[cb.buf, B_Tn], w=[B_CK[n]])
                P.op("dve", I("tensor_tensor", out=Tn, in0=cb.f[:, 16:32], in1=Tn, op=ALU.add),
                     r=[cb.buf], w=[B_Tn])
                CSv = CS[:, n, :].rearrange("p (h i) -> p h i", i=3)
                P.op("pool", I("tensor_scalar", out=CSv[:, :, 0], in0=CK[:, n, :], scalar1=-8.0,
                                                                      scalar2=None, op0=ALU.mult),
                     r=[B_CK[n]], w=[B_CS[n]])
                P.op("dve", I("scalar_tensor_tensor", out=r1, in0=CK[:, n, :], scalar=-8.0,
                                                                           in1=CSv[:, :, 0], op0=ALU.mult,
                                                                           op1=ALU.subtract),
                     r=[B_CK[n], B_CS[n]], w=[B_r1])
                P.op("pool", I("tensor_copy", out=CSv[:, :, 1], in_=r1), r=[B_r1], wj=[B_CS[n]])
                P.op("pool", I("tensor_tensor", out=r2, in0=r1, in1=CSv[:, :, 1], op=ALU.subtract),
                     r=[B_r1, B_CS[n]], w=[B_r2])
                P.op("pool", I("tensor_copy", out=CSv[:, :, 2], in_=r2), r=[B_r2], wj=[B_CS[n]])
            for c in range(8):
                bk = nb()
                proj(bk, wkv, B_wkv, c * 128, 128, xt, Bxt, 512)
                if c % 2 == 0:
                    P.op("act", I("activation", out=ksb[:, c, :], in_=bk.f[:, :], func=AF.Copy),
                         r=[bk.buf], w=[B_ksb] if c == 0 else [], wj=[] if c == 0 else [B_ksb])
                else:
                    P.op("dve", I("tensor_copy", out=ksb[:, c, :], in_=bk.f[:, :]),
                         r=[bk.buf], wj=[B_ksb])
            P.op("sp", I("dma_start", out=KT[:, :, g * 512:(g + 1) * 512].rearrange("c p t -> p c t"),
                                                  in_=ksb), r=[B_ksb], w=[B_KV[g]], dma=True, key="kst")
            for b in range(4):
                for hf in range(2):
                    bk = nb()
                    for k in range(8):
                        P.op("pe", I("matmul",
                            bk.f[:, :], xt[:, k, b * 128:(b + 1) * 128], wkv[:, k, 1024 + hf * 512:1024 + (hf + 1) * 512],
                            start=(k == 0), stop=(k == 7)), r=[Bxt, B_wkv], w=[bk.buf])
                    src_v = bk.f[:, :].rearrange("p (h d) -> p h d", h=8)
                    first = (b == 0 and hf == 0)
                    if (b + hf) % 2 == 0:
                        P.op("dve", I("tensor_copy",
                            out=vsb[:, b, hf * 8:(hf + 1) * 8, 0:64], in_=src_v),
                            r=[bk.buf], w=[B_vsb] if first else [], wj=[] if first else [B_vsb])
                    else:
                        P.op("act", I("activation",
                            out=vsb[:, b, hf * 8:(hf + 1) * 8, 0:64], in_=src_v, func=AF.Copy),
                            r=[bk.buf], wj=[B_vsb])
            P.op("sp", I("dma_start", out=VA[g * 4:(g + 1) * 4].rearrange("b p h e -> p b h e"), in_=vsb),
                 r=[B_vsb], wj=[B_KV[g]], dma=True, key="vst")
            kcb = nb()
            for b in range(4):
                P.op("pe", I("transpose", out=kcb.h[0:48, b * 128:(b + 1) * 128], in_=CS[:, g * 4 + b, :], identity=ident),
                     r=[B_CS[g * 4 + b], B_const], w=[kcb.buf])
            P.op("act", I("activation", out=kcsb[0:48, :], in_=kcb.h[0:48, 0:512], func=AF.Copy, scale=-1.0),
                 r=[kcb.buf], w=[B_kcsb])
            P.op("sp", I("dma_start", out=KC[:, g * 512:(g + 1) * 512], in_=kcsb[0:48, :]), r=[B_kcsb], wj=[B_KV[g]],
                 dma=True, key="kcst")
        if debug:
            P.op("sp", I("dma_start", out=CK_d, in_=CK.rearrange("p n h -> p (n h)")), r=B_CK, dma=True, key="dbg")
        P.barrier(B_KV + B_CK + B_CS + [B_mkv])
        A.release(a2_base)

    if "b" in phases:
        win, wout, B_win, B_wout = winB, woutB, B_winB, B_woutB
        masks = A.t(BF16, 2, 4, 512)
        B_masks = Buf("masks")
        am = A.t(F32, 2)
        P.op("dve", I("tensor_scalar", out=am, in0=selt, scalar1=float(ALPHA), scalar2=None, op0=ALU.mult),
             r=[B_const], wj=[B_const])
        if "a2" not in phases:
            P.barrier([B_mkv])

        xst = [A.t(BF16, 2, 2, 512)] * 2
        B_xst = [Buf("xst0")] * 2
        xsel = A.t(BF16, 8, 512)
        B_xsel = Buf("xsel")
        csel = A.t(BF16, 4, 48)
        B_csel = Buf("csel")
        csT = A.t(BF16, 512)
        B_csT = Buf("csT")
        gmain = [A.t(F32, 2, 512) for _ in range(2)]
        B_gmain = [Buf("gm0"), Buf("gm1")]
        Qa = [A.t(BF16, 4, 512) for _ in range(2)]
        B_Qa = [Buf("Qa0"), Buf("Qa1")]
        kbuf = [A.t(BF16, 4, 512) for _ in range(2)]
        B_kbuf = [Buf("kb0"), Buf("kb1")]
        vbuf = [A.t(BF16, 4, 4, 128) for _ in range(2)]
        B_vbuf = [Buf("vb0"), Buf("vb1")]
        PTb = [A.t(BF16, 512) for _ in range(3)]
        B_PTb = [Buf(f"ptb{i}") for i in range(3)]
        gsb = [A.t(F32, 512) for _ in range(2)]
        B_gsb = [Buf("gsbB0"), Buf("gsbB1")]
        qm = A.t(BF16, 4, 512)
        B_qm = Buf("qmB")
        PT = PTb[0:2]
        B_PT = B_PTb[0:2]
        rd = A.t(F32, 512)
        t1 = A.t(F32, 512)
        B_rd, B_t1 = Buf("rdB"), Buf("t1B")
        rden = [A.t(F32, 512) for _ in range(2)]
        B_rden = [Buf("rden0"), Buf("rden1")]
        gated = A.t(BF16, 12, 512)
        B_gated = Buf("gatedB")
        xfA = [A.t(F32, D)] * 2
        xfB = [A.t(F32, D)] * 2
        B_xfA = [Buf("xfA0")] * 2
        B_xfB = [Buf("xfB0")] * 2
        zz = [A.t(F32, D) for _ in range(2)]
        zn = [A.t(F32, D) for _ in range(2)]
        B_zz = [Buf("zB0"), Buf("zB1")]
        B_zn = [Buf("znB0"), Buf("znB1")]
        junk, B_junk = None, None
        tmpn = [rd, t1, gsb[0], gsb[1]]
        B_tmpn = [B_rd, B_t1, B_gsb[0], B_gsb[1]]
        B_out = Buf("out")
        for i in range(2):
            P.op("pool", I("memset", kbuf[i][64:70, :, :], 1.0), w=[B_kbuf[i]])
            P.op("pool", I("memset", Qa[i][64:70, :, :], 1.0), w=[B_Qa[i]])

        Obk = banks[0:4]
        Sbk = banks[4:7]
        Mbk = banks[7]
        kv_rr = [0]
        s_rr = [0]
        pt_rr = [0]
        gs_rr = [0]
        xs_rr = [0]
        def do_select(g0, g1):
                for pc in range(4):
                    q = xs_rr[0] % 2
                    xs_rr[0] += 1
                    pq_load(xst[q][:, 0, :, :], X1T[2 * pc:2 * pc + 2, :, g0 * 512:(g0 + 1) * 512].rearrange("c p t -> p c t"),
                            [B_xst[q]], "xst")
                    P.op("pool", I("dma_start",
                        out=xst[q][:, 1, :, :], in_=X1T[2 * pc:2 * pc + 2, :, g1 * 512:(g1 + 1) * 512].rearrange("c p t -> p c t")),
                        wj=[B_xst[q]], dma=True, key="xst")
                    P.op("dve", I("tensor_scalar", out=xsel[:, 2 * pc:2 * pc + 2, :], in0=xst[q][:, 0, :, :],
                                                                     scalar1=selt[:, 0:1], scalar2=None, op0=ALU.mult),
                         r=[B_xst[q], B_const], w=[B_xsel] if pc == 0 else [], wj=[] if pc == 0 else [B_xsel])
                    P.op("dve", I("scalar_tensor_tensor",
                        out=xsel[:, 2 * pc:2 * pc + 2, :], in0=xst[q][:, 1, :, :], scalar=selt[:, 1:2],
                        in1=xsel[:, 2 * pc:2 * pc + 2, :], op0=ALU.mult, op1=ALU.add),
                        r=[B_xst[q], B_const, B_xsel], wj=[B_xsel])
                P.op("dve", I("tensor_scalar", out=csel, in0=CS[:, g0 * 4:(g0 + 1) * 4, :], scalar1=selt[:, 0:1],
                                                             scalar2=None, op0=ALU.mult),
                     r=B_CS[g0 * 4:(g0 + 1) * 4] + [B_const], w=[B_csel])
                P.op("dve", I("scalar_tensor_tensor", out=csel, in0=CS[:, g1 * 4:(g1 + 1) * 4, :],
                                                                    scalar=selt[:, 1:2], in1=csel, op0=ALU.mult, op1=ALU.add),
                     r=B_CS[g1 * 4:(g1 + 1) * 4] + [B_const, B_csel], wj=[B_csel])

        slots = slot_tiles(NT512)
        def proj_items(G):
            gq = G % 2
            items = []
            for oc in range(2):
                def it(bkf, oc=oc):
                    gb = bkf()
                    proj(gb, win, B_win, DMIX + (2 * G + oc) * 128, 128, xsel, B_xsel, 512)
                    P.op("act", I("activation", out=gmain[gq][:, oc, :], in_=gb.f[:, :], func=AF.Silu),
                         r=[gb.buf], w=[B_gmain[gq]] if oc == 0 else [], wj=[] if oc == 0 else [B_gmain[gq]])
                items.append(it)
            for hl in range(4):
                def it(bkf, hl=hl):
                    h = 4 * G + hl
                    qb = bkf()

                    def efirst(bk):
                        P.op("pe", I("matmul", bk.f[0:67, :], Emat[0:48, h, :], csT[0:48, :], start=True, stop=False),
                             r=[B_const, B_csT], w=[bk.buf])
                    proj(qb, win, B_win, h * 64, 64, xsel, B_xsel, 512, extra_first=efirst)
                    if hl % 2 == 0:
                        P.op("act", I("activation", out=Qa[gq][0:67, hl, :], in_=qb.f[0:67, :], func=AF.Copy),
                             r=[qb.buf], w=[B_Qa[gq]] if hl == 0 else [], wj=[] if hl == 0 else [B_Qa[gq]])
                    else:
                        P.op("dve", I("tensor_copy", out=Qa[gq][0:67, hl, :], in_=qb.f[0:67, :]),
                             r=[qb.buf], wj=[B_Qa[gq]])
                items.append(it)
            return items

        def pre_A(j, g0, g1):
            sp_load(masks, masks_d[j % 2].rearrange("p (a b c) -> p a b c", a=2, b=4), [B_masks], "masks")
            if j == 0:
                do_select(g0, g1)
            cbk = nb()
            for b in range(4):
                P.op("pe", I("transpose", out=cbk.h[0:48, b * 128:(b + 1) * 128], in_=csel[:, b, :],
                                                               identity=ident), r=[B_csel, B_const], w=[cbk.buf])
            P.op("dve", I("tensor_copy", out=csT[0:48, :], in_=cbk.h[0:48, 0:512]), r=[cbk.buf], w=[B_csT])

            for h in range(4):
                bk = nb()
                proj(bk, win, B_win, D + h * 128, 128, xsel, B_xsel, 512)
                P.op("act", I("activation", out=qm[:, h, :], in_=bk.f[:, :], func=AF.Copy,
                                                               scale=128.0 ** -0.5),
                     r=[bk.buf], w=[B_qm] if h == 0 else [], wj=[] if h == 0 else [B_qm])
            for it in proj_items(0):
                it(nb)

        def pre_B(j):
            for h in range(4):
                ybk, dbk = nb(), nb()
                mem_attn(h, qm, B_qm, 512, PT, B_PT, ybk, dbk)
                gb = nb()
                proj(gb, win, B_win, DMIX + D + h * 128, 128, xsel, B_xsel, 512)
                gi = gs_rr[0] % 2
                gs_rr[0] += 1
                P.op("act", I("activation", out=gsb[gi], in_=gb.f[:, :], func=AF.Silu),
                     r=[gb.buf], w=[B_gsb[gi]])
                P.op("dve", I("reciprocal", out=rd, in_=dbk.f[:, :]), r=[dbk.buf], w=[B_rd])
                P.op("dve", I("tensor_tensor", out=t1, in0=ybk.f[:, :], in1=rd, op=ALU.mult),
                     r=[ybk.buf, B_rd], w=[B_t1])
                P.op("dve", I("tensor_tensor", out=gated[:, 8 + h, :], in0=t1, in1=gsb[gi], op=ALU.mult),
                     r=[B_t1, B_gsb[gi]], w=[B_gated] if h == 0 else [], wj=[] if h == 0 else [B_gated])

        def attention(j, nch):
            norm_pending = []
            for G in range(4):
                gq = G % 2
                nxt = proj_items(G + 1) if G < 3 else []
                if G == 3 and j + 1 < len(slots):
                    do_select(slots[j + 1][0], slots[j + 1][1])
                units = [(kc, kb, hl) for kc in range(nch) for kb in range(4) for hl in range(4)]
                pend = []
                kq = 0
                for idx, (kc, kb, hl) in enumerate(units):
                    if norm_pending and idx in (2, 4):
                        norm_pending.pop(0)()
                    h = 4 * G + hl
                    n = kc * 4 + kb
                    mtype = kc - (nch - 2)
                    if kb == 0 and hl == 0:
                        kq = kv_rr[0] % 2
                        kv_rr[0] += 1
                        sp_load(kbuf[kq][0:64, :, :],
                                KT.rearrange("c (two d) t -> (c two) d t", two=2)[4 * G:4 * G + 4, :, kc * 512:(kc + 1) * 512]
                                .rearrange("h d t -> d h t"), [B_kbuf[kq]], f"kb{kq}", r=[B_KV[kc]])
                        P.op("sp", I("dma_start", out=kbuf[kq][67:70, :, :],
                                     in_=KC.rearrange("(h i) t -> i h t", i=3)[:, 4 * G:4 * G + 4, kc * 512:(kc + 1) * 512]),
                             r=[B_KV[kc]], wj=[B_kbuf[kq]], dma=True, key=f"kb{kq}")
                        sp_load(vbuf[kq], VA[kc * 4:(kc + 1) * 4, :, 4 * G:4 * G + 4, :].rearrange("b p h e -> p b h e"),
                                [B_vbuf[kq]], f"vb{kq}", r=[B_KV[kc]])
                    sb = Sbk[s_rr[0] % 3]
                    s_rr[0] += 1
                    P.op("pe", I("matmul", sb.f[:, :], kbuf[kq][0:70, hl, kb * 128:(kb + 1) * 128], Qa[gq][0:70, hl, :],
                                 start=True, stop=(mtype < 0)), r=[B_kbuf[kq], B_Qa[gq]], w=[sb.buf])
                    if mtype >= 0:
                        P.op("pe", I("matmul", sb.f[:, :], ident, masks[:, mtype, kb, :], start=False, stop=True),
                             r=[B_const, B_masks], w=[sb.buf])
                    pi = pt_rr[0] % 3
                    pt_rr[0] += 1
                    P.op("act", I("activation", out=PTb[pi], in_=sb.f[:, :], func=AF.Exp, scale=0.125),
                         r=[sb.buf], w=[B_PTb[pi]])

                    def pv(hl=hl, kq=kq, kb=kb, pi=pi, first=(idx < 4), last=(idx >= len(units) - 4)):
                        P.op("pe", I("matmul", Obk[hl].f[:, :], vbuf[kq][:, kb, hl, :], PTb[pi], start=first, stop=last),
                             r=[B_vbuf[kq], B_PTb[pi]], w=[Obk[hl].buf])
                    pend.append(pv)
                    if len(pend) > 2:
                        pend.pop(0)()
                    if nxt and idx >= 3 and idx % 2 == 1:
                        nxt.pop(0)(lambda: Mbk)
                for f in pend:
                    f()
                for it in nxt:
                    it(lambda: Mbk)
                def mk_norm(G, pair, gq):
                    def f():
                        hls = (2 * pair, 2 * pair + 1)
                        for hl in hls:
                            po = (hl % 2) * 64
                            P.op("dve", I("tensor_tensor", out=tmpn[hl][0:64, :], in0=Obk[hl].f[0:64, :],
                                          in1=gmain[gq][po:po + 64, hl // 2, :], op=ALU.mult),
                                 r=[Obk[hl].buf, B_gmain[gq]], w=[B_tmpn[hl]])
                        for hl in hls:
                            P.op("act", I("activation", out=rden[hl % 2][0:64, :], in_=Obk[hl].f[64:128, :], func=AF.Ln),
                                 r=[Obk[hl].buf], w=[B_rden[hl % 2]])
                        for hl in hls:
                            P.op("act", I("activation", out=rden[hl % 2][0:64, :], in_=rden[hl % 2][0:64, :], func=AF.Exp,
                                          scale=-1.0), r=[B_rden[hl % 2]], w=[B_rden[hl % 2]])
                        for hl in hls:
                            h = 4 * G + hl
                            po = (hl % 2) * 64
                            P.op("dve", I("tensor_tensor", out=gated[po:po + 64, h // 2, :], in0=tmpn[hl][0:64, :],
                                          in1=rden[hl % 2][0:64, :], op=ALU.mult), r=[B_tmpn[hl], B_rden[hl % 2]],
                                 wj=[B_gated])
                    return f
                norm_pending = [mk_norm(G, pair, gq) for pair in range(2)]
                if G == 3:
                    for f in norm_pending:
                        f()
                    norm_pending = []

        def post(j, g0, g1):
            for b in (0, 1, -1, 2, 3, -2):
                if b < 0:
                    for b2 in ((0, 1) if b == -1 else (2, 3)):
                        q = b2 % 2
                        row0 = j * 512 + b2 * 128
                        ln_block(None, None, [(xfA[q], B_xfA[q], am[:, 0:1]), (xfB[q], B_xfB[q], am[:, 1:2])], None,
                                 zz[q], B_zz[q], zn[q], B_zn[q], junk, B_junk, q, None, None,
                                 out_d[row0:row0 + 128, :], B_out, f"outst{q}", B_dst_join=True, st_eng="pool",
                                 part="rest", ln_act=True)
                    continue
                q = b % 2
                pq_load(xfA[q], X1F[g0 * 512 + b * 128:g0 * 512 + (b + 1) * 128, :], [B_xfA[q]], "xfA", r=[B_X1F[g0]])
                pq_load(xfB[q], X1F[g1 * 512 + b * 128:g1 * 512 + (b + 1) * 128, :], [B_xfB[q]], "xfB", r=[B_X1F[g1]])
                yb = [nb(), nb()]
                for hf in range(2):
                    for k in range(12):
                        P.op("pe", I("matmul",
                            yb[hf].f[:, :], gated[:, k, b * 128:(b + 1) * 128], wout[:, k, hf * 512:(hf + 1) * 512],
                            start=(k == 0), stop=(k == 11)), r=[B_gated, B_wout], w=[yb[hf].buf])
                row0 = j * 512 + b * 128
                ln_block(yb[0], yb[1], [(xfA[q], B_xfA[q], am[:, 0:1]), (xfB[q], B_xfB[q], am[:, 1:2])], None,
                         zz[q], B_zz[q], zn[q], B_zn[q], junk, B_junk, q, None, None,
                         out_d[row0:row0 + 128, :], B_out, f"outst{q}", B_dst_join=True, st_eng="pool", part="z", ln_act=True)
        for j, (g0, g1, nch) in enumerate(slots):
            if j == 0:
                pre_A(0, g0, g1)
                pre_B(0)
            attention(j, nch)
            if j + 1 < len(slots):
                pre_A(j + 1, slots[j + 1][0], slots[j + 1][1])
            post(j, g0, g1)
            if j + 1 < len(slots):
                pre_B(j + 1)
        A.release(base_mark)

    P.emit()
    return nc


def _consts():
    bf = ml_dtypes.bfloat16
    cb = np.zeros((128, 3 * 128 + 16 * 67), np.float32)
    cb[:, 0:128] = np.eye(128)
    cb[:, 128:256] = 1.0
    E = np.zeros((128, 16, 67), np.float32)
    for h in range(16):
        for i in range(3):
            E[h * 3 + i, h, 64 + i] = 1.0
    cb[:, 384:] = E.reshape(128, -1)
    cf = np.zeros((128, 384), np.float32)
    s = np.arange(128)
    cf[:, 0:128] = (s[:, None] <= s[None, :]).astype(np.float32)
    cf[:, 128:256] = 1.0
    invc = np.zeros((4, 16), np.float32)
    for g, w in enumerate(POOLW):
        invc[g] = 1.0 / np.minimum(np.arange(16) + 1, w)
    cf[:, 256:320] = invc.reshape(1, 64)
    cf[:, 320:384] = 1.0
    return cb.astype(bf), cf


def _masks(hh):
    bf = ml_dtypes.bfloat16
    s = np.arange(128)[:, None, None]
    kb = np.arange(4)[None, :, None]
    t = np.arange(512)[None, None, :]
    diag = np.where(kb * 128 + s > t, MASKNEG, 0.0).astype(np.float32)
    full = np.full((128, 4, 512), MASKNEG, np.float32)
    none = np.zeros((128, 4, 512), np.float32)
    a = np.stack([diag, full], axis=1).reshape(128, -1)
    b = np.stack([none, diag], axis=1).reshape(128, -1)
    m = np.stack([a, b] if hh == 0 else [b, a], axis=0)
    return m.astype(bf)


def make_in_maps(inputs, S):
    cb, cf = _consts()
    x = np.asarray(inputs["x"], np.float32)
    B = x.shape[0]
    lnp = np.ascontiguousarray(np.stack([inputs["ln_g"], inputs["ln_b"]], axis=1).astype(np.float32))
    pscale = np.ascontiguousarray(np.asarray(inputs["pool_scale"], np.float32).reshape(8, 128).T)
    maps = []
    for c in range(2 * B):
        b, hh = c // 2, c % 2
        sel = np.zeros((128, 2), np.float32)
        sel[:, hh] = 1.0
        maps.append({
            "x": np.ascontiguousarray(x[b]),
            "mem": np.ascontiguousarray(np.asarray(inputs["mem"], np.float32)[b]),
            "w_in": np.asarray(inputs["w_in"], np.float32),
            "w_mem_kv": np.asarray(inputs["w_mem_kv"], np.float32),
            "w_out": np.asarray(inputs["w_out"], np.float32),
            "pool_w": np.ascontiguousarray(np.asarray(inputs["pool_w"], np.float32)[0]),
            "w_kv": np.asarray(inputs["w_kv_shared"], np.float32),
            "lnp": lnp,
            "pscale": pscale,
            "bfor": np.asarray(inputs["b_forget"], np.float32),
            "cb16": cb,
            "cf32": cf,
            "sel": sel,
            "masks": _masks(hh),
        })
    return maps


_NC_CACHE = {}


def kernel(x, mem, w_in, w_mem_kv, w_out, ln_g, ln_b, pool_w, pool_scale, w_kv_shared, b_forget):
    inputs = dict(x=x, mem=mem, w_in=w_in, w_mem_kv=w_mem_kv, w_out=w_out, ln_g=ln_g, ln_b=ln_b,
                  pool_w=pool_w, pool_scale=pool_scale, w_kv_shared=w_kv_shared, b_forget=b_forget)
    B, S, _ = np.asarray(x).shape
    if S not in _NC_CACHE:
        _NC_CACHE[S] = build(S)
    nc = _NC_CACHE[S]
    maps = make_in_maps(inputs, S)
    res = run_bass_kernel_spmd(nc, maps, core_ids=list(range(2 * B)))
    out = np.zeros((B, S, D), np.float32)
    st = slot_tiles(S // 512)
    for c in range(2 * B):
        b, hh = c // 2, c % 2
        o = np.asarray(res.results[c]["out"], np.float32)
        for j, (g0, g1, _) in enumerate(st):
            g = g0 if hh == 0 else g1
            out[b, g * 512:(g + 1) * 512, :] = o[j * 512:(j + 1) * 512, :]
    return out
```
